# Optimizing a Trainium2 kernel written in Bass

```python
import jax, jax.numpy as jnp
from jax import lax
import numpy as np

D_MODEL = 1024
BATCH = 8
SEQ = 2048
DEPTH = 1

CHUNK = 64
D_RNN = D_MODEL // 2
RNN_BLOCKS = 8
RNN_BLOCK_W = D_RNN // RNN_BLOCKS
RNN_CONV_W = 4
RG_C = 8.0
ATT_HEAD_DIM = 64
D_ATT = D_MODEL // 2
N_ATT_HEADS = D_ATT // ATT_HEAD_DIM
LOOKBACK = 8
BAND = (LOOKBACK + 1) * CHUNK
REL_CLIP = 128
D_MIX = D_RNN + D_ATT
D_IN = 2 * D_RNN + 3 * D_ATT
D_FF = 2816
FFN_CONV_W = 3
EPS = 1e-6
ADA_SCALE = 0.5
NEG_INF = -1e30

kernel_name = "hybrid_rglru_chunkattn_convffn_adaln"


def rmsnorm(x, g):
    xf = x.astype(jnp.float32)
    y = xf * lax.rsqrt(jnp.mean(xf * xf, axis=-1, keepdims=True) + EPS)
    return (y * g.astype(jnp.float32)).astype(x.dtype)


def modulate(h, shift, scale):
    return h * (1.0 + scale[:, None, :]) + shift[:, None, :]


def causal_dwconv(x, w, b):
    width = w.shape[0]
    seq = x.shape[1]
    xp = jnp.pad(x, ((0, 0), (width - 1, 0), (0, 0)))
    y = xp[:, 0:seq] * w[0]
    for k in range(1, width):
        y = y + xp[:, k:k + seq] * w[k]
    return y + b


def rg_lru_group(xr, gr, conv_w, conv_b, wa, ba, wx, bx, lam):
    bsz, seq, _ = xr.shape
    xc = causal_dwconv(xr, conv_w, conv_b)
    xb = xc.reshape(bsz, seq, RNN_BLOCKS, RNN_BLOCK_W)
    r = jax.nn.sigmoid(jnp.einsum('bsnc,ncd->bsnd', xb, wa).reshape(bsz, seq, D_RNN) + ba)
    i = jax.nn.sigmoid(jnp.einsum('bsnc,ncd->bsnd', xb, wx).reshape(bsz, seq, D_RNN) + bx)
    log_a = RG_C * r.astype(jnp.float32) * jax.nn.log_sigmoid(lam.astype(jnp.float32))
    a = jnp.exp(log_a)
    mult = jnp.sqrt(-jnp.expm1(2.0 * log_a))
    bterm = mult * (i * xc).astype(jnp.float32)

    def combine(left, right):
        a1, b1 = left
        a2, b2 = right
        return a1 * a2, a2 * b1 + b2

    _, h = lax.associative_scan(combine, (a, bterm), axis=1)
    return h.astype(xr.dtype) * jax.nn.gelu(gr)


def chunk_attention_group(q, k, v, rel_bias):
    bsz, seq, _ = q.shape
    nc = seq // CHUNK
    shp = (bsz, nc, CHUNK, N_ATT_HEADS, ATT_HEAD_DIM)
    q = q.reshape(shp)
    k = k.reshape(shp)
    v = v.reshape(shp)
    pad = ((0, 0), (LOOKBACK, 0), (0, 0), (0, 0), (0, 0))
    kp = jnp.pad(k, pad)
    vp = jnp.pad(v, pad)
    band_idx = jnp.arange(nc)[:, None] + jnp.arange(LOOKBACK + 1)[None, :]
    kb = kp[:, band_idx].reshape(bsz, nc, BAND, N_ATT_HEADS, ATT_HEAD_DIM)
    vb = vp[:, band_idx].reshape(bsz, nc, BAND, N_ATT_HEADS, ATT_HEAD_DIM)
    qi = jnp.arange(CHUNK)
    kj = jnp.arange(BAND)
    rel = LOOKBACK * CHUNK + qi[:, None] - kj[None, :]
    bias = rel_bias[:, jnp.clip(rel, -REL_CLIP, REL_CLIP) + REL_CLIP]
    valid = (jnp.arange(nc)[:, None] - LOOKBACK + kj[None, :] // CHUNK) >= 0
    scale = ATT_HEAD_DIM ** -0.5
    s = jnp.einsum('bnqhd,bnkhd->bnhqk', q, kb).astype(jnp.float32) * scale
    s = s + bias.astype(jnp.float32)[None, None]
    s = jnp.where(valid[None, :, None, None, :], s, NEG_INF)
    p = jax.nn.softmax(s, axis=-1).astype(v.dtype)
    o = jnp.einsum('bnhqk,bnkhd->bnqhd', p, vb)
    return o.reshape(bsz, seq, D_ATT)


def setup_inputs(seed: int = 0) -> dict:
    key = jax.random.key(seed)
    ks = jax.random.split(key, 24)
    f32 = jnp.float32
    nrm = lambda k, shape, s: jax.random.normal(k, shape, f32) * s
    u = jax.random.uniform(ks[12], (DEPTH, D_RNN), f32, 0.9, 0.999)
    return {
        "x": nrm(ks[0], (BATCH, SEQ, D_MODEL), 1.0),
        "c": nrm(ks[1], (BATCH, D_MODEL), 1.0),
        "ada_w": nrm(ks[2], (DEPTH, D_MODEL, 6 * D_MODEL), ADA_SCALE * D_MODEL ** -0.5),
        "ada_b": nrm(ks[3], (DEPTH, 6 * D_MODEL), 0.02),
        "norm1_g": 1.0 + nrm(ks[4], (DEPTH, D_MODEL), 0.02),
        "w_in": nrm(ks[5], (DEPTH, D_MODEL, D_IN), D_MODEL ** -0.5),
        "rnn_conv_w": nrm(ks[6], (DEPTH, RNN_CONV_W, D_RNN), RNN_CONV_W ** -0.5),
        "rnn_conv_b": nrm(ks[7], (DEPTH, D_RNN), 0.02),
        "rg_wa": nrm(ks[8], (DEPTH, RNN_BLOCKS, RNN_BLOCK_W, RNN_BLOCK_W), RNN_BLOCK_W ** -0.5),
        "rg_ba": nrm(ks[9], (DEPTH, D_RNN), 0.02),
        "rg_wx": nrm(ks[10], (DEPTH, RNN_BLOCKS, RNN_BLOCK_W, RNN_BLOCK_W), RNN_BLOCK_W ** -0.5),
        "rg_bx": nrm(ks[11], (DEPTH, D_RNN), 0.02),
        "rg_lambda": jnp.log(u) - jnp.log1p(-u),
        "rel_bias": nrm(ks[13], (DEPTH, N_ATT_HEADS, 2 * REL_CLIP + 1), 0.5),
        "w_out": nrm(ks[14], (DEPTH, D_MIX, D_MODEL), D_MIX ** -0.5),
        "norm2_g": 1.0 + nrm(ks[15], (DEPTH, D_MODEL), 0.02),
        "w_up": nrm(ks[16], (DEPTH, D_MODEL, 2 * D_FF), D_MODEL ** -0.5),
        "ffn_conv_w": nrm(ks[17], (DEPTH, FFN_CONV_W, 2 * D_FF), FFN_CONV_W ** -0.5),
        "ffn_conv_b": nrm(ks[18], (DEPTH, 2 * D_FF), 0.02),
        "w_down": nrm(ks[19], (DEPTH, D_FF, D_MODEL), D_FF ** -0.5),
        "final_g": 1.0 + nrm(ks[20], (D_MODEL,), 0.02),
    }


def reference(x, c, ada_w, ada_b, norm1_g, w_in, rnn_conv_w, rnn_conv_b, rg_wa, rg_ba, rg_wx, rg_bx,
              rg_lambda, rel_bias, w_out, norm2_g, w_up, ffn_conv_w, ffn_conv_b, w_down, final_g):
    splits = [D_RNN, 2 * D_RNN, 2 * D_RNN + D_ATT, 2 * D_RNN + 2 * D_ATT]
    for l in range(DEPTH):
        mod = jax.nn.silu(c) @ ada_w[l] + ada_b[l]
        sh1, sc1, g1, sh2, sc2, g2 = jnp.split(mod, 6, axis=-1)
        h = modulate(rmsnorm(x, norm1_g[l]), sh1, sc1)
        proj = h @ w_in[l]
        xr, gr, q, k, v = jnp.split(proj, splits, axis=-1)
        y_rnn = rg_lru_group(xr, gr, rnn_conv_w[l], rnn_conv_b[l], rg_wa[l], rg_ba[l],
                             rg_wx[l], rg_bx[l], rg_lambda[l])
        y_att = chunk_attention_group(q, k, v, rel_bias[l])
        y = jnp.concatenate([y_rnn, y_att], axis=-1) @ w_out[l]
        x = x + g1[:, None, :] * y
        h = modulate(rmsnorm(x, norm2_g[l]), sh2, sc2)
        up = causal_dwconv(h @ w_up[l], ffn_conv_w[l], ffn_conv_b[l])
        ug, uv = jnp.split(up, 2, axis=-1)
        x = x + g2[:, None, :] * ((jax.nn.silu(ug) * uv) @ w_down[l])
    return rmsnorm(x, final_g)
```

```python
import numpy as np
import concourse.bass as bass
import concourse.mybir as mybir
from concourse.bass_utils import run_bass_kernel_spmd

F32 = mybir.dt.float32
BF16 = mybir.dt.bfloat16
ALU = mybir.AluOpType
AF = mybir.ActivationFunctionType

S = 2048
D = 1024
NCORE = 8
DFF = 2816
NFC = DFF // 128
EPS = 1e-6
ATT_WARM = 3
NU = 5
ND = 2
B0 = 16512
SB_END = 229344


class Buf:
    __slots__ = ("name", "w", "r")

    def __init__(self, name):
        self.name = name
        self.w = None
        self.r = []


class Sched:
    ENGS = ["pe", "act", "dve", "pool", "sp"]

    def __init__(self):
        self.ops = {e: [] for e in self.ENGS}
        self.cnt = {}
        self.seen = {e: {} for e in self.ENGS}

    def _need(self, eng, waits, tok, war=False):
        if tok is None:
            return
        s, c = tok
        if s == eng and eng == "pe":
            return
        if self.seen[eng].get(s, 0) >= c:
            return
        if waits.get(s, 0) < c:
            waits[s] = c

    def op(self, eng, fn, reads=(), writes=(), dma=None):
        waits = {}
        for b in reads:
            self._need(eng, waits, b.w)
        for b in writes:
            self._need(eng, waits, b.w)
            for t in b.r:
                self._need(eng, waits, t, war=True)
        if dma is not None:
            c = self.cnt.get(dma, 0)
            if c > 0:
                self._need(eng, waits, (dma, c))
            self.cnt[dma] = c + 16
            tok = (dma, c + 16)
        else:
            self.cnt[eng] = self.cnt.get(eng, 0) + 1
            tok = (eng, self.cnt[eng])
        for s, c in waits.items():
            self.seen[eng][s] = c
        self.ops[eng].append((sorted(waits.items()), fn, dma))
        for b in reads:
            b.r.append(tok)
        for b in writes:
            b.w = tok
            b.r = []
        return tok

    def wait_only(self, eng, toks):
        waits = {}
        for t in toks:
            self._need(eng, waits, t)
        for s, c in waits.items():
            self.seen[eng][s] = c
        self.ops[eng].append((sorted(waits.items()), None, None))

    def barrier(self):
        toks = [(s, c) for s, c in self.cnt.items()]
        for e in self.ENGS:
            self.wait_only(e, toks)

    def alias(self, new_bufs, old_bufs):
        toks = []
        for b in old_bufs:
            if b.w is not None:
                toks.append(b.w)
            toks.extend(b.r)
        for nb in new_bufs:
            nb.w = None
            nb.r = list(toks)


class _Stop(Exception):
    pass


def build_program(debug=False, stop_at=None):
    nc = bass.Bass("TRN2", target_bir_lowering=False)
    sch = Sched()

    def din(name, shape):
        return nc.dram_tensor(name, list(shape), F32, kind="ExternalInput").ap()

    xT = din("xT", [D, S])
    cT = din("cT", [128, 8])
    ada_w = din("ada_w", [D, 6 * D])
    ada_bT = din("ada_bT", [128, 48])
    n1g = din("n1g", [128, 8])
    n2g = din("n2g", [128, 8])
    nfg = din("nfg", [128, 8])
    w_in = din("w_in", [D, 2560])
    cw = din("cw", [128, 16])
    cb = din("cb", [128, 4])
    wa_bd = din("wa_bd", [128, 512])
    wx_bd = din("wx_bd", [128, 512])
    rba = din("rba", [128, 4])
    rbx = din("rbx", [128, 4])
    rlam = din("rlam", [128, 4])
    tbias = din("tbias", [8, 128, 640])
    ident = din("ident", [128, 128])
    w_out = din("w_out", [D, D])
    w_up = din("w_up", [D, 2 * DFF])
    fcw = din("fcw", [128, 44 * 3])
    fcb = din("fcb", [128, 44])
    w_down = din("w_down", [DFF, D])
    outT = nc.dram_tensor("outT", [D, S], F32, kind="ExternalOutput").ap()
    dbg = {}
    if debug:
        for nm, shp in [("d_h", [D, S]), ("d_xr", [512, S]), ("d_gg", [512, S]), ("d_q", [512, S]),
                        ("d_k", [512, S]), ("d_v", [S, 512]), ("d_y", [D, S]), ("d_dl", [D, S]),
                        ("d_mod", [128, 48])]:
            dbg[nm] = nc.dram_tensor(nm, shp, F32, kind="ExternalOutput").ap()

    ada_v = ada_w.rearrange("(kc p) n -> p kc n", p=128)
    win_v = w_in.rearrange("(kc p) n -> p kc n", p=128)
    wout_v = w_out.rearrange("(kc p) n -> p kc n", p=128)
    wup_v = w_up.rearrange("(kc p) n -> p kc n", p=128)
    wdn_v = w_down.rearrange("(kc p) n -> p kc n", p=128)

    off = [B0]

    def alloc(name, shape, dt, at=None):
        nbytes = int(np.prod(shape[1:])) * (4 if dt == F32 else 2)
        if at is None:
            at = off[0]
            off[0] += (nbytes + 31) // 32 * 32
            assert off[0] <= SB_END, (name, off[0])
        return nc.alloc_sbuf_tensor_at(name, list(shape), dt, offset=at)

    prm = alloc("prm", [128, 576], F32)
    PC = {}
    pc = [0]

    def pslot(name, n):
        PC[name] = (pc[0], n)
        pc[0] += n
        assert pc[0] <= 576
        return prm[:, PC[name][0]:PC[name][0] + n]

    p_c = pslot("c", 8)
    p_adab = pslot("adab", 48)
    p_mod = pslot("mod", 48)
    p_n1g = pslot("n1g", 8)
    p_n2g = pslot("n2g", 8)
    p_nfg = pslot("nfg", 8)
    p_gm1 = pslot("gm1", 8)
    p_gm2 = pslot("gm2", 8)
    p_cw = pslot("cw", 16)
    p_cb = pslot("cb", 4)
    p_ba = pslot("ba", 4)
    p_bx = pslot("bx", 4)
    p_lam = pslot("lam", 4)
    p_cl = pslot("cl", 4)
    p_cl2 = pslot("cl2", 4)
    p_tmp = pslot("tmp", 8)
    p_fcw = pslot("fcw", 132)
    p_fcb = pslot("fcb", 44)
    p_carry = pslot("carry", 4)
    p_hz = pslot("hz", 8)
    p_hzt = pslot("hzt", 8)
    p_halo = pslot("halo", 176)
    b_prm = {k: Buf("prm_" + k) for k in PC}
    c_bf = alloc("c_bf", [128, 8], BF16)
    b_cbf = Buf("c_bf")
    ones_bf = alloc("ones_bf", [128, 128], BF16)
    b_ones = Buf("ones")
    ident_bf = alloc("ident_bf", [128, 128], BF16)
    b_ident = Buf("ident")
    wabd = alloc("wabd", [128, 512], BF16)
    wxbd = alloc("wxbd", [128, 512], BF16)
    b_wabd, b_wxbd = Buf("wabd"), Buf("wxbd")
    Etab = alloc("Etab", [128, 8, 640], BF16)
    b_E = [Buf("E%d" % h) for h in range(8)]

    STAGE0 = off[0]
    XR0 = off[0]
    xfull = alloc("xfull", [128, 8, 2048], F32)
    b_x = [Buf("x%d" % i) for i in range(8)]
    xrp = alloc("xrp", [128, 4, 2056], BF16, at=XR0)
    gg = alloc("gg", [128, 4, 2048], BF16, at=XR0 + 16448)
    qs = alloc("qs", [128, 4, 2048], BF16, at=XR0 + 16448 + 16384)
    ks = alloc("ks", [128, 4, 2048], BF16, at=XR0 + 16448 + 32768)
    off[0] = XR0 + 16448 + 49152
    DL0 = XR0 + 16448 + 16384
    delta1 = alloc("delta1", [128, 8, 2048], BF16, at=DL0)
    b_xr = [[Buf("xr%d_%d" % (i, t)) for t in range(4)] for i in range(4)]
    b_gg = [[Buf("gg%d_%d" % (i, t)) for t in range(4)] for i in range(4)]
    b_q = [[Buf("q%d_%d" % (i, t)) for t in range(4)] for i in range(4)]
    b_k = [[Buf("k%d_%d" % (i, t)) for t in range(4)] for i in range(4)]

    def flat(ll):
        return [b for row in ll for b in row]
    b_dl = [[Buf("dl%d_%d" % (i, t)) for t in range(4)] for i in range(8)]
    hy = alloc("hy", [128, 8, 2048], BF16)
    b_hh = [[Buf("h%d_%d" % (i, k)) for k in range(2)] for i in range(8)]
    b_h = [b for row in b_hh for b in row]
    b_xn = [[Buf("xn%d_%d" % (i, k)) for k in range(2)] for i in range(8)]
    b_y = [[Buf("y%d_%d" % (i, t)) for t in range(4)] for i in range(8)]
    vt = alloc("vt", [128, 16, 512], BF16)
    b_v = [Buf("v%d" % i) for i in range(16)]
    TMP0 = off[0]
    t_acc = alloc("t_acc", [128, 1024], F32)
    t_xcb = alloc("t_xcb", [128, 1024], BF16)
    t_A = alloc("t_A", [128, 1024], F32)
    t_B = [alloc("t_B%d" % i, [128, 1024], F32) for i in range(2)]
    t_C = alloc("t_C", [128, 1024], F32)
    t_D = alloc("t_D", [128, 1024], F32)
    b_acc, b_xcb, b_A, b_C, b_D = Buf("acc"), Buf("xcb"), Buf("A"), Buf("C"), Buf("D")
    b_B = [Buf("B0"), Buf("B1")]
    rstd = alloc("rstd", [128, 2048], F32, at=TMP0)
    sqb = [alloc("sqb%d" % i, [128, 2048], BF16, at=TMP0 + 8192 + 4096 * i) for i in range(2)]
    b_rstd = Buf("rstd")
    b_sq = [Buf("sq0"), Buf("sq1")]
    wring = [alloc("wring%d" % i, [128, 8, 512], BF16) for i in range(3)]
    b_wr = [Buf("wr%d" % i) for i in range(3)]
    pbuf2 = [alloc("pbuf%d" % i, [128, 2, 512], BF16) for i in range(4)]
    b_pp = [Buf("pp%d" % i) for i in range(4)]
    etmp = [nc.alloc_sbuf_tensor_at("etmp%d" % i, [128, 640], F32, offset=TMP0 + 2560 * i) for i in range(2)]
    b_et = [Buf("et0"), Buf("et1")]
    gt = [alloc("gt%d" % i, [128, 512], F32) for i in range(2)]
    b_gt = [Buf("gt0"), Buf("gt1")]
    rden = alloc("rden", [128, 512], F32)
    b_rden = Buf("rden")
    aring = [alloc("aring%d" % i, [128, 8, 512], BF16) for i in range(2)]
    b_ar = [Buf("ar0"), Buf("ar1")]
    STAGEA_END = off[0]

    off[0] = STAGE0
    h2b = [alloc("h2h%d" % i, [128, 8, 1024], BF16) for i in range(2)]
    b_h2 = [[Buf("h2_%d_%d" % (k, i)) for i in range(8)] for k in range(2)]
    b_ca = [Buf("ca%d" % i) for i in range(4)]
    b_cah = [Buf("cah%d" % i) for i in range(4)]
    b_halo = [Buf("halo0"), Buf("halo1")]
    b_hz = [[Buf("hz%d_%d" % (i, k)) for k in range(3)] for i in range(2)]
    assert off[0] <= DL0
    HY0 = DL0 + 32768
    off[0] = HY0
    sq2 = [alloc("sq2_%d" % i, [128, 1024], BF16) for i in range(2)]
    b_sq2 = [Buf("sq2_0"), Buf("sq2_1")]
    rstd2 = alloc("rstd2", [128, 1024], F32)
    b_rstd2 = Buf("rstd2")
    ntmp = [alloc("ntmp%d" % i, [128, 1024], F32) for i in range(2)]
    b_nt = [Buf("nt0"), Buf("nt1")]
    assert off[0] == HY0 + 16384, off[0]
    CACC0 = off[0]
    cacc = [alloc("cacc%d" % i, [128, 1024], F32) for i in range(4)]
    sq3 = [nc.alloc_sbuf_tensor_at("sq3_%d" % i, [128, 1024], BF16, offset=CACC0 + 4096 * i) for i in range(2)]
    assert off[0] == HY0 + 32768, off[0]
    x2h = alloc("x2h", [128, 8, 1024], F32)
    b_x2 = [Buf("x2_%d" % i) for i in range(8)]
    assert off[0] <= TMP0 + 16384 + 2048, (off[0], TMP0)
    uring = [alloc("uring%d" % i, [128, 8, 128], BF16) for i in range(NU)]
    b_ur = [Buf("ur%d" % i) for i in range(NU)]
    dring = [alloc("dring%d" % i, [128, NFC, 128], BF16) for i in range(ND)]
    b_dr = [Buf("dr%d" % i) for i in range(ND)]
    b_dr2 = [[Buf("dr%d_%d" % (i, k)) for k in range(3)] for i in range(ND)]
    gbuf = alloc("gbuf", [128, NFC, 1024], BF16)
    b_g = [Buf("g%d" % i) for i in range(NFC)]

    psum = nc.alloc_psum_tensor("psum", [128, 4096], F32)
    b_bank = [Buf("bank%d" % i) for i in range(8)]

    def bank(i, n=512, c0=0):
        return psum[:, i * 512 + c0:i * 512 + c0 + n]

    def dma_sp(out, in_, sem, reads=(), writes=()):
        return sch.op("sp", lambda h, o=out, i=in_: h.dma_start(out=o, in_=i), reads=reads, writes=writes, dma=sem)

    def dma_pool(out, in_, sem, reads=(), writes=()):
        return sch.op("pool", lambda h, o=out, i=in_: h.dma_start(out=o, in_=i), reads=reads, writes=writes, dma=sem)

    def act(out, in_, func, reads, writes, bias=None, scale=None):
        kw = {}
        if bias is not None:
            kw["bias"] = bias
        if scale is not None:
            kw["scale"] = scale
        return sch.op("act", lambda h, o=out, i=in_, f=func, kw=kw: h.activation(o, i, f, **kw), reads=reads, writes=writes)

    def tt(eng, out, a, b, op, reads, writes):
        return sch.op(eng, lambda h, o=out, a=a, b=b, op=op: h.tensor_tensor(o, a, b, op), reads=reads, writes=writes)

    def ts(eng, out, a, s1, s2, op0, op1, reads, writes):
        if op1 is None:
            return sch.op(eng, lambda h, o=out, a=a, s1=s1, op0=op0: h.tensor_scalar(o, a, s1, None, op0), reads=reads, writes=writes)
        return sch.op(eng, lambda h, o=out, a=a, s1=s1, s2=s2, op0=op0, op1=op1: h.tensor_scalar(o, a, s1, s2, op0, op1),
                      reads=reads, writes=writes)

    def stt(eng, out, a, s, b, op0, op1, reads, writes):
        return sch.op(eng, lambda h, o=out, a=a, s=s, b=b, op0=op0, op1=op1: h.scalar_tensor_tensor(o, a, s, b, op0, op1),
                      reads=reads, writes=writes)

    def mm_group(mms, reads, writes):
        def fn(h, mms=mms):
            ins = None
            for (o, l, r, st, sp_) in mms:
                ins = h.matmul(o, l, r, start=st, stop=sp_)
            return ins
        return sch.op("pe", fn, reads=reads, writes=writes)

    dd = [0]

    def dump(dst, src, rd):
        tmpf = gt[dd[0] % 2]
        bt = b_gt[dd[0] % 2]
        dd[0] += 1
        sch.op("dve", lambda h, t=tmpf, s=src: h.tensor_copy(t[:], s), reads=rd, writes=[bt])
        dma_sp(dst, tmpf[:], "dbg%d" % (dd[0] % 2), reads=[bt])

    prm_i = [0]

    def load_prm(dst, src, key):
        sem = "pr%d" % (prm_i[0] % 4)
        prm_i[0] += 1
        dma_sp(dst, src, sem, writes=[b_prm[key]])

    def _chk(name):
        if stop_at == name:
            raise _Stop()

    try:
        load_prm(p_c, cT[:, :], "c")
        load_prm(p_adab, ada_bT[:, :], "adab")
        load_prm(p_n1g, n1g[:, :], "n1g")
        load_prm(p_cw, cw[:, :], "cw")
        load_prm(p_cb, cb[:, :], "cb")
        load_prm(p_ba, rba[:, :], "ba")
        load_prm(p_bx, rbx[:, :], "bx")
        load_prm(p_lam, rlam[:, :], "lam")
        load_prm(p_n2g, n2g[:, :], "n2g")
        load_prm(p_nfg, nfg[:, :], "nfg")
        load_prm(p_fcw, fcw[:, :], "fcw")
        load_prm(p_fcb, fcb[:, :], "fcb")

        for kc in range(8):
            dma_sp(xfull[:, kc, :], xT[kc * 128:(kc + 1) * 128, :], "x%d" % (kc % 4), writes=[b_x[kc]])

        sch.op("dve", lambda h: h.memset(ones_bf[:], 1.0), writes=[b_ones])
        sch.op("dve", lambda h: h.memset(p_carry, 0.0), writes=[b_prm["carry"]])
        sch.op("dve", lambda h: h.memset(p_hz, 0.0), writes=[b_prm["hz"]])
        sch.op("dve", lambda h: h.memset(p_halo, 0.0), writes=[b_prm["halo"]] + b_halo)

        pieces = []
        for i in range(4):
            pieces.append(("ada", i, ada_v[:, :, i * 512:(i + 1) * 512]))
        for i in range(5):
            pieces.append(("win", i, win_v[:, :, i * 512:(i + 1) * 512]))
        for i in range(2):
            pieces.append(("wout", i, wout_v[:, :, i * 512:(i + 1) * 512]))
        piece_pos = {(k, i): n for n, (k, i, _) in enumerate(pieces)}
        issued = [0]

        def issue_pieces(upto):
            while issued[0] < min(upto, len(pieces)):
                n = issued[0]
                slot = n % 3
                dma_pool(wring[slot][:], pieces[n][2], "w%d" % slot, writes=[b_wr[slot]])
                issued[0] += 1

        def piece_slot(kind, idx):
            n = piece_pos[(kind, idx)]
            issue_pieces(n + 3)
            return n % 3

        issue_pieces(2)
        dma_pool(wabd[:], wa_bd[:, :], "wg", writes=[b_wabd])
        dma_pool(wxbd[:], wx_bd[:, :], "wg", writes=[b_wxbd])
        issue_pieces(3)

        act(p_tmp, p_c, AF.Tanh, [b_prm["c"]], [b_prm["tmp"]], scale=0.5)
        stt("dve", p_tmp, p_tmp, 1.0, p_c, ALU.add, ALU.mult, [b_prm["tmp"], b_prm["c"]], [b_prm["tmp"]])
        ts("dve", c_bf[:], p_tmp, 0.5, None, ALU.mult, None, [b_prm["tmp"]], [b_cbf])

        act(p_cl, p_lam, AF.Exp, [b_prm["lam"]], [b_prm["cl"]], scale=-1.0)
        act(p_cl, p_cl, AF.Ln, [b_prm["cl"]], [b_prm["cl"]], bias=1.0)
        ts("dve", p_cl2, p_cl, -8.0, None, ALU.mult, None, [b_prm["cl"]], [b_prm["cl2"]])
        ts("dve", p_cl, p_cl, -4.0, None, ALU.mult, None, [b_prm["cl"]], [b_prm["cl"]])
        ts("dve", p_ba, p_ba, 0.5, None, ALU.mult, None, [b_prm["ba"]], [b_prm["ba"]])
        ts("dve", p_bx, p_bx, 0.5, None, ALU.mult, None, [b_prm["bx"]], [b_prm["bx"]])


        MODB = 3

        def mod_piece(i, mb=MODB):
            slot = piece_slot("ada", i)
            mms = []
            for jj in range(4):
                j = i * 4 + jj
                for kc in range(8):
                    mms.append((bank(mb, 1, j), wring[slot][:, kc, jj * 128:(jj + 1) * 128], c_bf[:, kc:kc + 1], kc == 0, kc == 7))
            mm_group(mms, [b_wr[slot], b_cbf], [b_bank[mb]])
            tt("dve", p_mod[:, i * 4:(i + 1) * 4], bank(mb, 4, i * 4), p_adab[:, i * 4:(i + 1) * 4], ALU.add,
               [b_bank[mb], b_prm["adab"]], [b_prm["mod"]])

        aiss = [4]

        def issue_a(upto):
            while aiss[0] < min(upto, 12):
                i = aiss[0]
                dma_pool(aring[i % 2][:], ada_v[:, :, i * 512:(i + 1) * 512], "a%d" % (i % 2), writes=[b_ar[i % 2]])
                aiss[0] += 1

        def mod_piece2(i):
            issue_a(i + 2)
            slot = i % 2
            mms = []
            for jj in range(4):
                j = i * 4 + jj
                for kc in range(8):
                    mms.append((bank(MODB, 1, j), aring[slot][:, kc, jj * 128:(jj + 1) * 128], c_bf[:, kc:kc + 1], kc == 0, kc == 7))
            mm_group(mms, [b_ar[slot], b_cbf], [b_bank[MODB]])
            tt("dve", p_mod[:, i * 4:(i + 1) * 4], bank(MODB, 4, i * 4), p_adab[:, i * 4:(i + 1) * 4], ALU.add,
               [b_bank[MODB], b_prm["adab"]], [b_prm["mod"]])

        sh1, sc1, g1m = p_mod[:, 0:8], p_mod[:, 8:16], p_mod[:, 16:24]
        sh2, sc2, g2m = p_mod[:, 24:32], p_mod[:, 32:40], p_mod[:, 40:48]


        _chk('p0')
        sch.alias(b_sq + [b_rstd], b_et)
        for kc in range(8):
            s_ = kc % 2
            act(sqb[s_][:], xfull[:, kc, :], AF.Square, [b_x[kc]], [b_sq[s_]])
            mms = [(bank(4 + t), ones_bf[:], sqb[s_][:, t * 512:(t + 1) * 512], kc == 0, kc == 7) for t in range(4)]
            mm_group(mms, [b_sq[s_], b_ones], [b_bank[4 + t] for t in range(4)])
            if kc % 2 == 1:
                mod_piece(kc // 2)
        stt("dve", p_gm1, sc1, 1.0, p_n1g, ALU.add, ALU.mult, [b_prm["mod"], b_prm["n1g"]], [b_prm["gm1"]])
        act(rstd[:], psum[:, 2048:4096], AF.Ln, [b_bank[4 + t] for t in range(4)], [b_rstd], bias=EPS, scale=1.0 / D)
        act(rstd[:], rstd[:], AF.Exp, [b_rstd], [b_rstd], scale=-0.5)
        for hf_ in range(2):
            hs_ = slice(hf_ * 1024, (hf_ + 1) * 1024)
            for kc in range(8):
                stt("dve", xfull[:, kc, hs_], xfull[:, kc, hs_], p_gm1[:, kc:kc + 1], rstd[:, hs_], ALU.mult, ALU.mult,
                    [b_x[kc], b_prm["gm1"], b_rstd], [b_xn[kc][hf_]])
                act(hy[:, kc, hs_], xfull[:, kc, hs_], AF.Identity, [b_xn[kc][hf_], b_prm["mod"]], [b_hh[kc][hf_]], bias=sh1[:, kc:kc + 1])
        if debug:
            for kc in range(8):
                for t in range(4):
                    dump(dbg["d_h"][kc * 128:(kc + 1) * 128, t * 512:(t + 1) * 512], hy[:, kc, t * 512:(t + 1) * 512], b_hh[kc])
            dma_sp(dbg["d_mod"][:, :], p_mod, "dbg0", reads=[b_prm["mod"]])

        _chk('p1')
        sch.alias(flat(b_xr) + flat(b_gg) + flat(b_q) + flat(b_k), b_x + flat(b_xn))
        sch.alias([b_acc, b_xcb, b_A, b_C, b_D] + b_B, b_sq + [b_rstd] + b_et)
        for j in range(4):
            sch.op("dve", lambda h, j=j: h.memset(xrp[:, j, 0:8], 0.0), writes=[b_xr[j][0]])

        pb = [0]

        def next_bank():
            b = pb[0] % 3
            pb[0] += 1
            return b

        ev = [0]

        def inproj_fm(piece, mc, t):
            slot = piece_slot("win", piece)
            bk = next_bank()
            mms = [(bank(bk), wring[slot][:, kc, mc * 128:(mc + 1) * 128], hy[:, kc, t * 512:(t + 1) * 512], kc == 0, kc == 7)
                   for kc in range(8)]
            mm_group(mms, [b_wr[slot]] + [b_hh[kc][t // 2] for kc in range(8)], [b_bank[bk]])
            c0 = t * 512
            if piece == 0:
                dst = xrp[:, mc, 8 + c0:8 + c0 + 512]
                if ev[0] % 2 == 0:
                    act(dst, bank(bk), AF.Copy, [b_bank[bk]], [b_xr[mc][t]])
                else:
                    sch.op("dve", lambda h, d=dst, s=bank(bk): h.tensor_copy(d, s), reads=[b_bank[bk]], writes=[b_xr[mc][t]])
                ev[0] += 1
            elif piece == 1:
                g_ = ev[0] % 2
                ev[0] += 1
                act(gt[g_][:], bank(bk), AF.Square, [b_bank[bk]], [b_gt[g_]], scale=float(np.sqrt(0.044715)))
                stt("dve", gt[g_][:], gt[g_][:], 1.0, bank(bk), ALU.add, ALU.mult, [b_gt[g_], b_bank[bk]], [b_gt[g_]])
                act(gt[g_][:], gt[g_][:], AF.Tanh, [b_gt[g_]], [b_gt[g_]], scale=float(np.sqrt(2.0 / np.pi)))
                stt("dve", gg[:, mc, c0:c0 + 512], gt[g_][:], 1.0, bank(bk), ALU.add, ALU.mult, [b_gt[g_], b_bank[bk]], [b_gg[mc][t]])
            elif piece == 2:
                dst = qs[:, mc, c0:c0 + 512]
                if ev[0] % 4 != 3:
                    act(dst, bank(bk), AF.Copy, [b_bank[bk]], [b_q[mc][t]], scale=0.125)
                else:
                    ts("dve", dst, bank(bk), 0.125, None, ALU.mult, None, [b_bank[bk]], [b_q[mc][t]])
                ev[0] += 1
            else:
                dst = ks[:, mc, c0:c0 + 512]
                if ev[0] % 4 != 3:
                    act(dst, bank(bk), AF.Copy, [b_bank[bk]], [b_k[mc][t]])
                else:
                    sch.op("dve", lambda h, d=dst, s=bank(bk): h.tensor_copy(d, s), reads=[b_bank[bk]], writes=[b_k[mc][t]])
                ev[0] += 1

        def inproj_v(t16):
            slot = piece_slot("win", 4)
            bk = next_bank()
            mms = [(bank(bk), hy[:, kc, t16 * 128:(t16 + 1) * 128], wring[slot][:, kc, :], kc == 0, kc == 7) for kc in range(8)]
            mm_group(mms, [b_wr[slot]] + [b_hh[kc][t16 // 8] for kc in range(8)], [b_bank[bk]])
            dst = vt[:, t16, :]
            if ev[0] % 4 != 3:
                act(dst, bank(bk), AF.Copy, [b_bank[bk]], [b_v[t16]])
            else:
                sch.op("dve", lambda h, d=dst, s=bank(bk): h.tensor_copy(d, s), reads=[b_bank[bk]], writes=[b_v[t16]])
            ev[0] += 1

        for piece in (0,):
            for t in range(4):
                for mc in range(4):
                    inproj_fm(piece, mc, t)

        _chk('p2a')
        def rglru_chunk(j):
            for hf in range(2):
                c0 = hf * 1024
                act(t_acc[:], xrp[:, j, 8 + c0:8 + c0 + 1024], AF.Identity, b_xr[j] + [b_prm["cw"], b_prm["cb"]], [b_acc],
                    bias=p_cb[:, j:j + 1], scale=p_cw[:, j * 4 + 3:j * 4 + 4])
                for k in range(3):
                    src = xrp[:, j, 5 + k + c0:5 + k + c0 + 1024]
                    if k < 2:
                        stt("dve", t_acc[:], src, p_cw[:, j * 4 + k:j * 4 + k + 1], t_acc[:], ALU.mult, ALU.add,
                            b_xr[j] + [b_prm["cw"], b_acc], [b_acc])
                    else:
                        stt("dve", t_xcb[:], src, p_cw[:, j * 4 + k:j * 4 + k + 1], t_acc[:], ALU.mult, ALU.add,
                            b_xr[j] + [b_prm["cw"], b_acc], [b_xcb])
                yield
                yield
                mm_group([(bank(4 + t), wabd[:, j * 128:(j + 1) * 128], t_xcb[:, t * 512:(t + 1) * 512], True, True) for t in range(2)],
                         [b_wabd, b_xcb], [b_bank[4], b_bank[5]])
                mm_group([(bank(6 + t), wxbd[:, j * 128:(j + 1) * 128], t_xcb[:, t * 512:(t + 1) * 512], True, True) for t in range(2)],
                         [b_wxbd, b_xcb], [b_bank[6], b_bank[7]])
                act(t_A[:], psum[:, 2048:3072], AF.Tanh, [b_bank[4], b_bank[5], b_prm["ba"]], [b_A], bias=p_ba[:, j:j + 1], scale=0.5)
                act(t_C[:], psum[:, 3072:4096], AF.Tanh, [b_bank[6], b_bank[7], b_prm["bx"]], [b_C], bias=p_bx[:, j:j + 1], scale=0.5)
                act(t_B[hf][:], t_A[:], AF.Exp, [b_A, b_prm["cl2"]], [b_B[hf]], bias=p_cl2[:, j:j + 1], scale=p_cl2[:, j:j + 1])
                act(t_A[:], t_A[:], AF.Exp, [b_A, b_prm["cl"]], [b_A], bias=p_cl[:, j:j + 1], scale=p_cl[:, j:j + 1])
                yield
                stt("dve", t_C[:], t_C[:], 1.0, t_xcb[:], ALU.add, ALU.mult, [b_C, b_xcb], [b_C])
                act(t_B[hf][:], t_B[hf][:], AF.Sqrt, [b_B[hf]], [b_B[hf]], bias=1.0 / 16, scale=-1.0 / 16)
                tt("dve", t_C[:], t_C[:], t_B[hf][:], ALU.mult, [b_C, b_B[hf]], [b_C])
                sch.op("dve", lambda h, j=j: h.tensor_tensor_scan(t_D[:], t_A[:], t_C[:], p_carry[:, j:j + 1], ALU.mult, ALU.add),
                       reads=[b_A, b_C, b_prm["carry"]], writes=[b_D])
                sch.op("dve", lambda h, j=j: h.tensor_copy(p_carry[:, j:j + 1], t_D[:, 1023:1024]), reads=[b_D], writes=[b_prm["carry"]])
                yield
                tt("dve", gg[:, j, c0:c0 + 1024], gg[:, j, c0:c0 + 1024], t_D[:], ALU.mult, b_gg[j][2 * hf:2 * hf + 2] + [b_D], b_gg[j][2 * hf:2 * hf + 2])
                yield

        TR = {0: (0, 1), 1: (0, 3), 2: (0, 5), 3: (0, 7), 4: (0, 7), 5: (2, 7), 6: (4, 7), 7: (6, 7)}
        pi = [0]

        def attention():
            LAG = 3
            SBP = [0, 4]
            steps = []
            for c in range(4):
                for qt in range(4):
                    js = [4] + [j for j in (0, 1, 2, 3, 5, 6, 7) if 4 * qt - 4 + j >= 0]
                    for idx, j in enumerate(js):
                        steps.append((c, qt, idx, j, len(js)))
            npairs = len(steps)
            psv = psum.rearrange("p (b n) -> p b n", n=512)

            def geom(k):
                c, qt, idx, j, nj = steps[k]
                tlo, thi = TR[j]
                return dict(c=c, qt=qt, idx=idx, j=j, nj=nj, gkb=4 * qt - 4 + j, tlo=tlo, nco=(thi - tlo + 1) * 64,
                            q0=qt * 512 + tlo * 64, d0=(tlo - 2 * j + 8) * 64, sb=SBP[k % 2], pp=pbuf2[k % 4], bpp=b_pp[k % 4],
                            ob=(2, 3) if (c * 4 + qt) % 2 == 0 else (6, 7))

            def front(k):
                g = geom(k)
                c, nco, sb = g["c"], g["nco"], g["sb"]
                mms = []
                for e in range(2):
                    pr = slice(e * 64, (e + 1) * 64)
                    mms.append((bank(sb + e, nco), ks[pr, c, g["gkb"] * 128:(g["gkb"] + 1) * 128], qs[pr, c, g["q0"]:g["q0"] + nco], True, False))
                for e in range(2):
                    mms.append((bank(sb + e, nco), ident_bf[:], Etab[:, 2 * c + e, g["d0"]:g["d0"] + nco], False, True))
                mm_group(mms, b_k[c] + b_q[c] + [b_ident, b_E[2 * c], b_E[2 * c + 1]], [b_bank[sb], b_bank[sb + 1]])
                act(g["pp"][:, :, 0:nco], psv[:, sb:sb + 2, 0:nco], AF.Exp, [b_bank[sb], b_bank[sb + 1]], [g["bpp"]])

            def back(k):
                g = geom(k)
                c, qt, nco, tlo = g["c"], g["qt"], g["nco"], g["tlo"]
                ob, db_ = g["ob"]
                first = g["idx"] == 0
                last = g["idx"] == g["nj"] - 1
                mms = []
                for e in range(2):
                    pr = slice(e * 64, (e + 1) * 64)
                    hh = 2 * c + e
                    mms.append((psum[pr, ob * 512 + tlo * 64:ob * 512 + tlo * 64 + nco], vt[:, g["gkb"], hh * 64:(hh + 1) * 64], g["pp"][:, e, 0:nco], first, last))
                for e in range(2):
                    pr = slice(e * 64, (e + 1) * 64)
                    mms.append((psum[pr, db_ * 512 + tlo * 64:db_ * 512 + tlo * 64 + nco], ones_bf[:, 0:64], g["pp"][:, e, 0:nco], first, last))
                mm_group(mms, [b_v[g["gkb"]], g["bpp"], b_ones], [b_bank[ob], b_bank[db_]])
                if last:
                    sch.op("dve", lambda h, db_=db_: h.reciprocal(rden[:], bank(db_)), reads=[b_bank[db_]], writes=[b_rden])
                    tt("dve", hy[:, 4 + c, qt * 512:(qt + 1) * 512], bank(ob), rden[:], ALU.mult, [b_bank[ob], b_rden],
                       [b_y[4 + c][qt]] + b_hh[4 + c])

            nmod = [12]
            for k in range(npairs + LAG):
                if k < npairs:
                    front(k)
                if k >= LAG:
                    back(k - LAG)
                if k % 12 == 8 and nmod[0] < 12:
                    mod_piece(nmod[0], mb=SBP[(k + 1) % 2])
                    nmod[0] += 1
                yield
            while nmod[0] < 12:
                mod_piece(nmod[0], mb=SBP[0])
                nmod[0] += 1

        def load_bias_tables():
            dma_pool(ident_bf[:], ident[:, :], "wg", writes=[b_ident])
            dma_pool(Etab[:], tbias.rearrange("h p n -> p h n"), "wg", writes=b_E)
            for hh in range(8):
                sch.op("dve", lambda h, hh=hh: h.memset(Etab[0:64, hh, 576:640], -30000.0), writes=[b_E[hh]])
                sch.op("dve", lambda h, hh=hh: h.memset(Etab[64:128, hh, 0:64], -30000.0), writes=[b_E[hh]])

        def inproj_rest():
            cnt = [0]
            nm = [4]

            def tick():
                cnt[0] += 1
                if cnt[0] % 8 == 4 and nm[0] < 12:
                    mod_piece2(nm[0])
                    nm[0] += 1

            issue_a(6)
            for piece in (1,):
                for mc in range(4):
                    for t in range(4):
                        inproj_fm(piece, mc, t)
                        tick()
                        yield
            load_bias_tables()
            for piece in (2, 3):
                for mc in range(4):
                    for t in range(4):
                        inproj_fm(piece, mc, t)
                        tick()
                        yield
            for t16 in range(16):
                inproj_v(t16)
                tick()
                yield
            while nm[0] < 12:
                mod_piece2(nm[0])
                nm[0] += 1

        def rglru_all():
            for j in range(4):
                yield from rglru_chunk(j)

        ga, gb = rglru_all(), inproj_rest()
        NA, NB = 40, 64
        ia = ib = 0
        da = db = False
        while not (da and db):
            take_a = (not da) and (db or ia * NB <= ib * NA)
            if take_a:
                try:
                    next(ga)
                    ia += 1
                except StopIteration:
                    da = True
            else:
                try:
                    next(gb)
                    ib += 1
                except StopIteration:
                    db = True
        for _ in attention():
            pass

        _chk('p3')
        stt("dve", p_gm2, sc2, 1.0, p_n2g, ALU.add, ALU.mult, [b_prm["mod"], b_prm["n2g"]], [b_prm["gm2"]])

        if debug:
            for j in range(4):
                for t in range(4):
                    cs = slice(t * 512, (t + 1) * 512)
                    dump(dbg["d_xr"][j * 128:(j + 1) * 128, cs], xrp[:, j, 8 + t * 512:8 + (t + 1) * 512], b_xr[j])
                    dump(dbg["d_q"][j * 128:(j + 1) * 128, cs], qs[:, j, cs], b_q[j])
                    dump(dbg["d_k"][j * 128:(j + 1) * 128, cs], ks[:, j, cs], b_k[j])
            for t16 in range(16):
                dump(dbg["d_v"][t16 * 128:(t16 + 1) * 128, :], vt[:, t16, :], [b_v[t16]])
            for j in range(8):
                for t in range(4):
                    cs = slice(t * 512, (t + 1) * 512)
                    dump(dbg["d_y"][j * 128:(j + 1) * 128, cs], gg[:, j, cs] if j < 4 else hy[:, j, cs], b_gg[j] if j < 4 else b_y[j])

        _chk('p3d')
        xi = [0]

        def xload(dst, bdst, hf, kc):
            dma_sp(dst, xT[kc * 128:(kc + 1) * 128, hf * 1024:(hf + 1) * 1024], "x%d" % (xi[0] % 4), writes=[bdst])
            xi[0] += 1

        def add_delta(dst, bdst, hf, kc):
            tt("dve", dst, dst, delta1[:, kc, hf * 1024:(hf + 1) * 1024], ALU.add, [bdst] + b_dl[kc], [bdst])

        def stats_sq(src, bsrc, i):
            act(sq2[i % 2][:], src, AF.Square, [bsrc], [b_sq2[i % 2]])

        def stats_mm(i, b0_, n_):
            mm_group([(bank(b0_ + t), ones_bf[:], sq2[i % 2][:, t * 512:(t + 1) * 512], i == 0, i == n_ - 1) for t in range(2)],
                     [b_sq2[i % 2], b_ones], [b_bank[b0_], b_bank[b0_ + 1]])

        def fstat_sq(m):
            act(sq3[m % 2][:], x2h[:, m, :], AF.Square, [b_x2[m]], [b_ca[m % 2], b_cah[m % 2]])

        def fstat_mm(m, b0_):
            mm_group([(bank(b0_ + t), ones_bf[:], sq3[m % 2][:, t * 512:(t + 1) * 512], m == 0, m == 7) for t in range(2)],
                     [b_ca[m % 2], b_cah[m % 2], b_ones], [b_bank[b0_], b_bank[b0_ + 1]])

        def make_rstd(b0_):
            act(rstd2[:], psum[:, b0_ * 512:b0_ * 512 + 1024], AF.Ln, [b_bank[b0_], b_bank[b0_ + 1]], [b_rstd2], bias=EPS, scale=1.0 / D)
            act(rstd2[:], rstd2[:], AF.Exp, [b_rstd2], [b_rstd2], scale=-0.5)

        sch.alias(b_x2, b_v + [b_acc, b_xcb, b_A, b_C, b_D] + b_B)
        sch.alias(b_sq2, [b for row in b_hh[0:4] for b in row])
        for kc in range(8):
            xload(x2h[:, kc, :], b_x2[kc], 0, kc)

        sch.alias(flat(b_dl), flat(b_q) + flat(b_k))
        y_all = flat(b_gg) + [b for row in b_y[4:] for b in row]

        def ysrc(kc, cs):
            return gg[:, kc, cs] if kc < 4 else hy[:, kc, cs]
        for m in range(8):
            slot = piece_slot("wout", m // 4)
            mo = (m % 4) * 128
            for t in range(4):
                bk = next_bank()
                mms = [(bank(bk), wring[slot][:, kc, mo:mo + 128], ysrc(kc, slice(t * 512, (t + 1) * 512)), kc == 0, kc == 7) for kc in range(8)]
                mm_group(mms, [b_wr[slot]] + [b_gg[kc][t] for kc in range(4)] + [b_y[kc][t] for kc in range(4, 8)], [b_bank[bk]])
                dst = delta1[:, m, t * 512:(t + 1) * 512]
                if ev[0] % 2 == 0:
                    act(dst, bank(bk), AF.Copy, [b_bank[bk], b_prm["mod"]], [b_dl[m][t]], scale=g1m[:, m:m + 1])
                else:
                    ts("dve", dst, bank(bk), g1m[:, m:m + 1], None, ALU.mult, None, [b_bank[bk], b_prm["mod"]], [b_dl[m][t]])
                ev[0] += 1
            add_delta(x2h[:, m, :], b_x2[m], 0, m)
            stats_sq(x2h[:, m, :], b_x2[m], m)
            if m >= 1:
                stats_mm(m - 1, 6, 8)
        stats_mm(7, 6, 8)
        if debug:
            for j in range(8):
                for t in range(4):
                    cs = slice(t * 512, (t + 1) * 512)
                    dump(dbg["d_dl"][j * 128:(j + 1) * 128, cs], delta1[:, j, cs], b_dl[j])

        _chk('p4')
        sch.barrier()
        uiss = [0]
        upieces = []
        for hf in range(2):
            for f in range(NFC):
                for part in range(2):
                    upieces.append(wup_v[:, :, part * DFF + f * 128:part * DFF + (f + 1) * 128])

        def issue_u(upto):
            while uiss[0] < min(upto, len(upieces)):
                n = uiss[0]
                dma_pool(uring[n % NU][:], upieces[n], "u%d" % (n % NU), writes=[b_ur[n % NU]])
                uiss[0] += 1

        diss = [0]
        dpieces = [wdn_v[:, :, m * 128:(m + 1) * 128] for hf in range(2) for m in range(8)]

        def issue_d(upto):
            while diss[0] < min(upto, len(dpieces)):
                n = diss[0]
                for hh_, (ka, kb_) in enumerate(((0, 8), (8, 16), (16, 22))):
                    dma_pool(dring[n % ND][:, ka:kb_, :], dpieces[n][:, ka:kb_, :],
                             "d%d_%d" % (n % ND, hh_), writes=[b_dr2[n % ND][hh_]])
                diss[0] += 1

        issue_u(NU)
        pend = [None]
        ub = [0]
        oi = [0]
        def side_runner(units):
            it = iter(units)

            def run(n_):
                for _ in range(n_):
                    u = next(it, None)
                    if u is None:
                        return
                    u()
            return run

        make_rstd(6)
        for kc in range(8):
            s_ = kc % 2
            stt("dve", ntmp[s_][:], x2h[:, kc, :], p_gm2[:, kc:kc + 1], rstd2[:], ALU.mult, ALU.mult,
                [b_x2[kc], b_prm["gm2"], b_rstd2], [b_nt[s_]])
            act(h2b[0][:, kc, :], ntmp[s_][:], AF.Identity, [b_nt[s_], b_prm["mod"]], [b_h2[0][kc]], bias=sh2[:, kc:kc + 1])
        _chk('f_norm0')
        issue_d(ND - 1)

        def n2_units():
            us_ = []
            us_.append(lambda: xload(ntmp[0][:], b_nt[0], 1, 0))
            for kc in range(8):
                def p1(kc=kc):
                    if kc + 1 < 8:
                        xload(ntmp[(kc + 1) % 2][:], b_nt[(kc + 1) % 2], 1, kc + 1)
                    add_delta(ntmp[kc % 2][:], b_nt[kc % 2], 1, kc)
                    stats_sq(ntmp[kc % 2][:], b_nt[kc % 2], kc)
                us_.append(p1)
                if kc >= 1:
                    us_.append(lambda kc=kc: stats_mm(kc - 1, 4, 8))
            us_.append(lambda: stats_mm(7, 4, 8))
            us_.append(lambda: xload(ntmp[0][:], b_nt[0], 1, 0))
            us_.append(lambda: make_rstd(4))
            for kc in range(8):
                def p2(kc=kc):
                    if kc + 1 < 8:
                        xload(ntmp[(kc + 1) % 2][:], b_nt[(kc + 1) % 2], 1, kc + 1)
                    s_ = kc % 2
                    add_delta(ntmp[s_][:], b_nt[s_], 1, kc)
                    stt("dve", ntmp[s_][:], ntmp[s_][:], p_gm2[:, kc:kc + 1], rstd2[:], ALU.mult, ALU.mult,
                        [b_nt[s_], b_prm["gm2"], b_rstd2], [b_nt[s_]])
                    act(h2b[1][:, kc, :], ntmp[s_][:], AF.Identity, [b_nt[s_], b_prm["mod"]], [b_h2[1][kc]], bias=sh2[:, kc:kc + 1])
                us_.append(p2)
            return us_

        def fin_units(hf, preload_next):
            us_ = []
            if preload_next:
                for m in range(8):
                    us_.append(lambda m=m: stats_sq(x2h[:, m, :], b_x2[m], m))
                    if m >= 1:
                        us_.append(lambda m=m: stats_mm(m - 1, 6, 8))
                us_.append(lambda: stats_mm(7, 6, 8))
                us_.append(lambda: make_rstd(6))
            else:
                us_.append(lambda: make_rstd(4))
            for m in range(8):
                def st(m=m):
                    if preload_next or m % 3 == 2:
                        act(x2h[:, m, :], x2h[:, m, :], AF.Copy, [b_x2[m], b_prm["nfg"]], [b_x2[m]], scale=p_nfg[:, m:m + 1])
                        tt("pool", x2h[:, m, :], x2h[:, m, :], rstd2[:], ALU.mult, [b_x2[m], b_rstd2], [b_x2[m]])
                    else:
                        stt("dve", x2h[:, m, :], x2h[:, m, :], p_nfg[:, m:m + 1], rstd2[:], ALU.mult, ALU.mult,
                            [b_x2[m], b_prm["nfg"], b_rstd2], [b_x2[m]])
                    dma_sp(outT[m * 128:(m + 1) * 128, hf * 1024:(hf + 1) * 1024], x2h[:, m, :], "o%d" % (oi[0] % 4), reads=[b_x2[m]])
                    oi[0] += 1
                us_.append(st)
            if preload_next:
                for kc in range(8):
                    us_.append(lambda kc=kc: xload(x2h[:, kc, :], b_x2[kc], hf + 1, kc))
                for kc in range(8):
                    us_.append(lambda kc=kc: tt("pool", x2h[:, kc, :], x2h[:, kc, :], delta1[:, kc, (hf + 1) * 1024:(hf + 2) * 1024], ALU.add,
                                                [b_x2[kc]] + b_dl[kc], [b_x2[kc]]))
            return us_

        side = side_runner([])
        for hf in range(2):
            h2h = h2b[hf]
            bh2 = b_h2[hf]
            for f in range(NFC):
                for part in range(2):
                    n = (hf * NFC + f) * 2 + part
                    issue_u(n + NU)
                    us = n % NU
                    ci = part * NFC + f
                    b0_ = (ub[0] % 3) * 2
                    ub[0] += 1
                    mms = []
                    for t in range(2):
                        for kc in range(8):
                            mms.append((bank(b0_ + t), uring[us][:, kc, :], h2h[:, kc, t * 512:(t + 1) * 512], kc == 0, kc == 7))
                    if hf == 0 and f == 0 and part == 0:
                        for kc in range(8):
                            mm_group([(bank(b0_ + t), uring[us][:, kc, :], h2h[:, kc, t * 512:(t + 1) * 512], kc == 0, kc == 7)
                                      for t in range(2)], [b_ur[us], bh2[kc]], [b_bank[b0_], b_bank[b0_ + 1]])
                    else:
                        mm_group(mms, [b_ur[us]] + bh2, [b_bank[b0_], b_bank[b0_ + 1]])
                    pu = psum[:, b0_ * 512:b0_ * 512 + 1024]
                    bks = [b_bank[b0_], b_bank[b0_ + 1]]
                    ca = cacc[part * 2 + (f % 2)]
                    bca = b_ca[part * 2 + (f % 2)]
                    w0 = p_fcw[:, ci * 3 + 0:ci * 3 + 1]
                    w1 = p_fcw[:, ci * 3 + 1:ci * 3 + 2]
                    w2 = p_fcw[:, ci * 3 + 2:ci * 3 + 3]
                    hzs = (n % 2) * 4
                    hz = p_hz[:, hzs:hzs + 4]
                    bhz = b_hz[n % 2]
                    bcah = b_cah[part * 2 + (f % 2)]
                    hrd = (1 - hf) * 88 + 2 * ci
                    hwr = hf * 88 + 2 * ci
                    sch.op("dve", lambda h, hrd=hrd, hz=hz: h.tensor_copy(hz[:, 0:2], p_halo[:, hrd:hrd + 2]),
                           reads=[b_halo[1 - hf]], writes=[bhz[0]])
                    act(hz[:, 2:4], pu[:, 0:2], AF.Copy, bks, [bhz[1]])
                    act(p_halo[:, hwr:hwr + 2], pu[:, 1022:1024], AF.Copy, bks, [b_halo[hf]])
                    act(ca[:], pu, AF.Identity, bks + [b_prm["fcw"], b_prm["fcb"]], [bca, bcah], bias=p_fcb[:, ci:ci + 1], scale=w2)
                    stt("dve", ca[:, 2:1024], pu[:, 1:1023], w1, ca[:, 2:1024], ALU.mult, ALU.add, bks + [bca, b_prm["fcw"]], [bca])
                    stt("dve", ca[:, 2:1024], pu[:, 0:1022], w0, ca[:, 2:1024], ALU.mult, ALU.add, bks + [bca, b_prm["fcw"]], [bca])
                    stt("dve", ca[:, 0:2], hz[:, 1:3], w1, ca[:, 0:2], ALU.mult, ALU.add, [bhz[0], bhz[1], bcah, b_prm["fcw"]], [bcah])
                    stt("dve", ca[:, 0:2], hz[:, 0:2], w0, ca[:, 0:2], ALU.mult, ALU.add, [bhz[0], bhz[1], bcah, b_prm["fcw"]], [bcah])
                    if pend[0] is not None:
                        pend[0]()
                    if part == 0:
                        pend[0] = (lambda ca=ca, bca=bca, bcah=bcah: act(ca[:], ca[:], AF.Silu, [bca, bcah], [bca, bcah]))
                    else:
                        pend[0] = (lambda f=f, ca=ca, bca=bca, bcah=bcah: tt("pool", gbuf[:, f, :], cacc[f % 2][:], ca[:], ALU.mult,
                                                                         [b_ca[f % 2], b_cah[f % 2], bca, bcah], [b_g[f]]))
                side(2)
            if pend[0] is not None:
                pend[0]()
                pend[0] = None
            side(1000)
            _chk('f_up%d' % hf)
            side = side_runner(n2_units() if hf == 0 else [])
            fsb = 2 if hf == 0 else 4
            for m in range(8):
                n = hf * 8 + m
                issue_d(n + ND)
                ds_ = n % ND
                if hf == 1:
                    add_delta(x2h[:, m, :], b_x2[m], 1, m)
                for t in range(2):
                    bk = 6 + t
                    mms = [(bank(bk), dring[ds_][:, kc, :], gbuf[:, kc, t * 512:(t + 1) * 512], kc == 0, kc == NFC - 1) for kc in range(NFC)]
                    if m == 0 and t == 0:
                        mm_group(mms[:16], b_dr2[ds_] + b_g[:16], [b_bank[bk]])
                        mm_group(mms[16:20], b_dr2[ds_] + b_g[16:20], [b_bank[bk]])
                        mm_group(mms[20:], b_dr2[ds_] + b_g[20:], [b_bank[bk]])
                    else:
                        mm_group(mms, b_dr2[ds_] + b_g, [b_bank[bk]])
                    stt("dve", x2h[:, m, t * 512:(t + 1) * 512], bank(bk), g2m[:, m:m + 1], x2h[:, m, t * 512:(t + 1) * 512],
                        ALU.mult, ALU.add, [b_bank[bk], b_prm["mod"], b_x2[m]], [b_x2[m]])
                    side(2)
                fstat_sq(m)
                if m >= 1:
                    fstat_mm(m - 1, fsb)
            fstat_mm(7, fsb)
            side(1000)
            _chk('f_down%d' % hf)
            make_rstd(fsb)
            for m in range(8):
                if hf == 1 and m % 3 == 2:
                    act(x2h[:, m, :], x2h[:, m, :], AF.Copy, [b_x2[m], b_prm["nfg"]], [b_x2[m]], scale=p_nfg[:, m:m + 1])
                    tt("pool", x2h[:, m, :], x2h[:, m, :], rstd2[:], ALU.mult, [b_x2[m], b_rstd2], [b_x2[m]])
                else:
                    stt("dve", x2h[:, m, :], x2h[:, m, :], p_nfg[:, m:m + 1], rstd2[:], ALU.mult, ALU.mult,
                        [b_x2[m], b_prm["nfg"], b_rstd2], [b_x2[m]])
                dma_sp(outT[m * 128:(m + 1) * 128, hf * 1024:(hf + 1) * 1024], x2h[:, m, :], "o%d" % (oi[0] % 4), reads=[b_x2[m]])
                oi[0] += 1
            if hf == 0:
                for kc in range(8):
                    xload(x2h[:, kc, :], b_x2[kc], 1, kc)
            side = side_runner([])

    except _Stop:
        pass

    fin = [(s, c) for s, c in sch.cnt.items() if s.startswith("o") or s.startswith("dbg")]
    sch.wait_only("sp", fin)

    sem_names = sorted(sch.cnt.keys())
    sems = {}
    import contextlib
    with contextlib.ExitStack() as es:
        for nme in sem_names:
            sems[nme] = es.enter_context(nc.semaphore("s_" + nme))
        block = es.enter_context(nc.Block())
        handles = {"pe": block.tensor, "act": block.scalar, "dve": block.vector, "pool": block.gpsimd, "sp": block.sync}
        for e in Sched.ENGS:
            ops = sch.ops[e]

            def body(h, ops=ops, e=e):
                for waits, fn, dma in ops:
                    for s_, c_ in waits:
                        h.wait_ge(sems[s_], c_)
                    if fn is None:
                        continue
                    ins = fn(h)
                    if dma is not None:
                        ins.then_inc(sems[dma], 16)
                    else:
                        ins.then_inc(sems[e], 1)
            handles[e](body)
    return nc


def _fm(v):
    v = np.asarray(v, np.float32)
    return np.ascontiguousarray(v.reshape(-1, 128).T)


def prep_inputs(b, x, c, ada_w, ada_b, norm1_g, w_in, rnn_conv_w, rnn_conv_b, rg_wa, rg_ba, rg_wx, rg_bx,
                rg_lambda, rel_bias, w_out, norm2_g, w_up, ffn_conv_w, ffn_conv_b, w_down, final_g, shared):
    m = dict(shared)
    m["xT"] = np.ascontiguousarray(np.asarray(x[b], np.float32).T)
    m["cT"] = _fm(c[b])
    return m


def prep_shared(ada_w, ada_b, norm1_g, w_in, rnn_conv_w, rnn_conv_b, rg_wa, rg_ba, rg_wx, rg_bx,
                rg_lambda, rel_bias, w_out, norm2_g, w_up, ffn_conv_w, ffn_conv_b, w_down, final_g):
    f = np.float32
    sh = {}
    sh["ada_w"] = np.ascontiguousarray(np.asarray(ada_w[0], f))
    sh["ada_bT"] = _fm(ada_b[0])
    sh["n1g"] = _fm(norm1_g[0])
    sh["n2g"] = _fm(norm2_g[0])
    sh["nfg"] = _fm(final_g)
    sh["w_in"] = np.ascontiguousarray(np.asarray(w_in[0], f))
    cwv = np.asarray(rnn_conv_w[0], f)
    sh["cw"] = np.ascontiguousarray(cwv.reshape(4, 4, 128).transpose(2, 1, 0).reshape(128, 16))
    sh["cb"] = _fm(rnn_conv_b[0])

    def bd(w):
        w = np.asarray(w, f)
        o = np.zeros((128, 4, 128), f)
        for j in range(4):
            o[0:64, j, 0:64] = w[2 * j]
            o[64:128, j, 64:128] = w[2 * j + 1]
        return np.ascontiguousarray(o.reshape(128, 512))
    sh["wa_bd"] = bd(rg_wa[0])
    sh["wx_bd"] = bd(rg_wx[0])
    sh["rba"] = _fm(rg_ba[0])
    sh["rbx"] = _fm(rg_bx[0])
    sh["rlam"] = _fm(rg_lambda[0])
    kk = np.arange(128)[:, None]
    col = np.arange(640)[None, :]
    idx = np.clip(col - kk, -128, 128) + 128
    sh["tbias"] = np.ascontiguousarray(np.asarray(rel_bias[0], f)[:, idx])
    sh["ident"] = np.eye(128, dtype=f)
    sh["w_out"] = np.ascontiguousarray(np.asarray(w_out[0], f))
    sh["w_up"] = np.ascontiguousarray(np.asarray(w_up[0], f))
    fw = np.asarray(ffn_conv_w[0], f)
    sh["fcw"] = np.ascontiguousarray(fw.reshape(3, 44, 128).transpose(2, 1, 0).reshape(128, 132))
    sh["fcb"] = _fm(ffn_conv_b[0])
    sh["w_down"] = np.ascontiguousarray(np.asarray(w_down[0], f))
    return sh


_NC_CACHE = {}


def kernel(x, c, ada_w, ada_b, norm1_g, w_in, rnn_conv_w, rnn_conv_b, rg_wa, rg_ba, rg_wx, rg_bx,
           rg_lambda, rel_bias, w_out, norm2_g, w_up, ffn_conv_w, ffn_conv_b, w_down, final_g, _debug=False, _stop=None):
    x = np.asarray(x)
    c = np.asarray(c)
    shared = prep_shared(ada_w, ada_b, norm1_g, w_in, rnn_conv_w, rnn_conv_b, rg_wa, rg_ba, rg_wx, rg_bx,
                         rg_lambda, rel_bias, w_out, norm2_g, w_up, ffn_conv_w, ffn_conv_b, w_down, final_g)
    in_maps = []
    for b in range(NCORE):
        m = dict(shared)
        m["xT"] = np.ascontiguousarray(np.asarray(x[b], np.float32).T)
        m["cT"] = _fm(c[b])
        in_maps.append(m)
    nc = build_program(debug=_debug, stop_at=_stop)
    res = run_bass_kernel_spmd(nc, in_maps, core_ids=list(range(NCORE)))
    out = np.stack([np.ascontiguousarray(res.results[b]["outT"].T) for b in range(NCORE)], axis=0).astype(np.float32)
    if _debug:
        return out, res.results
    return out
```

```python
import numpy as np
import concourse.bass as bass
import concourse.mybir as mybir
from concourse.bass_utils import run_bass_kernel_spmd

F32 = mybir.dt.float32
BF16 = mybir.dt.bfloat16
ALU = mybir.AluOpType
AF = mybir.ActivationFunctionType

S = 2048
D = 1024
NCORE = 8
DFF = 2816
NFC = DFF // 128
EPS = 1e-6
ATT_WARM = 3
NU = 5
ND = 2
B0 = 16512
SB_END = 229344


class Buf:
    __slots__ = ("name", "w", "r")

    def __init__(self, name):
        self.name = name
        self.w = None
        self.r = []


class Sched:
    ENGS = ["pe", "act", "dve", "pool", "sp"]

    def __init__(self):
        self.ops = {e: [] for e in self.ENGS}
        self.cnt = {}
        self.seen = {e: {} for e in self.ENGS}

    def _need(self, eng, waits, tok, war=False):
        if tok is None:
            return
        s, c = tok
        if s == eng and eng == "pe":
            return
        if self.seen[eng].get(s, 0) >= c:
            return
        if waits.get(s, 0) < c:
            waits[s] = c

    def op(self, eng, fn, reads=(), writes=(), dma=None):
        waits = {}
        for b in reads:
            self._need(eng, waits, b.w)
        for b in writes:
            self._need(eng, waits, b.w)
            for t in b.r:
                self._need(eng, waits, t, war=True)
        if dma is not None:
            c = self.cnt.get(dma, 0)
            if c > 0:
                self._need(eng, waits, (dma, c))
            self.cnt[dma] = c + 16
            tok = (dma, c + 16)
        else:
            self.cnt[eng] = self.cnt.get(eng, 0) + 1
            tok = (eng, self.cnt[eng])
        for s, c in waits.items():
            self.seen[eng][s] = c
        self.ops[eng].append((sorted(waits.items()), fn, dma))
        for b in reads:
            b.r.append(tok)
        for b in writes:
            b.w = tok
            b.r = []
        return tok

    def wait_only(self, eng, toks):
        waits = {}
        for t in toks:
            self._need(eng, waits, t)
        for s, c in waits.items():
            self.seen[eng][s] = c
        self.ops[eng].append((sorted(waits.items()), None, None))

    def barrier(self):
        toks = [(s, c) for s, c in self.cnt.items()]
        for e in self.ENGS:
            self.wait_only(e, toks)

    def alias(self, new_bufs, old_bufs):
        toks = []
        for b in old_bufs:
            if b.w is not None:
                toks.append(b.w)
            toks.extend(b.r)
        for nb in new_bufs:
            nb.w = None
            nb.r = list(toks)


class _Stop(Exception):
    pass


def build_program(debug=False, stop_at=None):
    nc = bass.Bass("TRN2", target_bir_lowering=False)
    sch = Sched()

    def din(name, shape):
        return nc.dram_tensor(name, list(shape), F32, kind="ExternalInput").ap()

    xT = din("xT", [D, S])
    cT = din("cT", [128, 8])
    ada_w = din("ada_w", [D, 6 * D])
    ada_bT = din("ada_bT", [128, 48])
    n1g = din("n1g", [128, 8])
    n2g = din("n2g", [128, 8])
    nfg = din("nfg", [128, 8])
    w_in = din("w_in", [D, 2560])
    cw = din("cw", [128, 16])
    cb = din("cb", [128, 4])
    wa_bd = din("wa_bd", [128, 512])
    wx_bd = din("wx_bd", [128, 512])
    rba = din("rba", [128, 4])
    rbx = din("rbx", [128, 4])
    rlam = din("rlam", [128, 4])
    tbias = din("tbias", [8, 128, 640])
    ident = din("ident", [128, 128])
    w_out = din("w_out", [D, D])
    w_up = din("w_up", [D, 2 * DFF])
    fcw = din("fcw", [128, 44 * 3])
    fcb = din("fcb", [128, 44])
    w_down = din("w_down", [DFF, D])
    outT = nc.dram_tensor("outT", [D, S], F32, kind="ExternalOutput").ap()
    dbg = {}
    if debug:
        for nm, shp in [("d_h", [D, S]), ("d_xr", [512, S]), ("d_gg", [512, S]), ("d_q", [512, S]),
                        ("d_k", [512, S]), ("d_v", [S, 512]), ("d_y", [D, S]), ("d_dl", [D, S]),
                        ("d_mod", [128, 48])]:
            dbg[nm] = nc.dram_tensor(nm, shp, F32, kind="ExternalOutput").ap()

    ada_v = ada_w.rearrange("(kc p) n -> p kc n", p=128)
    win_v = w_in.rearrange("(kc p) n -> p kc n", p=128)
    wout_v = w_out.rearrange("(kc p) n -> p kc n", p=128)
    wup_v = w_up.rearrange("(kc p) n -> p kc n", p=128)
    wdn_v = w_down.rearrange("(kc p) n -> p kc n", p=128)

    off = [B0]

    def alloc(name, shape, dt, at=None):
        nbytes = int(np.prod(shape[1:])) * (4 if dt == F32 else 2)
        if at is None:
            at = off[0]
            off[0] += (nbytes + 31) // 32 * 32
            assert off[0] <= SB_END, (name, off[0])
        return nc.alloc_sbuf_tensor_at(name, list(shape), dt, offset=at)

    prm = alloc("prm", [128, 576], F32)
    PC = {}
    pc = [0]

    def pslot(name, n):
        PC[name] = (pc[0], n)
        pc[0] += n
        assert pc[0] <= 576
        return prm[:, PC[name][0]:PC[name][0] + n]

    p_c = pslot("c", 8)
    p_adab = pslot("adab", 48)
    p_mod = pslot("mod", 48)
    p_n1g = pslot("n1g", 8)
    p_n2g = pslot("n2g", 8)
    p_nfg = pslot("nfg", 8)
    p_gm1 = pslot("gm1", 8)
    p_gm2 = pslot("gm2", 8)
    p_cw = pslot("cw", 16)
    p_cb = pslot("cb", 4)
    p_ba = pslot("ba", 4)
    p_bx = pslot("bx", 4)
    p_lam = pslot("lam", 4)
    p_cl = pslot("cl", 4)
    p_cl2 = pslot("cl2", 4)
    p_tmp = pslot("tmp", 8)
    p_fcw = pslot("fcw", 132)
    p_fcb = pslot("fcb", 44)
    p_carry = pslot("carry", 4)
    p_hz = pslot("hz", 8)
    p_hzt = pslot("hzt", 8)
    p_halo = pslot("halo", 176)
    b_prm = {k: Buf("prm_" + k) for k in PC}
    c_bf = alloc("c_bf", [128, 8], BF16)
    b_cbf = Buf("c_bf")
    ones_bf = alloc("ones_bf", [128, 128], BF16)
    b_ones = Buf("ones")
    ident_bf = alloc("ident_bf", [128, 128], BF16)
    b_ident = Buf("ident")
    wabd = alloc("wabd", [128, 512], BF16)
    wxbd = alloc("wxbd", [128, 512], BF16)
    b_wabd, b_wxbd = Buf("wabd"), Buf("wxbd")
    Etab = alloc("Etab", [128, 8, 640], BF16)
    b_E = [Buf("E%d" % h) for h in range(8)]

    STAGE0 = off[0]
    XR0 = off[0]
    xfull = alloc("xfull", [128, 8, 2048], F32)
    b_x = [Buf("x%d" % i) for i in range(8)]
    xrp = alloc("xrp", [128, 4, 2056], BF16, at=XR0)
    gg = alloc("gg", [128, 4, 2048], BF16, at=XR0 + 16448)
    qs = alloc("qs", [128, 4, 2048], BF16, at=XR0 + 16448 + 16384)
    ks = alloc("ks", [128, 4, 2048], BF16, at=XR0 + 16448 + 32768)
    off[0] = XR0 + 16448 + 49152
    DL0 = XR0 + 16448 + 16384
    delta1 = alloc("delta1", [128, 8, 2048], BF16, at=DL0)
    b_xr = [[Buf("xr%d_%d" % (i, t)) for t in range(4)] for i in range(4)]
    b_gg = [[Buf("gg%d_%d" % (i, t)) for t in range(4)] for i in range(4)]
    b_q = [[Buf("q%d_%d" % (i, t)) for t in range(4)] for i in range(4)]
    b_k = [[Buf("k%d_%d" % (i, t)) for t in range(4)] for i in range(4)]

    def flat(ll):
        return [b for row in ll for b in row]
    b_dl = [[Buf("dl%d_%d" % (i, t)) for t in range(4)] for i in range(8)]
    hy = alloc("hy", [128, 8, 2048], BF16)
    b_hh = [[Buf("h%d_%d" % (i, k)) for k in range(2)] for i in range(8)]
    b_h = [b for row in b_hh for b in row]
    b_xn = [[Buf("xn%d_%d" % (i, k)) for k in range(2)] for i in range(8)]
    b_y = [[Buf("y%d_%d" % (i, t)) for t in range(4)] for i in range(8)]
    vt = alloc("vt", [128, 16, 512], BF16)
    b_v = [Buf("v%d" % i) for i in range(16)]
    TMP0 = off[0]
    t_acc = alloc("t_acc", [128, 1024], F32)
    t_xcb = alloc("t_xcb", [128, 1024], BF16)
    t_A = alloc("t_A", [128, 1024], F32)
    t_B = [alloc("t_B%d" % i, [128, 1024], F32) for i in range(2)]
    t_C = alloc("t_C", [128, 1024], F32)
    t_D = alloc("t_D", [128, 1024], F32)
    b_acc, b_xcb, b_A, b_C, b_D = Buf("acc"), Buf("xcb"), Buf("A"), Buf("C"), Buf("D")
    b_B = [Buf("B0"), Buf("B1")]
    rstd = alloc("rstd", [128, 2048], F32, at=TMP0)
    sqb = [alloc("sqb%d" % i, [128, 2048], BF16, at=TMP0 + 8192 + 4096 * i) for i in range(2)]
    b_rstd = Buf("rstd")
    b_sq = [Buf("sq0"), Buf("sq1")]
    wring = [alloc("wring%d" % i, [128, 8, 512], BF16) for i in range(3)]
    b_wr = [Buf("wr%d" % i) for i in range(3)]
    pbuf2 = [alloc("pbuf%d" % i, [128, 2, 512], BF16) for i in range(4)]
    b_pp = [Buf("pp%d" % i) for i in range(4)]
    etmp = [nc.alloc_sbuf_tensor_at("etmp%d" % i, [128, 640], F32, offset=TMP0 + 2560 * i) for i in range(2)]
    b_et = [Buf("et0"), Buf("et1")]
    gt = [alloc("gt%d" % i, [128, 512], F32) for i in range(2)]
    b_gt = [Buf("gt0"), Buf("gt1")]
    rden = alloc("rden", [128, 512], F32)
    b_rden = Buf("rden")
    aring = [alloc("aring%d" % i, [128, 8, 512], BF16) for i in range(2)]
    b_ar = [Buf("ar0"), Buf("ar1")]
    STAGEA_END = off[0]

    off[0] = STAGE0
    h2b = [alloc("h2h%d" % i, [128, 8, 1024], BF16) for i in range(2)]
    b_h2 = [[Buf("h2_%d_%d" % (k, i)) for i in range(8)] for k in range(2)]
    b_ca = [Buf("ca%d" % i) for i in range(4)]
    b_cah = [Buf("cah%d" % i) for i in range(4)]
    b_halo = [Buf("halo0"), Buf("halo1")]
    b_hz = [[Buf("hz%d_%d" % (i, k)) for k in range(3)] for i in range(2)]
    assert off[0] <= DL0
    HY0 = DL0 + 32768
    off[0] = HY0
    sq2 = [alloc("sq2_%d" % i, [128, 1024], BF16) for i in range(2)]
    b_sq2 = [Buf("sq2_0"), Buf("sq2_1")]
    rstd2 = alloc("rstd2", [128, 1024], F32)
    b_rstd2 = Buf("rstd2")
    ntmp = [alloc("ntmp%d" % i, [128, 1024], F32) for i in range(2)]
    b_nt = [Buf("nt0"), Buf("nt1")]
    assert off[0] == HY0 + 16384, off[0]
    CACC0 = off[0]
    cacc = [alloc("cacc%d" % i, [128, 1024], F32) for i in range(4)]
    sq3 = [nc.alloc_sbuf_tensor_at("sq3_%d" % i, [128, 1024], BF16, offset=CACC0 + 4096 * i) for i in range(2)]
    assert off[0] == HY0 + 32768, off[0]
    x2h = alloc("x2h", [128, 8, 1024], F32)
    b_x2 = [Buf("x2_%d" % i) for i in range(8)]
    assert off[0] <= TMP0 + 16384 + 2048, (off[0], TMP0)
    uring = [alloc("uring%d" % i, [128, 8, 128], BF16) for i in range(NU)]
    b_ur = [Buf("ur%d" % i) for i in range(NU)]
    dring = [alloc("dring%d" % i, [128, NFC, 128], BF16) for i in range(ND)]
    b_dr = [Buf("dr%d" % i) for i in range(ND)]
    b_dr2 = [[Buf("dr%d_%d" % (i, k)) for k in range(3)] for i in range(ND)]
    gbuf = alloc("gbuf", [128, NFC, 1024], BF16)
    b_g = [Buf("g%d" % i) for i in range(NFC)]

    psum = nc.alloc_psum_tensor("psum", [128, 4096], F32)
    b_bank = [Buf("bank%d" % i) for i in range(8)]

    def bank(i, n=512, c0=0):
        return psum[:, i * 512 + c0:i * 512 + c0 + n]

    def dma_sp(out, in_, sem, reads=(), writes=()):
        return sch.op("sp", lambda h, o=out, i=in_: h.dma_start(out=o, in_=i), reads=reads, writes=writes, dma=sem)

    def dma_pool(out, in_, sem, reads=(), writes=()):
        return sch.op("pool", lambda h, o=out, i=in_: h.dma_start(out=o, in_=i), reads=reads, writes=writes, dma=sem)

    def act(out, in_, func, reads, writes, bias=None, scale=None):
        kw = {}
        if bias is not None:
            kw["bias"] = bias
        if scale is not None:
            kw["scale"] = scale
        return sch.op("act", lambda h, o=out, i=in_, f=func, kw=kw: h.activation(o, i, f, **kw), reads=reads, writes=writes)

    def tt(eng, out, a, b, op, reads, writes):
        return sch.op(eng, lambda h, o=out, a=a, b=b, op=op: h.tensor_tensor(o, a, b, op), reads=reads, writes=writes)

    def ts(eng, out, a, s1, s2, op0, op1, reads, writes):
        if op1 is None:
            return sch.op(eng, lambda h, o=out, a=a, s1=s1, op0=op0: h.tensor_scalar(o, a, s1, None, op0), reads=reads, writes=writes)
        return sch.op(eng, lambda h, o=out, a=a, s1=s1, s2=s2, op0=op0, op1=op1: h.tensor_scalar(o, a, s1, s2, op0, op1),
                      reads=reads, writes=writes)

    def stt(eng, out, a, s, b, op0, op1, reads, writes):
        return sch.op(eng, lambda h, o=out, a=a, s=s, b=b, op0=op0, op1=op1: h.scalar_tensor_tensor(o, a, s, b, op0, op1),
                      reads=reads, writes=writes)

    def mm_group(mms, reads, writes):
        def fn(h, mms=mms):
            ins = None
            for (o, l, r, st, sp_) in mms:
                ins = h.matmul(o, l, r, start=st, stop=sp_)
            return ins
        return sch.op("pe", fn, reads=reads, writes=writes)

    dd = [0]

    def dump(dst, src, rd):
        tmpf = gt[dd[0] % 2]
        bt = b_gt[dd[0] % 2]
        dd[0] += 1
        sch.op("dve", lambda h, t=tmpf, s=src: h.tensor_copy(t[:], s), reads=rd, writes=[bt])
        dma_sp(dst, tmpf[:], "dbg%d" % (dd[0] % 2), reads=[bt])

    prm_i = [0]

    def load_prm(dst, src, key):
        sem = "pr%d" % (prm_i[0] % 4)
        prm_i[0] += 1
        dma_sp(dst, src, sem, writes=[b_prm[key]])

    def _chk(name):
        if stop_at == name:
            raise _Stop()

    try:
        load_prm(p_c, cT[:, :], "c")
        load_prm(p_adab, ada_bT[:, :], "adab")
        load_prm(p_n1g, n1g[:, :], "n1g")
        load_prm(p_cw, cw[:, :], "cw")
        load_prm(p_cb, cb[:, :], "cb")
        load_prm(p_ba, rba[:, :], "ba")
        load_prm(p_bx, rbx[:, :], "bx")
        load_prm(p_lam, rlam[:, :], "lam")
        load_prm(p_n2g, n2g[:, :], "n2g")
        load_prm(p_nfg, nfg[:, :], "nfg")
        load_prm(p_fcw, fcw[:, :], "fcw")
        load_prm(p_fcb, fcb[:, :], "fcb")

        for kc in range(8):
            dma_sp(xfull[:, kc, :], xT[kc * 128:(kc + 1) * 128, :], "x%d" % (kc % 4), writes=[b_x[kc]])

        sch.op("dve", lambda h: h.memset(ones_bf[:], 1.0), writes=[b_ones])
        sch.op("dve", lambda h: h.memset(p_carry, 0.0), writes=[b_prm["carry"]])
        sch.op("dve", lambda h: h.memset(p_hz, 0.0), writes=[b_prm["hz"]])
        sch.op("dve", lambda h: h.memset(p_halo, 0.0), writes=[b_prm["halo"]] + b_halo)

        pieces = []
        for i in range(4):
            pieces.append(("ada", i, ada_v[:, :, i * 512:(i + 1) * 512]))
        for i in range(5):
            pieces.append(("win", i, win_v[:, :, i * 512:(i + 1) * 512]))
        for i in range(2):
            pieces.append(("wout", i, wout_v[:, :, i * 512:(i + 1) * 512]))
        piece_pos = {(k, i): n for n, (k, i, _) in enumerate(pieces)}
        issued = [0]

        def issue_pieces(upto):
            while issued[0] < min(upto, len(pieces)):
                n = issued[0]
                slot = n % 3
                dma_pool(wring[slot][:], pieces[n][2], "w%d" % slot, writes=[b_wr[slot]])
                issued[0] += 1

        def piece_slot(kind, idx):
            n = piece_pos[(kind, idx)]
            issue_pieces(n + 3)
            return n % 3

        issue_pieces(2)
        dma_pool(wabd[:], wa_bd[:, :], "wg", writes=[b_wabd])
        dma_pool(wxbd[:], wx_bd[:, :], "wg", writes=[b_wxbd])
        issue_pieces(3)

        act(p_tmp, p_c, AF.Tanh, [b_prm["c"]], [b_prm["tmp"]], scale=0.5)
        stt("dve", p_tmp, p_tmp, 1.0, p_c, ALU.add, ALU.mult, [b_prm["tmp"], b_prm["c"]], [b_prm["tmp"]])
        ts("dve", c_bf[:], p_tmp, 0.5, None, ALU.mult, None, [b_prm["tmp"]], [b_cbf])

        act(p_cl, p_lam, AF.Exp, [b_prm["lam"]], [b_prm["cl"]], scale=-1.0)
        act(p_cl, p_cl, AF.Ln, [b_prm["cl"]], [b_prm["cl"]], bias=1.0)
        ts("dve", p_cl2, p_cl, -8.0, None, ALU.mult, None, [b_prm["cl"]], [b_prm["cl2"]])
        ts("dve", p_cl, p_cl, -4.0, None, ALU.mult, None, [b_prm["cl"]], [b_prm["cl"]])
        ts("dve", p_ba, p_ba, 0.5, None, ALU.mult, None, [b_prm["ba"]], [b_prm["ba"]])
        ts("dve", p_bx, p_bx, 0.5, None, ALU.mult, None, [b_prm["bx"]], [b_prm["bx"]])


        MODB = 3

        def mod_piece(i, mb=MODB):
            slot = piece_slot("ada", i)
            mms = []
            for jj in range(4):
                j = i * 4 + jj
                for kc in range(8):
                    mms.append((bank(mb, 1, j), wring[slot][:, kc, jj * 128:(jj + 1) * 128], c_bf[:, kc:kc + 1], kc == 0, kc == 7))
            mm_group(mms, [b_wr[slot], b_cbf], [b_bank[mb]])
            tt("dve", p_mod[:, i * 4:(i + 1) * 4], bank(mb, 4, i * 4), p_adab[:, i * 4:(i + 1) * 4], ALU.add,
               [b_bank[mb], b_prm["adab"]], [b_prm["mod"]])

        aiss = [4]

        def issue_a(upto):
            while aiss[0] < min(upto, 12):
                i = aiss[0]
                dma_pool(aring[i % 2][:], ada_v[:, :, i * 512:(i + 1) * 512], "a%d" % (i % 2), writes=[b_ar[i % 2]])
                aiss[0] += 1

        def mod_piece2(i):
            issue_a(i + 2)
            slot = i % 2
            mms = []
            for jj in range(4):
                j = i * 4 + jj
                for kc in range(8):
                    mms.append((bank(MODB, 1, j), aring[slot][:, kc, jj * 128:(jj + 1) * 128], c_bf[:, kc:kc + 1], kc == 0, kc == 7))
            mm_group(mms, [b_ar[slot], b_cbf], [b_bank[MODB]])
            tt("dve", p_mod[:, i * 4:(i + 1) * 4], bank(MODB, 4, i * 4), p_adab[:, i * 4:(i + 1) * 4], ALU.add,
               [b_bank[MODB], b_prm["adab"]], [b_prm["mod"]])

        sh1, sc1, g1m = p_mod[:, 0:8], p_mod[:, 8:16], p_mod[:, 16:24]
        sh2, sc2, g2m = p_mod[:, 24:32], p_mod[:, 32:40], p_mod[:, 40:48]


        _chk('p0')
        sch.alias(b_sq + [b_rstd], b_et)
        for kc in range(8):
            s_ = kc % 2
            act(sqb[s_][:], xfull[:, kc, :], AF.Square, [b_x[kc]], [b_sq[s_]])
            mms = [(bank(4 + t), ones_bf[:], sqb[s_][:, t * 512:(t + 1) * 512], kc == 0, kc == 7) for t in range(4)]
            mm_group(mms, [b_sq[s_], b_ones], [b_bank[4 + t] for t in range(4)])
            if kc % 2 == 1:
                mod_piece(kc // 2)
        stt("dve", p_gm1, sc1, 1.0, p_n1g, ALU.add, ALU.mult, [b_prm["mod"], b_prm["n1g"]], [b_prm["gm1"]])
        act(rstd[:], psum[:, 2048:4096], AF.Ln, [b_bank[4 + t] for t in range(4)], [b_rstd], bias=EPS, scale=1.0 / D)
        act(rstd[:], rstd[:], AF.Exp, [b_rstd], [b_rstd], scale=-0.5)
        for hf_ in range(2):
            hs_ = slice(hf_ * 1024, (hf_ + 1) * 1024)
            for kc in range(8):
                stt("dve", xfull[:, kc, hs_], xfull[:, kc, hs_], p_gm1[:, kc:kc + 1], rstd[:, hs_], ALU.mult, ALU.mult,
                    [b_x[kc], b_prm["gm1"], b_rstd], [b_xn[kc][hf_]])
                act(hy[:, kc, hs_], xfull[:, kc, hs_], AF.Identity, [b_xn[kc][hf_], b_prm["mod"]], [b_hh[kc][hf_]], bias=sh1[:, kc:kc + 1])
        if debug:
            for kc in range(8):
                for t in range(4):
                    dump(dbg["d_h"][kc * 128:(kc + 1) * 128, t * 512:(t + 1) * 512], hy[:, kc, t * 512:(t + 1) * 512], b_hh[kc])
            dma_sp(dbg["d_mod"][:, :], p_mod, "dbg0", reads=[b_prm["mod"]])

        _chk('p1')
        sch.alias(flat(b_xr) + flat(b_gg) + flat(b_q) + flat(b_k), b_x + flat(b_xn))
        sch.alias([b_acc, b_xcb, b_A, b_C, b_D] + b_B, b_sq + [b_rstd] + b_et)
        for j in range(4):
            sch.op("dve", lambda h, j=j: h.memset(xrp[:, j, 0:8], 0.0), writes=[b_xr[j][0]])

        pb = [0]

        def next_bank():
            b = pb[0] % 3
            pb[0] += 1
            return b

        ev = [0]

        def inproj_fm(piece, mc, t):
            slot = piece_slot("win", piece)
            bk = next_bank()
            mms = [(bank(bk), wring[slot][:, kc, mc * 128:(mc + 1) * 128], hy[:, kc, t * 512:(t + 1) * 512], kc == 0, kc == 7)
                   for kc in range(8)]
            mm_group(mms, [b_wr[slot]] + [b_hh[kc][t // 2] for kc in range(8)], [b_bank[bk]])
            c0 = t * 512
            if piece == 0:
                dst = xrp[:, mc, 8 + c0:8 + c0 + 512]
                if ev[0] % 2 == 0:
                    act(dst, bank(bk), AF.Copy, [b_bank[bk]], [b_xr[mc][t]])
                else:
                    sch.op("dve", lambda h, d=dst, s=bank(bk): h.tensor_copy(d, s), reads=[b_bank[bk]], writes=[b_xr[mc][t]])
                ev[0] += 1
            elif piece == 1:
                g_ = ev[0] % 2
                ev[0] += 1
                act(gt[g_][:], bank(bk), AF.Square, [b_bank[bk]], [b_gt[g_]], scale=float(np.sqrt(0.044715)))
                stt("dve", gt[g_][:], gt[g_][:], 1.0, bank(bk), ALU.add, ALU.mult, [b_gt[g_], b_bank[bk]], [b_gt[g_]])
                act(gt[g_][:], gt[g_][:], AF.Tanh, [b_gt[g_]], [b_gt[g_]], scale=float(np.sqrt(2.0 / np.pi)))
                stt("dve", gg[:, mc, c0:c0 + 512], gt[g_][:], 1.0, bank(bk), ALU.add, ALU.mult, [b_gt[g_], b_bank[bk]], [b_gg[mc][t]])
            elif piece == 2:
                dst = qs[:, mc, c0:c0 + 512]
                if ev[0] % 4 != 3:
                    act(dst, bank(bk), AF.Copy, [b_bank[bk]], [b_q[mc][t]], scale=0.125)
                else:
                    ts("dve", dst, bank(bk), 0.125, None, ALU.mult, None, [b_bank[bk]], [b_q[mc][t]])
                ev[0] += 1
            else:
                dst = ks[:, mc, c0:c0 + 512]
                if ev[0] % 4 != 3:
                    act(dst, bank(bk), AF.Copy, [b_bank[bk]], [b_k[mc][t]])
                else:
                    sch.op("dve", lambda h, d=dst, s=bank(bk): h.tensor_copy(d, s), reads=[b_bank[bk]], writes=[b_k[mc][t]])
                ev[0] += 1

        def inproj_v(t16):
            slot = piece_slot("win", 4)
            bk = next_bank()
            mms = [(bank(bk), hy[:, kc, t16 * 128:(t16 + 1) * 128], wring[slot][:, kc, :], kc == 0, kc == 7) for kc in range(8)]
            mm_group(mms, [b_wr[slot]] + [b_hh[kc][t16 // 8] for kc in range(8)], [b_bank[bk]])
            dst = vt[:, t16, :]
            if ev[0] % 4 != 3:
                act(dst, bank(bk), AF.Copy, [b_bank[bk]], [b_v[t16]])
            else:
                sch.op("dve", lambda h, d=dst, s=bank(bk): h.tensor_copy(d, s), reads=[b_bank[bk]], writes=[b_v[t16]])
            ev[0] += 1

        for piece in (0,):
            for t in range(4):
                for mc in range(4):
                    inproj_fm(piece, mc, t)

        _chk('p2a')
        def rglru_chunk(j):
            for hf in range(2):
                c0 = hf * 1024
                act(t_acc[:], xrp[:, j, 8 + c0:8 + c0 + 1024], AF.Identity, b_xr[j] + [b_prm["cw"], b_prm["cb"]], [b_acc],
                    bias=p_cb[:, j:j + 1], scale=p_cw[:, j * 4 + 3:j * 4 + 4])
                for k in range(3):
                    src = xrp[:, j, 5 + k + c0:5 + k + c0 + 1024]
                    if k < 2:
                        stt("dve", t_acc[:], src, p_cw[:, j * 4 + k:j * 4 + k + 1], t_acc[:], ALU.mult, ALU.add,
                            b_xr[j] + [b_prm["cw"], b_acc], [b_acc])
                    else:
                        stt("dve", t_xcb[:], src, p_cw[:, j * 4 + k:j * 4 + k + 1], t_acc[:], ALU.mult, ALU.add,
                            b_xr[j] + [b_prm["cw"], b_acc], [b_xcb])
                yield
                yield
                mm_group([(bank(4 + t), wabd[:, j * 128:(j + 1) * 128], t_xcb[:, t * 512:(t + 1) * 512], True, True) for t in range(2)],
                         [b_wabd, b_xcb], [b_bank[4], b_bank[5]])
                mm_group([(bank(6 + t), wxbd[:, j * 128:(j + 1) * 128], t_xcb[:, t * 512:(t + 1) * 512], True, True) for t in range(2)],
                         [b_wxbd, b_xcb], [b_bank[6], b_bank[7]])
                act(t_A[:], psum[:, 2048:3072], AF.Tanh, [b_bank[4], b_bank[5], b_prm["ba"]], [b_A], bias=p_ba[:, j:j + 1], scale=0.5)
                act(t_C[:], psum[:, 3072:4096], AF.Tanh, [b_bank[6], b_bank[7], b_prm["bx"]], [b_C], bias=p_bx[:, j:j + 1], scale=0.5)
                act(t_B[hf][:], t_A[:], AF.Exp, [b_A, b_prm["cl2"]], [b_B[hf]], bias=p_cl2[:, j:j + 1], scale=p_cl2[:, j:j + 1])
                act(t_A[:], t_A[:], AF.Exp, [b_A, b_prm["cl"]], [b_A], bias=p_cl[:, j:j + 1], scale=p_cl[:, j:j + 1])
                yield
                stt("dve", t_C[:], t_C[:], 1.0, t_xcb[:], ALU.add, ALU.mult, [b_C, b_xcb], [b_C])
                act(t_B[hf][:], t_B[hf][:], AF.Sqrt, [b_B[hf]], [b_B[hf]], bias=1.0 / 16, scale=-1.0 / 16)
                tt("dve", t_C[:], t_C[:], t_B[hf][:], ALU.mult, [b_C, b_B[hf]], [b_C])
                sch.op("dve", lambda h, j=j: h.tensor_tensor_scan(t_D[:], t_A[:], t_C[:], p_carry[:, j:j + 1], ALU.mult, ALU.add),
                       reads=[b_A, b_C, b_prm["carry"]], writes=[b_D])
                sch.op("dve", lambda h, j=j: h.tensor_copy(p_carry[:, j:j + 1], t_D[:, 1023:1024]), reads=[b_D], writes=[b_prm["carry"]])
                yield
                tt("dve", gg[:, j, c0:c0 + 1024], gg[:, j, c0:c0 + 1024], t_D[:], ALU.mult, b_gg[j][2 * hf:2 * hf + 2] + [b_D], b_gg[j][2 * hf:2 * hf + 2])
                yield

        TR = {0: (0, 1), 1: (0, 3), 2: (0, 5), 3: (0, 7), 4: (0, 7), 5: (2, 7), 6: (4, 7), 7: (6, 7)}
        pi = [0]

        def attention():
            LAG = 3
            SBP = [0, 4]
            steps = []
            for c in range(4):
                for qt in range(4):
                    js = [4] + [j for j in (0, 1, 2, 3, 5, 6, 7) if 4 * qt - 4 + j >= 0]
                    for idx, j in enumerate(js):
                        steps.append((c, qt, idx, j, len(js)))
            npairs = len(steps)
            psv = psum.rearrange("p (b n) -> p b n", n=512)

            def geom(k):
                c, qt, idx, j, nj = steps[k]
                tlo, thi = TR[j]
                return dict(c=c, qt=qt, idx=idx, j=j, nj=nj, gkb=4 * qt - 4 + j, tlo=tlo, nco=(thi - tlo + 1) * 64,
                            q0=qt * 512 + tlo * 64, d0=(tlo - 2 * j + 8) * 64, sb=SBP[k % 2], pp=pbuf2[k % 4], bpp=b_pp[k % 4],
                            ob=(2, 3) if (c * 4 + qt) % 2 == 0 else (6, 7))

            def front(k):
                g = geom(k)
                c, nco, sb = g["c"], g["nco"], g["sb"]
                mms = []
                for e in range(2):
                    pr = slice(e * 64, (e + 1) * 64)
                    mms.append((bank(sb + e, nco), ks[pr, c, g["gkb"] * 128:(g["gkb"] + 1) * 128], qs[pr, c, g["q0"]:g["q0"] + nco], True, False))
                for e in range(2):
                    mms.append((bank(sb + e, nco), ident_bf[:], Etab[:, 2 * c + e, g["d0"]:g["d0"] + nco], False, True))
                mm_group(mms, b_k[c] + b_q[c] + [b_ident, b_E[2 * c], b_E[2 * c + 1]], [b_bank[sb], b_bank[sb + 1]])
                act(g["pp"][:, :, 0:nco], psv[:, sb:sb + 2, 0:nco], AF.Exp, [b_bank[sb], b_bank[sb + 1]], [g["bpp"]])

            def back(k):
                g = geom(k)
                c, qt, nco, tlo = g["c"], g["qt"], g["nco"], g["tlo"]
                ob, db_ = g["ob"]
                first = g["idx"] == 0
                last = g["idx"] == g["nj"] - 1
                mms = []
                for e in range(2):
                    pr = slice(e * 64, (e + 1) * 64)
                    hh = 2 * c + e
                    mms.append((psum[pr, ob * 512 + tlo * 64:ob * 512 + tlo * 64 + nco], vt[:, g["gkb"], hh * 64:(hh + 1) * 64], g["pp"][:, e, 0:nco], first, last))
                for e in range(2):
                    pr = slice(e * 64, (e + 1) * 64)
                    mms.append((psum[pr, db_ * 512 + tlo * 64:db_ * 512 + tlo * 64 + nco], ones_bf[:, 0:64], g["pp"][:, e, 0:nco], first, last))
                mm_group(mms, [b_v[g["gkb"]], g["bpp"], b_ones], [b_bank[ob], b_bank[db_]])
                if last:
                    sch.op("dve", lambda h, db_=db_: h.reciprocal(rden[:], bank(db_)), reads=[b_bank[db_]], writes=[b_rden])
                    tt("dve", hy[:, 4 + c, qt * 512:(qt + 1) * 512], bank(ob), rden[:], ALU.mult, [b_bank[ob], b_rden],
                       [b_y[4 + c][qt]] + b_hh[4 + c])

            nmod = [12]
            for k in range(npairs + LAG):
                if k < npairs:
                    front(k)
                if k >= LAG:
                    back(k - LAG)
                if k % 12 == 8 and nmod[0] < 12:
                    mod_piece(nmod[0], mb=SBP[(k + 1) % 2])
                    nmod[0] += 1
                yield
            while nmod[0] < 12:
                mod_piece(nmod[0], mb=SBP[0])
                nmod[0] += 1

        def load_bias_tables():
            dma_pool(ident_bf[:], ident[:, :], "wg", writes=[b_ident])
            dma_pool(Etab[:], tbias.rearrange("h p n -> p h n"), "wg", writes=b_E)
            for hh in range(8):
                sch.op("dve", lambda h, hh=hh: h.memset(Etab[0:64, hh, 576:640], -30000.0), writes=[b_E[hh]])
                sch.op("dve", lambda h, hh=hh: h.memset(Etab[64:128, hh, 0:64], -30000.0), writes=[b_E[hh]])

        def inproj_rest():
            cnt = [0]
            nm = [4]

            def tick():
                cnt[0] += 1
                if cnt[0] % 8 == 4 and nm[0] < 12:
                    mod_piece2(nm[0])
                    nm[0] += 1

            issue_a(6)
            for piece in (1,):
                for mc in range(4):
                    for t in range(4):
                        inproj_fm(piece, mc, t)
                        tick()
                        yield
            load_bias_tables()
            for piece in (2, 3):
                for mc in range(4):
                    for t in range(4):
                        inproj_fm(piece, mc, t)
                        tick()
                        yield
            for t16 in range(16):
                inproj_v(t16)
                tick()
                yield
            while nm[0] < 12:
                mod_piece2(nm[0])
                nm[0] += 1

        def rglru_all():
            for j in range(4):
                yield from rglru_chunk(j)

        ga, gb = rglru_all(), inproj_rest()
        NA, NB = 40, 64
        ia = ib = 0
        da = db = False
        while not (da and db):
            take_a = (not da) and (db or ia * NB <= ib * NA)
            if take_a:
                try:
                    next(ga)
                    ia += 1
                except StopIteration:
                    da = True
            else:
                try:
                    next(gb)
                    ib += 1
                except StopIteration:
                    db = True
        for _ in attention():
            pass

        _chk('p3')
        stt("dve", p_gm2, sc2, 1.0, p_n2g, ALU.add, ALU.mult, [b_prm["mod"], b_prm["n2g"]], [b_prm["gm2"]])

        if debug:
            for j in range(4):
                for t in range(4):
                    cs = slice(t * 512, (t + 1) * 512)
                    dump(dbg["d_xr"][j * 128:(j + 1) * 128, cs], xrp[:, j, 8 + t * 512:8 + (t + 1) * 512], b_xr[j])
                    dump(dbg["d_q"][j * 128:(j + 1) * 128, cs], qs[:, j, cs], b_q[j])
                    dump(dbg["d_k"][j * 128:(j + 1) * 128, cs], ks[:, j, cs], b_k[j])
            for t16 in range(16):
                dump(dbg["d_v"][t16 * 128:(t16 + 1) * 128, :], vt[:, t16, :], [b_v[t16]])
            for j in range(8):
                for t in range(4):
                    cs = slice(t * 512, (t + 1) * 512)
                    dump(dbg["d_y"][j * 128:(j + 1) * 128, cs], gg[:, j, cs] if j < 4 else hy[:, j, cs], b_gg[j] if j < 4 else b_y[j])

        _chk('p3d')
        xi = [0]

        def xload(dst, bdst, hf, kc):
            dma_sp(dst, xT[kc * 128:(kc + 1) * 128, hf * 1024:(hf + 1) * 1024], "x%d" % (xi[0] % 4), writes=[bdst])
            xi[0] += 1

        def add_delta(dst, bdst, hf, kc):
            tt("dve", dst, dst, delta1[:, kc, hf * 1024:(hf + 1) * 1024], ALU.add, [bdst] + b_dl[kc][2 * hf:2 * hf + 2], [bdst])

        def stats_sq(src, bsrc, i):
            act(sq2[i % 2][:], src, AF.Square, [bsrc], [b_sq2[i % 2]])

        def stats_mm(i, b0_, n_):
            mm_group([(bank(b0_ + t), ones_bf[:], sq2[i % 2][:, t * 512:(t + 1) * 512], i == 0, i == n_ - 1) for t in range(2)],
                     [b_sq2[i % 2], b_ones], [b_bank[b0_], b_bank[b0_ + 1]])

        def fstat_sq(m):
            act(sq3[m % 2][:], x2h[:, m, :], AF.Square, [b_x2[m]], [b_ca[m % 2], b_cah[m % 2]])

        def fstat_mm(m, b0_):
            mm_group([(bank(b0_ + t), ones_bf[:], sq3[m % 2][:, t * 512:(t + 1) * 512], m == 0, m == 7) for t in range(2)],
                     [b_ca[m % 2], b_cah[m % 2], b_ones], [b_bank[b0_], b_bank[b0_ + 1]])

        def make_rstd(b0_):
            act(rstd2[:], psum[:, b0_ * 512:b0_ * 512 + 1024], AF.Ln, [b_bank[b0_], b_bank[b0_ + 1]], [b_rstd2], bias=EPS, scale=1.0 / D)
            act(rstd2[:], rstd2[:], AF.Exp, [b_rstd2], [b_rstd2], scale=-0.5)

        sch.alias(b_x2, b_v + [b_acc, b_xcb, b_A, b_C, b_D] + b_B)
        sch.alias(b_sq2, [b for row in b_hh[0:4] for b in row])
        for kc in range(8):
            xload(x2h[:, kc, :], b_x2[kc], 0, kc)

        sch.alias(flat(b_dl), flat(b_q) + flat(b_k))
        y_all = flat(b_gg) + [b for row in b_y[4:] for b in row]

        def ysrc(kc, cs):
            return gg[:, kc, cs] if kc < 4 else hy[:, kc, cs]
        for m in range(8):
            slot = piece_slot("wout", m // 4)
            mo = (m % 4) * 128
            for t in range(4):
                bk = next_bank()
                mms = [(bank(bk), wring[slot][:, kc, mo:mo + 128], ysrc(kc, slice(t * 512, (t + 1) * 512)), kc == 0, kc == 7) for kc in range(8)]
                mm_group(mms, [b_wr[slot]] + [b_gg[kc][t] for kc in range(4)] + [b_y[kc][t] for kc in range(4, 8)], [b_bank[bk]])
                dst = delta1[:, m, t * 512:(t + 1) * 512]
                if ev[0] % 2 == 0:
                    act(dst, bank(bk), AF.Copy, [b_bank[bk], b_prm["mod"]], [b_dl[m][t]], scale=g1m[:, m:m + 1])
                else:
                    ts("dve", dst, bank(bk), g1m[:, m:m + 1], None, ALU.mult, None, [b_bank[bk], b_prm["mod"]], [b_dl[m][t]])
                ev[0] += 1
            add_delta(x2h[:, m, :], b_x2[m], 0, m)
            stats_sq(x2h[:, m, :], b_x2[m], m)
            if m >= 1:
                stats_mm(m - 1, 6, 8)
        stats_mm(7, 6, 8)
        if debug:
            for j in range(8):
                for t in range(4):
                    cs = slice(t * 512, (t + 1) * 512)
                    dump(dbg["d_dl"][j * 128:(j + 1) * 128, cs], delta1[:, j, cs], b_dl[j])

        _chk('p4')
        sch.barrier()
        uiss = [0]
        upieces = []
        for hf in range(2):
            for f in range(NFC):
                for part in range(2):
                    upieces.append(wup_v[:, :, part * DFF + f * 128:part * DFF + (f + 1) * 128])

        def issue_u(upto):
            while uiss[0] < min(upto, len(upieces)):
                n = uiss[0]
                dma_pool(uring[n % NU][:], upieces[n], "u%d" % (n % NU), writes=[b_ur[n % NU]])
                uiss[0] += 1

        diss = [0]
        dpieces = [wdn_v[:, :, m * 128:(m + 1) * 128] for hf in range(2) for m in range(8)]

        def issue_d(upto):
            while diss[0] < min(upto, len(dpieces)):
                n = diss[0]
                for hh_, (ka, kb_) in enumerate(((0, 8), (8, 16), (16, 22))):
                    dma_pool(dring[n % ND][:, ka:kb_, :], dpieces[n][:, ka:kb_, :],
                             "d%d_%d" % (n % ND, hh_), writes=[b_dr2[n % ND][hh_]])
                diss[0] += 1

        issue_u(NU)
        pend = [None]
        ub = [0]
        oi = [0]
        def side_runner(units):
            it = iter(units)

            def run(n_):
                for _ in range(n_):
                    u = next(it, None)
                    if u is None:
                        return
                    u()
            return run

        make_rstd(6)
        for kc in range(8):
            s_ = kc % 2
            stt("dve", ntmp[s_][:], x2h[:, kc, :], p_gm2[:, kc:kc + 1], rstd2[:], ALU.mult, ALU.mult,
                [b_x2[kc], b_prm["gm2"], b_rstd2], [b_nt[s_]])
            act(h2b[0][:, kc, :], ntmp[s_][:], AF.Identity, [b_nt[s_], b_prm["mod"]], [b_h2[0][kc]], bias=sh2[:, kc:kc + 1])
        _chk('f_norm0')
        issue_d(ND - 1)

        def n2_units():
            us_ = []
            us_.append(lambda: xload(ntmp[0][:], b_nt[0], 1, 0))
            for kc in range(8):
                def p1(kc=kc):
                    if kc + 1 < 8:
                        xload(ntmp[(kc + 1) % 2][:], b_nt[(kc + 1) % 2], 1, kc + 1)
                    add_delta(ntmp[kc % 2][:], b_nt[kc % 2], 1, kc)
                    stats_sq(ntmp[kc % 2][:], b_nt[kc % 2], kc)
                us_.append(p1)
                if kc >= 1:
                    us_.append(lambda kc=kc: stats_mm(kc - 1, 4, 8))
            us_.append(lambda: stats_mm(7, 4, 8))
            us_.append(lambda: xload(ntmp[0][:], b_nt[0], 1, 0))
            us_.append(lambda: make_rstd(4))
            for kc in range(8):
                def p2(kc=kc):
                    if kc + 1 < 8:
                        xload(ntmp[(kc + 1) % 2][:], b_nt[(kc + 1) % 2], 1, kc + 1)
                    s_ = kc % 2
                    add_delta(ntmp[s_][:], b_nt[s_], 1, kc)
                    stt("dve", ntmp[s_][:], ntmp[s_][:], p_gm2[:, kc:kc + 1], rstd2[:], ALU.mult, ALU.mult,
                        [b_nt[s_], b_prm["gm2"], b_rstd2], [b_nt[s_]])
                    act(h2b[1][:, kc, :], ntmp[s_][:], AF.Identity, [b_nt[s_], b_prm["mod"]], [b_h2[1][kc]], bias=sh2[:, kc:kc + 1])
                us_.append(p2)
            return us_

        def fin_units(hf, preload_next):
            us_ = []
            if preload_next:
                for m in range(8):
                    us_.append(lambda m=m: stats_sq(x2h[:, m, :], b_x2[m], m))
                    if m >= 1:
                        us_.append(lambda m=m: stats_mm(m - 1, 6, 8))
                us_.append(lambda: stats_mm(7, 6, 8))
                us_.append(lambda: make_rstd(6))
            else:
                us_.append(lambda: make_rstd(4))
            for m in range(8):
                def st(m=m):
                    if preload_next or m % 3 == 2:
                        act(x2h[:, m, :], x2h[:, m, :], AF.Copy, [b_x2[m], b_prm["nfg"]], [b_x2[m]], scale=p_nfg[:, m:m + 1])
                        tt("pool", x2h[:, m, :], x2h[:, m, :], rstd2[:], ALU.mult, [b_x2[m], b_rstd2], [b_x2[m]])
                    else:
                        stt("dve", x2h[:, m, :], x2h[:, m, :], p_nfg[:, m:m + 1], rstd2[:], ALU.mult, ALU.mult,
                            [b_x2[m], b_prm["nfg"], b_rstd2], [b_x2[m]])
                    dma_sp(outT[m * 128:(m + 1) * 128, hf * 1024:(hf + 1) * 1024], x2h[:, m, :], "o%d" % (oi[0] % 4), reads=[b_x2[m]])
                    oi[0] += 1
                us_.append(st)
            if preload_next:
                for kc in range(8):
                    us_.append(lambda kc=kc: xload(x2h[:, kc, :], b_x2[kc], hf + 1, kc))
                for kc in range(8):
                    us_.append(lambda kc=kc: tt("pool", x2h[:, kc, :], x2h[:, kc, :], delta1[:, kc, (hf + 1) * 1024:(hf + 2) * 1024], ALU.add,
                                                [b_x2[kc]] + b_dl[kc], [b_x2[kc]]))
            return us_

        side = side_runner([])
        for hf in range(2):
            h2h = h2b[hf]
            bh2 = b_h2[hf]
            for f in range(NFC):
                for part in range(2):
                    n = (hf * NFC + f) * 2 + part
                    issue_u(n + NU)
                    us = n % NU
                    ci = part * NFC + f
                    b0_ = (ub[0] % 3) * 2
                    ub[0] += 1
                    mms = []
                    for t in range(2):
                        for kc in range(8):
                            mms.append((bank(b0_ + t), uring[us][:, kc, :], h2h[:, kc, t * 512:(t + 1) * 512], kc == 0, kc == 7))
                    mm_group(mms, [b_ur[us]] + bh2, [b_bank[b0_], b_bank[b0_ + 1]])
                    pu = psum[:, b0_ * 512:b0_ * 512 + 1024]
                    bks = [b_bank[b0_], b_bank[b0_ + 1]]
                    ca = cacc[part * 2 + (f % 2)]
                    bca = b_ca[part * 2 + (f % 2)]
                    w0 = p_fcw[:, ci * 3 + 0:ci * 3 + 1]
                    w1 = p_fcw[:, ci * 3 + 1:ci * 3 + 2]
                    w2 = p_fcw[:, ci * 3 + 2:ci * 3 + 3]
                    hzs = (n % 2) * 4
                    hz = p_hz[:, hzs:hzs + 4]
                    bhz = b_hz[n % 2]
                    bcah = b_cah[part * 2 + (f % 2)]
                    hrd = (1 - hf) * 88 + 2 * ci
                    hwr = hf * 88 + 2 * ci
                    sch.op("dve", lambda h, hrd=hrd, hz=hz: h.tensor_copy(hz[:, 0:2], p_halo[:, hrd:hrd + 2]),
                           reads=[b_halo[1 - hf]], writes=[bhz[0]])
                    act(hz[:, 2:4], pu[:, 0:2], AF.Copy, bks, [bhz[1]])
                    act(p_halo[:, hwr:hwr + 2], pu[:, 1022:1024], AF.Copy, bks, [b_halo[hf]])
                    act(ca[:], pu, AF.Identity, bks + [b_prm["fcw"], b_prm["fcb"]], [bca, bcah], bias=p_fcb[:, ci:ci + 1], scale=w2)
                    stt("dve", ca[:, 2:1024], pu[:, 1:1023], w1, ca[:, 2:1024], ALU.mult, ALU.add, bks + [bca, b_prm["fcw"]], [bca])
                    stt("dve", ca[:, 2:1024], pu[:, 0:1022], w0, ca[:, 2:1024], ALU.mult, ALU.add, bks + [bca, b_prm["fcw"]], [bca])
                    stt("dve", ca[:, 0:2], hz[:, 1:3], w1, ca[:, 0:2], ALU.mult, ALU.add, [bhz[0], bhz[1], bcah, b_prm["fcw"]], [bcah])
                    stt("dve", ca[:, 0:2], hz[:, 0:2], w0, ca[:, 0:2], ALU.mult, ALU.add, [bhz[0], bhz[1], bcah, b_prm["fcw"]], [bcah])
                    if pend[0] is not None:
                        pend[0]()
                    if part == 0:
                        pend[0] = (lambda ca=ca, bca=bca, bcah=bcah: act(ca[:], ca[:], AF.Silu, [bca, bcah], [bca, bcah]))
                    else:
                        pend[0] = (lambda f=f, ca=ca, bca=bca, bcah=bcah: tt("pool", gbuf[:, f, :], cacc[f % 2][:], ca[:], ALU.mult,
                                                                         [b_ca[f % 2], b_cah[f % 2], bca, bcah], [b_g[f]]))
                side(2)
            if pend[0] is not None:
                pend[0]()
                pend[0] = None
            side(1000)
            _chk('f_up%d' % hf)
            side = side_runner(n2_units() if hf == 0 else [])
            fsb = 2 if hf == 0 else 4
            for m in range(8):
                n = hf * 8 + m
                issue_d(n + ND)
                ds_ = n % ND
                if hf == 1:
                    add_delta(x2h[:, m, :], b_x2[m], 1, m)
                for t in range(2):
                    bk = 6 + t
                    mms = [(bank(bk), dring[ds_][:, kc, :], gbuf[:, kc, t * 512:(t + 1) * 512], kc == 0, kc == NFC - 1) for kc in range(NFC)]
                    if m == 0 and t == 0:
                        mm_group(mms[:16], b_dr2[ds_] + b_g[:16], [b_bank[bk]])
                        mm_group(mms[16:20], b_dr2[ds_] + b_g[16:20], [b_bank[bk]])
                        mm_group(mms[20:], b_dr2[ds_] + b_g[20:], [b_bank[bk]])
                    else:
                        mm_group(mms, b_dr2[ds_] + b_g, [b_bank[bk]])
                    stt("dve", x2h[:, m, t * 512:(t + 1) * 512], bank(bk), g2m[:, m:m + 1], x2h[:, m, t * 512:(t + 1) * 512],
                        ALU.mult, ALU.add, [b_bank[bk], b_prm["mod"], b_x2[m]], [b_x2[m]])
                    side(2)
                fstat_sq(m)
                if m >= 1:
                    fstat_mm(m - 1, fsb)
            fstat_mm(7, fsb)
            side(1000)
            _chk('f_down%d' % hf)
            make_rstd(fsb)
            for m in range(8):
                if hf == 1 and m % 3 == 2:
                    act(x2h[:, m, :], x2h[:, m, :], AF.Copy, [b_x2[m], b_prm["nfg"]], [b_x2[m]], scale=p_nfg[:, m:m + 1])
                    tt("pool", x2h[:, m, :], x2h[:, m, :], rstd2[:], ALU.mult, [b_x2[m], b_rstd2], [b_x2[m]])
                else:
                    stt("dve", x2h[:, m, :], x2h[:, m, :], p_nfg[:, m:m + 1], rstd2[:], ALU.mult, ALU.mult,
                        [b_x2[m], b_prm["nfg"], b_rstd2], [b_x2[m]])
                dma_sp(outT[m * 128:(m + 1) * 128, hf * 1024:(hf + 1) * 1024], x2h[:, m, :], "o%d" % (oi[0] % 4), reads=[b_x2[m]])
                oi[0] += 1
            if hf == 0:
                for kc in range(8):
                    xload(x2h[:, kc, :], b_x2[kc], 1, kc)
            side = side_runner([])

    except _Stop:
        pass

    fin = [(s, c) for s, c in sch.cnt.items() if s.startswith("o") or s.startswith("dbg")]
    sch.wait_only("sp", fin)

    sem_names = sorted(sch.cnt.keys())
    sems = {}
    import contextlib
    with contextlib.ExitStack() as es:
        for nme in sem_names:
            sems[nme] = es.enter_context(nc.semaphore("s_" + nme))
        block = es.enter_context(nc.Block())
        handles = {"pe": block.tensor, "act": block.scalar, "dve": block.vector, "pool": block.gpsimd, "sp": block.sync}
        for e in Sched.ENGS:
            ops = sch.ops[e]

            def body(h, ops=ops, e=e):
                for waits, fn, dma in ops:
                    for s_, c_ in waits:
                        h.wait_ge(sems[s_], c_)
                    if fn is None:
                        continue
                    ins = fn(h)
                    if dma is not None:
                        ins.then_inc(sems[dma], 16)
                    else:
                        ins.then_inc(sems[e], 1)
            handles[e](body)
    return nc


def _fm(v):
    v = np.asarray(v, np.float32)
    return np.ascontiguousarray(v.reshape(-1, 128).T)


def prep_inputs(b, x, c, ada_w, ada_b, norm1_g, w_in, rnn_conv_w, rnn_conv_b, rg_wa, rg_ba, rg_wx, rg_bx,
                rg_lambda, rel_bias, w_out, norm2_g, w_up, ffn_conv_w, ffn_conv_b, w_down, final_g, shared):
    m = dict(shared)
    m["xT"] = np.ascontiguousarray(np.asarray(x[b], np.float32).T)
    m["cT"] = _fm(c[b])
    return m


def prep_shared(ada_w, ada_b, norm1_g, w_in, rnn_conv_w, rnn_conv_b, rg_wa, rg_ba, rg_wx, rg_bx,
                rg_lambda, rel_bias, w_out, norm2_g, w_up, ffn_conv_w, ffn_conv_b, w_down, final_g):
    f = np.float32
    sh = {}
    sh["ada_w"] = np.ascontiguousarray(np.asarray(ada_w[0], f))
    sh["ada_bT"] = _fm(ada_b[0])
    sh["n1g"] = _fm(norm1_g[0])
    sh["n2g"] = _fm(norm2_g[0])
    sh["nfg"] = _fm(final_g)
    sh["w_in"] = np.ascontiguousarray(np.asarray(w_in[0], f))
    cwv = np.asarray(rnn_conv_w[0], f)
    sh["cw"] = np.ascontiguousarray(cwv.reshape(4, 4, 128).transpose(2, 1, 0).reshape(128, 16))
    sh["cb"] = _fm(rnn_conv_b[0])

    def bd(w):
        w = np.asarray(w, f)
        o = np.zeros((128, 4, 128), f)
        for j in range(4):
            o[0:64, j, 0:64] = w[2 * j]
            o[64:128, j, 64:128] = w[2 * j + 1]
        return np.ascontiguousarray(o.reshape(128, 512))
    sh["wa_bd"] = bd(rg_wa[0])
    sh["wx_bd"] = bd(rg_wx[0])
    sh["rba"] = _fm(rg_ba[0])
    sh["rbx"] = _fm(rg_bx[0])
    sh["rlam"] = _fm(rg_lambda[0])
    kk = np.arange(128)[:, None]
    col = np.arange(640)[None, :]
    idx = np.clip(col - kk, -128, 128) + 128
    sh["tbias"] = np.ascontiguousarray(np.asarray(rel_bias[0], f)[:, idx])
    sh["ident"] = np.eye(128, dtype=f)
    sh["w_out"] = np.ascontiguousarray(np.asarray(w_out[0], f))
    sh["w_up"] = np.ascontiguousarray(np.asarray(w_up[0], f))
    fw = np.asarray(ffn_conv_w[0], f)
    sh["fcw"] = np.ascontiguousarray(fw.reshape(3, 44, 128).transpose(2, 1, 0).reshape(128, 132))
    sh["fcb"] = _fm(ffn_conv_b[0])
    sh["w_down"] = np.ascontiguousarray(np.asarray(w_down[0], f))
    return sh


_NC_CACHE = {}


def kernel(x, c, ada_w, ada_b, norm1_g, w_in, rnn_conv_w, rnn_conv_b, rg_wa, rg_ba, rg_wx, rg_bx,
           rg_lambda, rel_bias, w_out, norm2_g, w_up, ffn_conv_w, ffn_conv_b, w_down, final_g, _debug=False, _stop=None):
    x = np.asarray(x)
    c = np.asarray(c)
    shared = prep_shared(ada_w, ada_b, norm1_g, w_in, rnn_conv_w, rnn_conv_b, rg_wa, rg_ba, rg_wx, rg_bx,
                         rg_lambda, rel_bias, w_out, norm2_g, w_up, ffn_conv_w, ffn_conv_b, w_down, final_g)
    in_maps = []
    for b in range(NCORE):
        m = dict(shared)
        m["xT"] = np.ascontiguousarray(np.asarray(x[b], np.float32).T)
        m["cT"] = _fm(c[b])
        in_maps.append(m)
    nc = build_program(debug=_debug, stop_at=_stop)
    res = run_bass_kernel_spmd(nc, in_maps, core_ids=list(range(NCORE)))
    out = np.stack([np.ascontiguousarray(res.results[b]["outT"].T) for b in range(NCORE)], axis=0).astype(np.float32)
    if _debug:
        return out, res.results
    return out
```

```python
import numpy as np
import concourse.bass as bass
import concourse.mybir as mybir
from concourse.bass_utils import run_bass_kernel_spmd

F32 = mybir.dt.float32
BF16 = mybir.dt.bfloat16
ALU = mybir.AluOpType
AF = mybir.ActivationFunctionType

S = 2048
D = 1024
NCORE = 8
DFF = 2816
NFC = DFF // 128
EPS = 1e-6
ATT_WARM = 3
NU = 5
ND = 2
B0 = 16512
SB_END = 229344


class Buf:
    __slots__ = ("name", "w", "r")

    def __init__(self, name):
        self.name = name
        self.w = None
        self.r = []


class Sched:
    ENGS = ["pe", "act", "dve", "pool", "sp"]

    def __init__(self):
        self.ops = {e: [] for e in self.ENGS}
        self.cnt = {}
        self.seen = {e: {} for e in self.ENGS}

    def _need(self, eng, waits, tok, war=False):
        if tok is None:
            return
        s, c = tok
        if s == eng and eng == "pe":
            return
        if self.seen[eng].get(s, 0) >= c:
            return
        if waits.get(s, 0) < c:
            waits[s] = c

    def op(self, eng, fn, reads=(), writes=(), dma=None):
        waits = {}
        for b in reads:
            self._need(eng, waits, b.w)
        for b in writes:
            self._need(eng, waits, b.w)
            for t in b.r:
                self._need(eng, waits, t, war=True)
        if dma is not None:
            c = self.cnt.get(dma, 0)
            if c > 0:
                self._need(eng, waits, (dma, c))
            self.cnt[dma] = c + 16
            tok = (dma, c + 16)
        else:
            self.cnt[eng] = self.cnt.get(eng, 0) + 1
            tok = (eng, self.cnt[eng])
        for s, c in waits.items():
            self.seen[eng][s] = c
        self.ops[eng].append((sorted(waits.items()), fn, dma))
        for b in reads:
            b.r.append(tok)
        for b in writes:
            b.w = tok
            b.r = []
        return tok

    def wait_only(self, eng, toks):
        waits = {}
        for t in toks:
            self._need(eng, waits, t)
        for s, c in waits.items():
            self.seen[eng][s] = c
        self.ops[eng].append((sorted(waits.items()), None, None))

    def barrier(self):
        toks = [(s, c) for s, c in self.cnt.items()]
        for e in self.ENGS:
            self.wait_only(e, toks)

    def alias(self, new_bufs, old_bufs):
        toks = []
        for b in old_bufs:
            if b.w is not None:
                toks.append(b.w)
            toks.extend(b.r)
        for nb in new_bufs:
            nb.w = None
            nb.r = list(toks)


class _Stop(Exception):
    pass


def build_program(debug=False, stop_at=None):
    nc = bass.Bass("TRN2", target_bir_lowering=False)
    sch = Sched()

    def din(name, shape):
        return nc.dram_tensor(name, list(shape), F32, kind="ExternalInput").ap()

    xT = din("xT", [D, S])
    cT = din("cT", [128, 8])
    ada_w = din("ada_w", [D, 6 * D])
    ada_bT = din("ada_bT", [128, 48])
    n1g = din("n1g", [128, 8])
    n2g = din("n2g", [128, 8])
    nfg = din("nfg", [128, 8])
    w_in = din("w_in", [D, 2560])
    cw = din("cw", [128, 16])
    cb = din("cb", [128, 4])
    wa_bd = din("wa_bd", [128, 512])
    wx_bd = din("wx_bd", [128, 512])
    rba = din("rba", [128, 4])
    rbx = din("rbx", [128, 4])
    rlam = din("rlam", [128, 4])
    tbias = din("tbias", [8, 128, 640])
    ident = din("ident", [128, 128])
    w_out = din("w_out", [D, D])
    w_up = din("w_up", [D, 2 * DFF])
    fcw = din("fcw", [128, 44 * 3])
    fcb = din("fcb", [128, 44])
    w_down = din("w_down", [DFF, D])
    outT = nc.dram_tensor("outT", [D, S], F32, kind="ExternalOutput").ap()
    dbg = {}
    if debug:
        for nm, shp in [("d_h", [D, S]), ("d_xr", [512, S]), ("d_gg", [512, S]), ("d_q", [512, S]),
                        ("d_k", [512, S]), ("d_v", [S, 512]), ("d_y", [D, S]), ("d_dl", [D, S]),
                        ("d_mod", [128, 48])]:
            dbg[nm] = nc.dram_tensor(nm, shp, F32, kind="ExternalOutput").ap()

    ada_v = ada_w.rearrange("(kc p) n -> p kc n", p=128)
    win_v = w_in.rearrange("(kc p) n -> p kc n", p=128)
    wout_v = w_out.rearrange("(kc p) n -> p kc n", p=128)
    wup_v = w_up.rearrange("(kc p) n -> p kc n", p=128)
    wdn_v = w_down.rearrange("(kc p) n -> p kc n", p=128)

    off = [B0]

    def alloc(name, shape, dt, at=None):
        nbytes = int(np.prod(shape[1:])) * (4 if dt == F32 else 2)
        if at is None:
            at = off[0]
            off[0] += (nbytes + 31) // 32 * 32
            assert off[0] <= SB_END, (name, off[0])
        return nc.alloc_sbuf_tensor_at(name, list(shape), dt, offset=at)

    prm = alloc("prm", [128, 576], F32)
    PC = {}
    pc = [0]

    def pslot(name, n):
        PC[name] = (pc[0], n)
        pc[0] += n
        assert pc[0] <= 576
        return prm[:, PC[name][0]:PC[name][0] + n]

    p_c = pslot("c", 8)
    p_adab = pslot("adab", 48)
    p_mod = pslot("mod", 48)
    p_n1g = pslot("n1g", 8)
    p_n2g = pslot("n2g", 8)
    p_nfg = pslot("nfg", 8)
    p_gm1 = pslot("gm1", 8)
    p_gm2 = pslot("gm2", 8)
    p_cw = pslot("cw", 16)
    p_cb = pslot("cb", 4)
    p_ba = pslot("ba", 4)
    p_bx = pslot("bx", 4)
    p_lam = pslot("lam", 4)
    p_cl = pslot("cl", 4)
    p_cl2 = pslot("cl2", 4)
    p_tmp = pslot("tmp", 8)
    p_fcw = pslot("fcw", 132)
    p_fcb = pslot("fcb", 44)
    p_carry = pslot("carry", 4)
    p_hz = pslot("hz", 8)
    p_hzt = pslot("hzt", 8)
    p_halo = pslot("halo", 176)
    b_prm = {k: Buf("prm_" + k) for k in PC}
    c_bf = alloc("c_bf", [128, 8], BF16)
    b_cbf = Buf("c_bf")
    ones_bf = alloc("ones_bf", [128, 128], BF16)
    b_ones = Buf("ones")
    ident_bf = alloc("ident_bf", [128, 128], BF16)
    b_ident = Buf("ident")
    wabd = alloc("wabd", [128, 512], BF16)
    wxbd = alloc("wxbd", [128, 512], BF16)
    b_wabd, b_wxbd = Buf("wabd"), Buf("wxbd")
    Etab = alloc("Etab", [128, 8, 640], BF16)
    b_E = [Buf("E%d" % h) for h in range(8)]

    STAGE0 = off[0]
    XR0 = off[0]
    xfull = alloc("xfull", [128, 8, 2048], F32)
    b_x = [Buf("x%d" % i) for i in range(8)]
    xrp = alloc("xrp", [128, 4, 2056], BF16, at=XR0)
    gg = alloc("gg", [128, 4, 2048], BF16, at=XR0 + 16448)
    qs = alloc("qs", [128, 4, 2048], BF16, at=XR0 + 16448 + 16384)
    ks = alloc("ks", [128, 4, 2048], BF16, at=XR0 + 16448 + 32768)
    off[0] = XR0 + 16448 + 49152
    DL0 = XR0 + 16448 + 16384
    delta1 = alloc("delta1", [128, 8, 2048], BF16, at=DL0)
    b_xr = [[Buf("xr%d_%d" % (i, t)) for t in range(4)] for i in range(4)]
    b_gg = [[Buf("gg%d_%d" % (i, t)) for t in range(4)] for i in range(4)]
    b_q = [[Buf("q%d_%d" % (i, t)) for t in range(4)] for i in range(4)]
    b_k = [[Buf("k%d_%d" % (i, t)) for t in range(4)] for i in range(4)]

    def flat(ll):
        return [b for row in ll for b in row]
    b_dl = [[Buf("dl%d_%d" % (i, t)) for t in range(4)] for i in range(8)]
    hy = alloc("hy", [128, 8, 2048], BF16)
    b_hh = [[Buf("h%d_%d" % (i, k)) for k in range(2)] for i in range(8)]
    b_h = [b for row in b_hh for b in row]
    b_xn = [[Buf("xn%d_%d" % (i, k)) for k in range(2)] for i in range(8)]
    b_y = [[Buf("y%d_%d" % (i, t)) for t in range(4)] for i in range(8)]
    vt = alloc("vt", [128, 16, 512], BF16)
    b_v = [Buf("v%d" % i) for i in range(16)]
    TMP0 = off[0]
    t_acc = alloc("t_acc", [128, 1024], F32)
    t_xcb = alloc("t_xcb", [128, 1024], BF16)
    t_A = alloc("t_A", [128, 1024], F32)
    t_B = [alloc("t_B%d" % i, [128, 1024], F32) for i in range(2)]
    t_C = alloc("t_C", [128, 1024], F32)
    t_D = alloc("t_D", [128, 1024], F32)
    b_acc, b_xcb, b_A, b_C, b_D = Buf("acc"), Buf("xcb"), Buf("A"), Buf("C"), Buf("D")
    b_B = [Buf("B0"), Buf("B1")]
    rstd = alloc("rstd", [128, 2048], F32, at=TMP0)
    sqb = [alloc("sqb%d" % i, [128, 2048], BF16, at=TMP0 + 8192 + 4096 * i) for i in range(2)]
    b_rstd = Buf("rstd")
    b_sq = [Buf("sq0"), Buf("sq1")]
    wring = [alloc("wring%d" % i, [128, 8, 512], BF16) for i in range(3)]
    b_wr = [Buf("wr%d" % i) for i in range(3)]
    pbuf2 = [alloc("pbuf%d" % i, [128, 2, 512], BF16) for i in range(4)]
    b_pp = [Buf("pp%d" % i) for i in range(4)]
    etmp = [nc.alloc_sbuf_tensor_at("etmp%d" % i, [128, 640], F32, offset=TMP0 + 2560 * i) for i in range(2)]
    b_et = [Buf("et0"), Buf("et1")]
    gt = [alloc("gt%d" % i, [128, 512], F32) for i in range(2)]
    b_gt = [Buf("gt0"), Buf("gt1")]
    rden = alloc("rden", [128, 512], F32)
    b_rden = Buf("rden")
    aring = [alloc("aring%d" % i, [128, 8, 512], BF16) for i in range(2)]
    b_ar = [Buf("ar0"), Buf("ar1")]
    STAGEA_END = off[0]

    off[0] = STAGE0
    h2b = [alloc("h2h%d" % i, [128, 8, 1024], BF16) for i in range(2)]
    b_h2 = [[Buf("h2_%d_%d" % (k, i)) for i in range(8)] for k in range(2)]
    b_ca = [Buf("ca%d" % i) for i in range(4)]
    b_cah = [Buf("cah%d" % i) for i in range(4)]
    b_halo = [Buf("halo0"), Buf("halo1")]
    b_hz = [[Buf("hz%d_%d" % (i, k)) for k in range(3)] for i in range(2)]
    assert off[0] <= DL0
    HY0 = DL0 + 32768
    off[0] = HY0
    sq2 = [alloc("sq2_%d" % i, [128, 1024], BF16) for i in range(2)]
    b_sq2 = [Buf("sq2_0"), Buf("sq2_1")]
    rstd2 = alloc("rstd2", [128, 1024], F32)
    b_rstd2 = Buf("rstd2")
    ntmp = [alloc("ntmp%d" % i, [128, 1024], F32) for i in range(2)]
    b_nt = [Buf("nt0"), Buf("nt1")]
    assert off[0] == HY0 + 16384, off[0]
    CACC0 = off[0]
    cacc = [alloc("cacc%d" % i, [128, 1024], F32) for i in range(4)]
    sq3 = [nc.alloc_sbuf_tensor_at("sq3_%d" % i, [128, 1024], BF16, offset=CACC0 + 4096 * i) for i in range(2)]
    assert off[0] == HY0 + 32768, off[0]
    x2h = alloc("x2h", [128, 8, 1024], F32)
    b_x2 = [Buf("x2_%d" % i) for i in range(8)]
    assert off[0] <= TMP0 + 16384 + 2048, (off[0], TMP0)
    uring = [alloc("uring%d" % i, [128, 8, 128], BF16) for i in range(NU)]
    b_ur = [Buf("ur%d" % i) for i in range(NU)]
    dring = [alloc("dring%d" % i, [128, NFC, 128], BF16) for i in range(ND)]
    b_dr = [Buf("dr%d" % i) for i in range(ND)]
    b_dr2 = [[Buf("dr%d_%d" % (i, k)) for k in range(3)] for i in range(ND)]
    gbuf = alloc("gbuf", [128, NFC, 1024], BF16)
    b_g = [Buf("g%d" % i) for i in range(NFC)]

    psum = nc.alloc_psum_tensor("psum", [128, 4096], F32)
    b_bank = [Buf("bank%d" % i) for i in range(8)]

    def bank(i, n=512, c0=0):
        return psum[:, i * 512 + c0:i * 512 + c0 + n]

    def dma_sp(out, in_, sem, reads=(), writes=()):
        return sch.op("sp", lambda h, o=out, i=in_: h.dma_start(out=o, in_=i), reads=reads, writes=writes, dma=sem)

    def dma_pool(out, in_, sem, reads=(), writes=()):
        return sch.op("pool", lambda h, o=out, i=in_: h.dma_start(out=o, in_=i), reads=reads, writes=writes, dma=sem)

    def act(out, in_, func, reads, writes, bias=None, scale=None):
        kw = {}
        if bias is not None:
            kw["bias"] = bias
        if scale is not None:
            kw["scale"] = scale
        return sch.op("act", lambda h, o=out, i=in_, f=func, kw=kw: h.activation(o, i, f, **kw), reads=reads, writes=writes)

    def tt(eng, out, a, b, op, reads, writes):
        return sch.op(eng, lambda h, o=out, a=a, b=b, op=op: h.tensor_tensor(o, a, b, op), reads=reads, writes=writes)

    def ts(eng, out, a, s1, s2, op0, op1, reads, writes):
        if op1 is None:
            return sch.op(eng, lambda h, o=out, a=a, s1=s1, op0=op0: h.tensor_scalar(o, a, s1, None, op0), reads=reads, writes=writes)
        return sch.op(eng, lambda h, o=out, a=a, s1=s1, s2=s2, op0=op0, op1=op1: h.tensor_scalar(o, a, s1, s2, op0, op1),
                      reads=reads, writes=writes)

    def stt(eng, out, a, s, b, op0, op1, reads, writes):
        return sch.op(eng, lambda h, o=out, a=a, s=s, b=b, op0=op0, op1=op1: h.scalar_tensor_tensor(o, a, s, b, op0, op1),
                      reads=reads, writes=writes)

    def mm_group(mms, reads, writes):
        def fn(h, mms=mms):
            ins = None
            for (o, l, r, st, sp_) in mms:
                ins = h.matmul(o, l, r, start=st, stop=sp_)
            return ins
        return sch.op("pe", fn, reads=reads, writes=writes)

    dd = [0]

    def dump(dst, src, rd):
        tmpf = gt[dd[0] % 2]
        bt = b_gt[dd[0] % 2]
        dd[0] += 1
        sch.op("dve", lambda h, t=tmpf, s=src: h.tensor_copy(t[:], s), reads=rd, writes=[bt])
        dma_sp(dst, tmpf[:], "dbg%d" % (dd[0] % 2), reads=[bt])

    prm_i = [0]

    def load_prm(dst, src, key):
        sem = "pr%d" % (prm_i[0] % 4)
        prm_i[0] += 1
        dma_sp(dst, src, sem, writes=[b_prm[key]])

    def _chk(name):
        if stop_at == name:
            raise _Stop()

    try:
        load_prm(p_c, cT[:, :], "c")
        load_prm(p_adab, ada_bT[:, :], "adab")
        load_prm(p_n1g, n1g[:, :], "n1g")
        load_prm(p_cw, cw[:, :], "cw")
        load_prm(p_cb, cb[:, :], "cb")
        load_prm(p_ba, rba[:, :], "ba")
        load_prm(p_bx, rbx[:, :], "bx")
        load_prm(p_lam, rlam[:, :], "lam")
        load_prm(p_n2g, n2g[:, :], "n2g")
        load_prm(p_nfg, nfg[:, :], "nfg")
        load_prm(p_fcw, fcw[:, :], "fcw")
        load_prm(p_fcb, fcb[:, :], "fcb")

        for kc in range(8):
            dma_sp(xfull[:, kc, :], xT[kc * 128:(kc + 1) * 128, :], "x%d" % (kc % 4), writes=[b_x[kc]])

        sch.op("dve", lambda h: h.memset(ones_bf[:], 1.0), writes=[b_ones])
        sch.op("dve", lambda h: h.memset(p_carry, 0.0), writes=[b_prm["carry"]])
        sch.op("dve", lambda h: h.memset(p_hz, 0.0), writes=[b_prm["hz"]])
        sch.op("dve", lambda h: h.memset(p_halo, 0.0), writes=[b_prm["halo"]] + b_halo)

        pieces = []
        for i in range(4):
            pieces.append(("ada", i, ada_v[:, :, i * 512:(i + 1) * 512]))
        for i in range(5):
            pieces.append(("win", i, win_v[:, :, i * 512:(i + 1) * 512]))
        for i in range(2):
            pieces.append(("wout", i, wout_v[:, :, i * 512:(i + 1) * 512]))
        piece_pos = {(k, i): n for n, (k, i, _) in enumerate(pieces)}
        issued = [0]

        def issue_pieces(upto):
            while issued[0] < min(upto, len(pieces)):
                n = issued[0]
                slot = n % 3
                dma_pool(wring[slot][:], pieces[n][2], "w%d" % slot, writes=[b_wr[slot]])
                issued[0] += 1

        def piece_slot(kind, idx):
            n = piece_pos[(kind, idx)]
            issue_pieces(n + 3)
            return n % 3

        issue_pieces(2)
        dma_pool(wabd[:], wa_bd[:, :], "wg", writes=[b_wabd])
        dma_pool(wxbd[:], wx_bd[:, :], "wg", writes=[b_wxbd])
        issue_pieces(3)

        act(p_tmp, p_c, AF.Tanh, [b_prm["c"]], [b_prm["tmp"]], scale=0.5)
        stt("dve", p_tmp, p_tmp, 1.0, p_c, ALU.add, ALU.mult, [b_prm["tmp"], b_prm["c"]], [b_prm["tmp"]])
        ts("dve", c_bf[:], p_tmp, 0.5, None, ALU.mult, None, [b_prm["tmp"]], [b_cbf])

        act(p_cl, p_lam, AF.Exp, [b_prm["lam"]], [b_prm["cl"]], scale=-1.0)
        act(p_cl, p_cl, AF.Ln, [b_prm["cl"]], [b_prm["cl"]], bias=1.0)
        ts("dve", p_cl2, p_cl, -8.0, None, ALU.mult, None, [b_prm["cl"]], [b_prm["cl2"]])
        ts("dve", p_cl, p_cl, -4.0, None, ALU.mult, None, [b_prm["cl"]], [b_prm["cl"]])
        ts("dve", p_ba, p_ba, 0.5, None, ALU.mult, None, [b_prm["ba"]], [b_prm["ba"]])
        ts("dve", p_bx, p_bx, 0.5, None, ALU.mult, None, [b_prm["bx"]], [b_prm["bx"]])


        MODB = 3

        def mod_piece(i, mb=MODB):
            slot = piece_slot("ada", i)
            mms = []
            for jj in range(4):
                j = i * 4 + jj
                for kc in range(8):
                    mms.append((bank(mb, 1, j), wring[slot][:, kc, jj * 128:(jj + 1) * 128], c_bf[:, kc:kc + 1], kc == 0, kc == 7))
            mm_group(mms, [b_wr[slot], b_cbf], [b_bank[mb]])
            tt("dve", p_mod[:, i * 4:(i + 1) * 4], bank(mb, 4, i * 4), p_adab[:, i * 4:(i + 1) * 4], ALU.add,
               [b_bank[mb], b_prm["adab"]], [b_prm["mod"]])

        aiss = [4]

        def issue_a(upto):
            while aiss[0] < min(upto, 12):
                i = aiss[0]
                dma_pool(aring[i % 2][:], ada_v[:, :, i * 512:(i + 1) * 512], "a%d" % (i % 2), writes=[b_ar[i % 2]])
                aiss[0] += 1

        def mod_piece2(i):
            issue_a(i + 2)
            slot = i % 2
            mms = []
            for jj in range(4):
                j = i * 4 + jj
                for kc in range(8):
                    mms.append((bank(MODB, 1, j), aring[slot][:, kc, jj * 128:(jj + 1) * 128], c_bf[:, kc:kc + 1], kc == 0, kc == 7))
            mm_group(mms, [b_ar[slot], b_cbf], [b_bank[MODB]])
            tt("dve", p_mod[:, i * 4:(i + 1) * 4], bank(MODB, 4, i * 4), p_adab[:, i * 4:(i + 1) * 4], ALU.add,
               [b_bank[MODB], b_prm["adab"]], [b_prm["mod"]])

        sh1, sc1, g1m = p_mod[:, 0:8], p_mod[:, 8:16], p_mod[:, 16:24]
        sh2, sc2, g2m = p_mod[:, 24:32], p_mod[:, 32:40], p_mod[:, 40:48]


        _chk('p0')
        sch.alias(b_sq + [b_rstd], b_et)
        for kc in range(8):
            s_ = kc % 2
            act(sqb[s_][:], xfull[:, kc, :], AF.Square, [b_x[kc]], [b_sq[s_]])
            mms = [(bank(4 + t), ones_bf[:], sqb[s_][:, t * 512:(t + 1) * 512], kc == 0, kc == 7) for t in range(4)]
            mm_group(mms, [b_sq[s_], b_ones], [b_bank[4 + t] for t in range(4)])
            if kc % 2 == 1:
                mod_piece(kc // 2)
        stt("dve", p_gm1, sc1, 1.0, p_n1g, ALU.add, ALU.mult, [b_prm["mod"], b_prm["n1g"]], [b_prm["gm1"]])
        act(rstd[:], psum[:, 2048:4096], AF.Ln, [b_bank[4 + t] for t in range(4)], [b_rstd], bias=EPS, scale=1.0 / D)
        act(rstd[:], rstd[:], AF.Exp, [b_rstd], [b_rstd], scale=-0.5)
        for hf_ in range(2):
            hs_ = slice(hf_ * 1024, (hf_ + 1) * 1024)
            for kc in range(8):
                stt("dve", xfull[:, kc, hs_], xfull[:, kc, hs_], p_gm1[:, kc:kc + 1], rstd[:, hs_], ALU.mult, ALU.mult,
                    [b_x[kc], b_prm["gm1"], b_rstd], [b_xn[kc][hf_]])
                act(hy[:, kc, hs_], xfull[:, kc, hs_], AF.Identity, [b_xn[kc][hf_], b_prm["mod"]], [b_hh[kc][hf_]], bias=sh1[:, kc:kc + 1])
        if debug:
            for kc in range(8):
                for t in range(4):
                    dump(dbg["d_h"][kc * 128:(kc + 1) * 128, t * 512:(t + 1) * 512], hy[:, kc, t * 512:(t + 1) * 512], b_hh[kc])
            dma_sp(dbg["d_mod"][:, :], p_mod, "dbg0", reads=[b_prm["mod"]])

        _chk('p1')
        sch.alias(flat(b_xr) + flat(b_gg) + flat(b_q) + flat(b_k), b_x + flat(b_xn))
        sch.alias([b_acc, b_xcb, b_A, b_C, b_D] + b_B, b_sq + [b_rstd] + b_et)
        for j in range(4):
            sch.op("dve", lambda h, j=j: h.memset(xrp[:, j, 0:8], 0.0), writes=[b_xr[j][0]])

        pb = [0]

        def next_bank():
            b = pb[0] % 3
            pb[0] += 1
            return b

        ev = [0]

        def inproj_fm(piece, mc, t):
            slot = piece_slot("win", piece)
            bk = next_bank()
            mms = [(bank(bk), wring[slot][:, kc, mc * 128:(mc + 1) * 128], hy[:, kc, t * 512:(t + 1) * 512], kc == 0, kc == 7)
                   for kc in range(8)]
            mm_group(mms, [b_wr[slot]] + [b_hh[kc][t // 2] for kc in range(8)], [b_bank[bk]])
            c0 = t * 512
            if piece == 0:
                dst = xrp[:, mc, 8 + c0:8 + c0 + 512]
                if ev[0] % 2 == 0:
                    act(dst, bank(bk), AF.Copy, [b_bank[bk]], [b_xr[mc][t]])
                else:
                    sch.op("dve", lambda h, d=dst, s=bank(bk): h.tensor_copy(d, s), reads=[b_bank[bk]], writes=[b_xr[mc][t]])
                ev[0] += 1
            elif piece == 1:
                g_ = ev[0] % 2
                ev[0] += 1
                act(gt[g_][:], bank(bk), AF.Square, [b_bank[bk]], [b_gt[g_]], scale=float(np.sqrt(0.044715)))
                stt("dve", gt[g_][:], gt[g_][:], 1.0, bank(bk), ALU.add, ALU.mult, [b_gt[g_], b_bank[bk]], [b_gt[g_]])
                act(gt[g_][:], gt[g_][:], AF.Tanh, [b_gt[g_]], [b_gt[g_]], scale=float(np.sqrt(2.0 / np.pi)))
                stt("dve", gg[:, mc, c0:c0 + 512], gt[g_][:], 1.0, bank(bk), ALU.add, ALU.mult, [b_gt[g_], b_bank[bk]], [b_gg[mc][t]])
            elif piece == 2:
                dst = qs[:, mc, c0:c0 + 512]
                if ev[0] % 4 != 3:
                    act(dst, bank(bk), AF.Copy, [b_bank[bk]], [b_q[mc][t]], scale=0.125)
                else:
                    ts("dve", dst, bank(bk), 0.125, None, ALU.mult, None, [b_bank[bk]], [b_q[mc][t]])
                ev[0] += 1
            else:
                dst = ks[:, mc, c0:c0 + 512]
                if ev[0] % 4 != 3:
                    act(dst, bank(bk), AF.Copy, [b_bank[bk]], [b_k[mc][t]])
                else:
                    sch.op("dve", lambda h, d=dst, s=bank(bk): h.tensor_copy(d, s), reads=[b_bank[bk]], writes=[b_k[mc][t]])
                ev[0] += 1

        def inproj_v(t16):
            slot = piece_slot("win", 4)
            bk = next_bank()
            mms = [(bank(bk), hy[:, kc, t16 * 128:(t16 + 1) * 128], wring[slot][:, kc, :], kc == 0, kc == 7) for kc in range(8)]
            mm_group(mms, [b_wr[slot]] + [b_hh[kc][t16 // 8] for kc in range(8)], [b_bank[bk]])
            dst = vt[:, t16, :]
            if ev[0] % 4 != 3:
                act(dst, bank(bk), AF.Copy, [b_bank[bk]], [b_v[t16]])
            else:
                sch.op("dve", lambda h, d=dst, s=bank(bk): h.tensor_copy(d, s), reads=[b_bank[bk]], writes=[b_v[t16]])
            ev[0] += 1

        for piece in (0,):
            for t in range(4):
                for mc in range(4):
                    inproj_fm(piece, mc, t)

        _chk('p2a')
        def rglru_chunk(j):
            for hf in range(2):
                c0 = hf * 1024
                act(t_acc[:], xrp[:, j, 8 + c0:8 + c0 + 1024], AF.Identity, b_xr[j] + [b_prm["cw"], b_prm["cb"]], [b_acc],
                    bias=p_cb[:, j:j + 1], scale=p_cw[:, j * 4 + 3:j * 4 + 4])
                for k in range(3):
                    src = xrp[:, j, 5 + k + c0:5 + k + c0 + 1024]
                    if k < 2:
                        stt("dve", t_acc[:], src, p_cw[:, j * 4 + k:j * 4 + k + 1], t_acc[:], ALU.mult, ALU.add,
                            b_xr[j] + [b_prm["cw"], b_acc], [b_acc])
                    else:
                        stt("dve", t_xcb[:], src, p_cw[:, j * 4 + k:j * 4 + k + 1], t_acc[:], ALU.mult, ALU.add,
                            b_xr[j] + [b_prm["cw"], b_acc], [b_xcb])
                yield
                yield
                mm_group([(bank(4 + t), wabd[:, j * 128:(j + 1) * 128], t_xcb[:, t * 512:(t + 1) * 512], True, True) for t in range(2)],
                         [b_wabd, b_xcb], [b_bank[4], b_bank[5]])
                mm_group([(bank(6 + t), wxbd[:, j * 128:(j + 1) * 128], t_xcb[:, t * 512:(t + 1) * 512], True, True) for t in range(2)],
                         [b_wxbd, b_xcb], [b_bank[6], b_bank[7]])
                act(t_A[:], psum[:, 2048:3072], AF.Tanh, [b_bank[4], b_bank[5], b_prm["ba"]], [b_A], bias=p_ba[:, j:j + 1], scale=0.5)
                act(t_C[:], psum[:, 3072:4096], AF.Tanh, [b_bank[6], b_bank[7], b_prm["bx"]], [b_C], bias=p_bx[:, j:j + 1], scale=0.5)
                act(t_B[hf][:], t_A[:], AF.Exp, [b_A, b_prm["cl2"]], [b_B[hf]], bias=p_cl2[:, j:j + 1], scale=p_cl2[:, j:j + 1])
                act(t_A[:], t_A[:], AF.Exp, [b_A, b_prm["cl"]], [b_A], bias=p_cl[:, j:j + 1], scale=p_cl[:, j:j + 1])
                yield
                stt("dve", t_C[:], t_C[:], 1.0, t_xcb[:], ALU.add, ALU.mult, [b_C, b_xcb], [b_C])
                act(t_B[hf][:], t_B[hf][:], AF.Sqrt, [b_B[hf]], [b_B[hf]], bias=1.0 / 16, scale=-1.0 / 16)
                tt("dve", t_C[:], t_C[:], t_B[hf][:], ALU.mult, [b_C, b_B[hf]], [b_C])
                sch.op("dve", lambda h, j=j: h.tensor_tensor_scan(t_D[:], t_A[:], t_C[:], p_carry[:, j:j + 1], ALU.mult, ALU.add),
                       reads=[b_A, b_C, b_prm["carry"]], writes=[b_D])
                sch.op("dve", lambda h, j=j: h.tensor_copy(p_carry[:, j:j + 1], t_D[:, 1023:1024]), reads=[b_D], writes=[b_prm["carry"]])
                yield
                tt("dve", gg[:, j, c0:c0 + 1024], gg[:, j, c0:c0 + 1024], t_D[:], ALU.mult, b_gg[j][2 * hf:2 * hf + 2] + [b_D], b_gg[j][2 * hf:2 * hf + 2])
                yield

        TR = {0: (0, 1), 1: (0, 3), 2: (0, 5), 3: (0, 7), 4: (0, 7), 5: (2, 7), 6: (4, 7), 7: (6, 7)}
        pi = [0]

        def attention():
            LAG = 3
            SBP = [0, 4]
            steps = []
            for c in range(4):
                for qt in range(4):
                    js = [4] + [j for j in (0, 1, 2, 3, 5, 6, 7) if 4 * qt - 4 + j >= 0]
                    for idx, j in enumerate(js):
                        steps.append((c, qt, idx, j, len(js)))
            npairs = len(steps)
            psv = psum.rearrange("p (b n) -> p b n", n=512)

            def geom(k):
                c, qt, idx, j, nj = steps[k]
                tlo, thi = TR[j]
                return dict(c=c, qt=qt, idx=idx, j=j, nj=nj, gkb=4 * qt - 4 + j, tlo=tlo, nco=(thi - tlo + 1) * 64,
                            q0=qt * 512 + tlo * 64, d0=(tlo - 2 * j + 8) * 64, sb=SBP[k % 2], pp=pbuf2[k % 4], bpp=b_pp[k % 4],
                            ob=(2, 3) if (c * 4 + qt) % 2 == 0 else (6, 7))

            def front(k):
                g = geom(k)
                c, nco, sb = g["c"], g["nco"], g["sb"]
                mms = []
                for e in range(2):
                    pr = slice(e * 64, (e + 1) * 64)
                    mms.append((bank(sb + e, nco), ks[pr, c, g["gkb"] * 128:(g["gkb"] + 1) * 128], qs[pr, c, g["q0"]:g["q0"] + nco], True, False))
                for e in range(2):
                    mms.append((bank(sb + e, nco), ident_bf[:], Etab[:, 2 * c + e, g["d0"]:g["d0"] + nco], False, True))
                mm_group(mms, b_k[c] + b_q[c] + [b_ident, b_E[2 * c], b_E[2 * c + 1]], [b_bank[sb], b_bank[sb + 1]])
                act(g["pp"][:, :, 0:nco], psv[:, sb:sb + 2, 0:nco], AF.Exp, [b_bank[sb], b_bank[sb + 1]], [g["bpp"]])

            def back(k):
                g = geom(k)
                c, qt, nco, tlo = g["c"], g["qt"], g["nco"], g["tlo"]
                ob, db_ = g["ob"]
                first = g["idx"] == 0
                last = g["idx"] == g["nj"] - 1
                mms = []
                for e in range(2):
                    pr = slice(e * 64, (e + 1) * 64)
                    hh = 2 * c + e
                    mms.append((psum[pr, ob * 512 + tlo * 64:ob * 512 + tlo * 64 + nco], vt[:, g["gkb"], hh * 64:(hh + 1) * 64], g["pp"][:, e, 0:nco], first, last))
                for e in range(2):
                    pr = slice(e * 64, (e + 1) * 64)
                    mms.append((psum[pr, db_ * 512 + tlo * 64:db_ * 512 + tlo * 64 + nco], ones_bf[:, 0:64], g["pp"][:, e, 0:nco], first, last))
                mm_group(mms, [b_v[g["gkb"]], g["bpp"], b_ones], [b_bank[ob], b_bank[db_]])
                if last:
                    sch.op("dve", lambda h, db_=db_: h.reciprocal(rden[:], bank(db_)), reads=[b_bank[db_]], writes=[b_rden])
                    tt("dve", hy[:, 4 + c, qt * 512:(qt + 1) * 512], bank(ob), rden[:], ALU.mult, [b_bank[ob], b_rden],
                       [b_y[4 + c][qt]] + b_hh[4 + c])

            nmod = [12]
            for k in range(npairs + LAG):
                if k < npairs:
                    front(k)
                if k >= LAG:
                    back(k - LAG)
                if k % 12 == 8 and nmod[0] < 12:
                    mod_piece(nmod[0], mb=SBP[(k + 1) % 2])
                    nmod[0] += 1
                yield
            while nmod[0] < 12:
                mod_piece(nmod[0], mb=SBP[0])
                nmod[0] += 1

        def load_bias_tables():
            dma_pool(ident_bf[:], ident[:, :], "wg", writes=[b_ident])
            dma_pool(Etab[:], tbias.rearrange("h p n -> p h n"), "wg", writes=b_E)
            for hh in range(8):
                sch.op("dve", lambda h, hh=hh: h.memset(Etab[0:64, hh, 576:640], -30000.0), writes=[b_E[hh]])
                sch.op("dve", lambda h, hh=hh: h.memset(Etab[64:128, hh, 0:64], -30000.0), writes=[b_E[hh]])

        def inproj_rest():
            cnt = [0]
            nm = [4]

            def tick():
                cnt[0] += 1
                if cnt[0] % 8 == 4 and nm[0] < 12:
                    mod_piece2(nm[0])
                    nm[0] += 1

            issue_a(6)
            for piece in (1,):
                for mc in range(4):
                    for t in range(4):
                        inproj_fm(piece, mc, t)
                        tick()
                        yield
            load_bias_tables()
            for piece in (2, 3):
                for mc in range(4):
                    for t in range(4):
                        inproj_fm(piece, mc, t)
                        tick()
                        yield
            for t16 in range(16):
                inproj_v(t16)
                tick()
                yield
            while nm[0] < 12:
                mod_piece2(nm[0])
                nm[0] += 1

        def rglru_all():
            for j in range(4):
                yield from rglru_chunk(j)

        ga, gb = rglru_all(), inproj_rest()
        NA, NB = 40, 64
        ia = ib = 0
        da = db = False
        while not (da and db):
            take_a = (not da) and (db or ia * NB <= ib * NA)
            if take_a:
                try:
                    next(ga)
                    ia += 1
                except StopIteration:
                    da = True
            else:
                try:
                    next(gb)
                    ib += 1
                except StopIteration:
                    db = True
        for _ in attention():
            pass

        _chk('p3')
        stt("dve", p_gm2, sc2, 1.0, p_n2g, ALU.add, ALU.mult, [b_prm["mod"], b_prm["n2g"]], [b_prm["gm2"]])

        if debug:
            for j in range(4):
                for t in range(4):
                    cs = slice(t * 512, (t + 1) * 512)
                    dump(dbg["d_xr"][j * 128:(j + 1) * 128, cs], xrp[:, j, 8 + t * 512:8 + (t + 1) * 512], b_xr[j])
                    dump(dbg["d_q"][j * 128:(j + 1) * 128, cs], qs[:, j, cs], b_q[j])
                    dump(dbg["d_k"][j * 128:(j + 1) * 128, cs], ks[:, j, cs], b_k[j])
            for t16 in range(16):
                dump(dbg["d_v"][t16 * 128:(t16 + 1) * 128, :], vt[:, t16, :], [b_v[t16]])
            for j in range(8):
                for t in range(4):
                    cs = slice(t * 512, (t + 1) * 512)
                    dump(dbg["d_y"][j * 128:(j + 1) * 128, cs], gg[:, j, cs] if j < 4 else hy[:, j, cs], b_gg[j] if j < 4 else b_y[j])

        _chk('p3d')
        xi = [0]

        def xload(dst, bdst, hf, kc):
            dma_sp(dst, xT[kc * 128:(kc + 1) * 128, hf * 1024:(hf + 1) * 1024], "x%d" % (xi[0] % 4), writes=[bdst])
            xi[0] += 1

        def add_delta(dst, bdst, hf, kc):
            tt("dve", dst, dst, delta1[:, kc, hf * 1024:(hf + 1) * 1024], ALU.add, [bdst] + b_dl[kc], [bdst])

        def stats_sq(src, bsrc, i):
            act(sq2[i % 2][:], src, AF.Square, [bsrc], [b_sq2[i % 2]])

        def stats_mm(i, b0_, n_):
            mm_group([(bank(b0_ + t), ones_bf[:], sq2[i % 2][:, t * 512:(t + 1) * 512], i == 0, i == n_ - 1) for t in range(2)],
                     [b_sq2[i % 2], b_ones], [b_bank[b0_], b_bank[b0_ + 1]])

        def fstat_sq(m):
            act(sq3[m % 2][:], x2h[:, m, :], AF.Square, [b_x2[m]], [b_ca[m % 2], b_cah[m % 2]])

        def fstat_mm(m, b0_):
            mm_group([(bank(b0_ + t), ones_bf[:], sq3[m % 2][:, t * 512:(t + 1) * 512], m == 0, m == 7) for t in range(2)],
                     [b_ca[m % 2], b_cah[m % 2], b_ones], [b_bank[b0_], b_bank[b0_ + 1]])

        def make_rstd(b0_):
            act(rstd2[:], psum[:, b0_ * 512:b0_ * 512 + 1024], AF.Ln, [b_bank[b0_], b_bank[b0_ + 1]], [b_rstd2], bias=EPS, scale=1.0 / D)
            act(rstd2[:], rstd2[:], AF.Exp, [b_rstd2], [b_rstd2], scale=-0.5)

        sch.alias(b_x2, b_v + [b_acc, b_xcb, b_A, b_C, b_D] + b_B)
        sch.alias(b_sq2, [b for row in b_hh[0:4] for b in row])
        for kc in range(8):
            xload(x2h[:, kc, :], b_x2[kc], 0, kc)

        sch.alias(flat(b_dl), flat(b_q) + flat(b_k))
        y_all = flat(b_gg) + [b for row in b_y[4:] for b in row]

        def ysrc(kc, cs):
            return gg[:, kc, cs] if kc < 4 else hy[:, kc, cs]
        for m in range(8):
            slot = piece_slot("wout", m // 4)
            mo = (m % 4) * 128
            for t in range(4):
                bk = next_bank()
                mms = [(bank(bk), wring[slot][:, kc, mo:mo + 128], ysrc(kc, slice(t * 512, (t + 1) * 512)), kc == 0, kc == 7) for kc in range(8)]
                mm_group(mms, [b_wr[slot]] + [b_gg[kc][t] for kc in range(4)] + [b_y[kc][t] for kc in range(4, 8)], [b_bank[bk]])
                dst = delta1[:, m, t * 512:(t + 1) * 512]
                if ev[0] % 2 == 0:
                    act(dst, bank(bk), AF.Copy, [b_bank[bk], b_prm["mod"]], [b_dl[m][t]], scale=g1m[:, m:m + 1])
                else:
                    ts("dve", dst, bank(bk), g1m[:, m:m + 1], None, ALU.mult, None, [b_bank[bk], b_prm["mod"]], [b_dl[m][t]])
                ev[0] += 1
            add_delta(x2h[:, m, :], b_x2[m], 0, m)
            stats_sq(x2h[:, m, :], b_x2[m], m)
            if m >= 1:
                stats_mm(m - 1, 6, 8)
        stats_mm(7, 6, 8)
        sch.alias([b_rstd2], [b for row in b_hh[0:4] for b in row])
        make_rstd(6)
        if debug:
            for j in range(8):
                for t in range(4):
                    cs = slice(t * 512, (t + 1) * 512)
                    dump(dbg["d_dl"][j * 128:(j + 1) * 128, cs], delta1[:, j, cs], b_dl[j])

        _chk('p4')
        sch.barrier()
        uiss = [0]
        upieces = []
        for hf in range(2):
            for f in range(NFC):
                for part in range(2):
                    upieces.append(wup_v[:, :, part * DFF + f * 128:part * DFF + (f + 1) * 128])

        def issue_u(upto):
            while uiss[0] < min(upto, len(upieces)):
                n = uiss[0]
                dma_pool(uring[n % NU][:], upieces[n], "u%d" % (n % NU), writes=[b_ur[n % NU]])
                uiss[0] += 1

        diss = [0]
        dpieces = [wdn_v[:, :, m * 128:(m + 1) * 128] for hf in range(2) for m in range(8)]

        def issue_d(upto):
            while diss[0] < min(upto, len(dpieces)):
                n = diss[0]
                for hh_, (ka, kb_) in enumerate(((0, 8), (8, 16), (16, 22))):
                    dma_pool(dring[n % ND][:, ka:kb_, :], dpieces[n][:, ka:kb_, :],
                             "d%d_%d" % (n % ND, hh_), writes=[b_dr2[n % ND][hh_]])
                diss[0] += 1

        issue_u(NU)
        pend = [None]
        ub = [0]
        oi = [0]
        def side_runner(units):
            it = iter(units)

            def run(n_):
                for _ in range(n_):
                    u = next(it, None)
                    if u is None:
                        return
                    u()
            return run

        for kc in range(8):
            s_ = kc % 2
            stt("dve", ntmp[s_][:], x2h[:, kc, :], p_gm2[:, kc:kc + 1], rstd2[:], ALU.mult, ALU.mult,
                [b_x2[kc], b_prm["gm2"], b_rstd2], [b_nt[s_]])
            act(h2b[0][:, kc, :], ntmp[s_][:], AF.Identity, [b_nt[s_], b_prm["mod"]], [b_h2[0][kc]], bias=sh2[:, kc:kc + 1])
        _chk('f_norm0')
        issue_d(ND - 1)

        def n2_units():
            us_ = []
            us_.append(lambda: xload(ntmp[0][:], b_nt[0], 1, 0))
            for kc in range(8):
                def p1(kc=kc):
                    if kc + 1 < 8:
                        xload(ntmp[(kc + 1) % 2][:], b_nt[(kc + 1) % 2], 1, kc + 1)
                    add_delta(ntmp[kc % 2][:], b_nt[kc % 2], 1, kc)
                    stats_sq(ntmp[kc % 2][:], b_nt[kc % 2], kc)
                us_.append(p1)
                if kc >= 1:
                    us_.append(lambda kc=kc: stats_mm(kc - 1, 4, 8))
            us_.append(lambda: stats_mm(7, 4, 8))
            us_.append(lambda: xload(ntmp[0][:], b_nt[0], 1, 0))
            us_.append(lambda: make_rstd(4))
            for kc in range(8):
                def p2(kc=kc):
                    if kc + 1 < 8:
                        xload(ntmp[(kc + 1) % 2][:], b_nt[(kc + 1) % 2], 1, kc + 1)
                    s_ = kc % 2
                    add_delta(ntmp[s_][:], b_nt[s_], 1, kc)
                    stt("dve", ntmp[s_][:], ntmp[s_][:], p_gm2[:, kc:kc + 1], rstd2[:], ALU.mult, ALU.mult,
                        [b_nt[s_], b_prm["gm2"], b_rstd2], [b_nt[s_]])
                    act(h2b[1][:, kc, :], ntmp[s_][:], AF.Identity, [b_nt[s_], b_prm["mod"]], [b_h2[1][kc]], bias=sh2[:, kc:kc + 1])
                us_.append(p2)
            return us_

        def fin_units(hf, preload_next):
            us_ = []
            if preload_next:
                for m in range(8):
                    us_.append(lambda m=m: stats_sq(x2h[:, m, :], b_x2[m], m))
                    if m >= 1:
                        us_.append(lambda m=m: stats_mm(m - 1, 6, 8))
                us_.append(lambda: stats_mm(7, 6, 8))
                us_.append(lambda: make_rstd(6))
            else:
                us_.append(lambda: make_rstd(4))
            for m in range(8):
                def st(m=m):
                    if preload_next or m % 3 == 2:
                        act(x2h[:, m, :], x2h[:, m, :], AF.Copy, [b_x2[m], b_prm["nfg"]], [b_x2[m]], scale=p_nfg[:, m:m + 1])
                        tt("pool", x2h[:, m, :], x2h[:, m, :], rstd2[:], ALU.mult, [b_x2[m], b_rstd2], [b_x2[m]])
                    else:
                        stt("dve", x2h[:, m, :], x2h[:, m, :], p_nfg[:, m:m + 1], rstd2[:], ALU.mult, ALU.mult,
                            [b_x2[m], b_prm["nfg"], b_rstd2], [b_x2[m]])
                    dma_sp(outT[m * 128:(m + 1) * 128, hf * 1024:(hf + 1) * 1024], x2h[:, m, :], "o%d" % (oi[0] % 4), reads=[b_x2[m]])
                    oi[0] += 1
                us_.append(st)
            if preload_next:
                for kc in range(8):
                    us_.append(lambda kc=kc: xload(x2h[:, kc, :], b_x2[kc], hf + 1, kc))
                for kc in range(8):
                    us_.append(lambda kc=kc: tt("pool", x2h[:, kc, :], x2h[:, kc, :], delta1[:, kc, (hf + 1) * 1024:(hf + 2) * 1024], ALU.add,
                                                [b_x2[kc]] + b_dl[kc], [b_x2[kc]]))
            return us_

        side = side_runner([])
        for hf in range(2):
            h2h = h2b[hf]
            bh2 = b_h2[hf]
            for f in range(NFC):
                for part in range(2):
                    n = (hf * NFC + f) * 2 + part
                    issue_u(n + NU)
                    us = n % NU
                    ci = part * NFC + f
                    b0_ = (ub[0] % 3) * 2
                    ub[0] += 1
                    mms = []
                    for t in range(2):
                        for kc in range(8):
                            mms.append((bank(b0_ + t), uring[us][:, kc, :], h2h[:, kc, t * 512:(t + 1) * 512], kc == 0, kc == 7))
                    mm_group(mms, [b_ur[us]] + bh2, [b_bank[b0_], b_bank[b0_ + 1]])
                    pu = psum[:, b0_ * 512:b0_ * 512 + 1024]
                    bks = [b_bank[b0_], b_bank[b0_ + 1]]
                    ca = cacc[part * 2 + (f % 2)]
                    bca = b_ca[part * 2 + (f % 2)]
                    w0 = p_fcw[:, ci * 3 + 0:ci * 3 + 1]
                    w1 = p_fcw[:, ci * 3 + 1:ci * 3 + 2]
                    w2 = p_fcw[:, ci * 3 + 2:ci * 3 + 3]
                    hzs = (n % 2) * 4
                    hz = p_hz[:, hzs:hzs + 4]
                    bhz = b_hz[n % 2]
                    bcah = b_cah[part * 2 + (f % 2)]
                    hrd = (1 - hf) * 88 + 2 * ci
                    hwr = hf * 88 + 2 * ci
                    sch.op("dve", lambda h, hrd=hrd, hz=hz: h.tensor_copy(hz[:, 0:2], p_halo[:, hrd:hrd + 2]),
                           reads=[b_halo[1 - hf]], writes=[bhz[0]])
                    act(hz[:, 2:4], pu[:, 0:2], AF.Copy, bks, [bhz[1]])
                    act(p_halo[:, hwr:hwr + 2], pu[:, 1022:1024], AF.Copy, bks, [b_halo[hf]])
                    act(ca[:], pu, AF.Identity, bks + [b_prm["fcw"], b_prm["fcb"]], [bca, bcah], bias=p_fcb[:, ci:ci + 1], scale=w2)
                    stt("dve", ca[:, 2:1024], pu[:, 1:1023], w1, ca[:, 2:1024], ALU.mult, ALU.add, bks + [bca, b_prm["fcw"]], [bca])
                    stt("dve", ca[:, 2:1024], pu[:, 0:1022], w0, ca[:, 2:1024], ALU.mult, ALU.add, bks + [bca, b_prm["fcw"]], [bca])
                    stt("dve", ca[:, 0:2], hz[:, 1:3], w1, ca[:, 0:2], ALU.mult, ALU.add, [bhz[0], bhz[1], bcah, b_prm["fcw"]], [bcah])
                    stt("dve", ca[:, 0:2], hz[:, 0:2], w0, ca[:, 0:2], ALU.mult, ALU.add, [bhz[0], bhz[1], bcah, b_prm["fcw"]], [bcah])
                    if pend[0] is not None:
                        pend[0]()
                    if part == 0:
                        pend[0] = (lambda ca=ca, bca=bca, bcah=bcah: act(ca[:], ca[:], AF.Silu, [bca, bcah], [bca, bcah]))
                    else:
                        pend[0] = (lambda f=f, ca=ca, bca=bca, bcah=bcah: tt("pool", gbuf[:, f, :], cacc[f % 2][:], ca[:], ALU.mult,
                                                                         [b_ca[f % 2], b_cah[f % 2], bca, bcah], [b_g[f]]))
                side(2)
            if pend[0] is not None:
                pend[0]()
                pend[0] = None
            side(1000)
            _chk('f_up%d' % hf)
            side = side_runner(n2_units() if hf == 0 else [])
            fsb = 2 if hf == 0 else 4
            for m in range(8):
                n = hf * 8 + m
                issue_d(n + ND)
                ds_ = n % ND
                if hf == 1:
                    add_delta(x2h[:, m, :], b_x2[m], 1, m)
                for t in range(2):
                    bk = 6 + t
                    mms = [(bank(bk), dring[ds_][:, kc, :], gbuf[:, kc, t * 512:(t + 1) * 512], kc == 0, kc == NFC - 1) for kc in range(NFC)]
                    if m == 0 and t == 0:
                        mm_group(mms[:16], b_dr2[ds_] + b_g[:16], [b_bank[bk]])
                        mm_group(mms[16:20], b_dr2[ds_] + b_g[16:20], [b_bank[bk]])
                        mm_group(mms[20:], b_dr2[ds_] + b_g[20:], [b_bank[bk]])
                    else:
                        mm_group(mms, b_dr2[ds_] + b_g, [b_bank[bk]])
                    stt("dve", x2h[:, m, t * 512:(t + 1) * 512], bank(bk), g2m[:, m:m + 1], x2h[:, m, t * 512:(t + 1) * 512],
                        ALU.mult, ALU.add, [b_bank[bk], b_prm["mod"], b_x2[m]], [b_x2[m]])
                    side(2)
                fstat_sq(m)
                if m >= 1:
                    fstat_mm(m - 1, fsb)
            fstat_mm(7, fsb)
            side(1000)
            _chk('f_down%d' % hf)
            make_rstd(fsb)
            for m in range(8):
                if hf == 1 and m % 3 == 2:
                    act(x2h[:, m, :], x2h[:, m, :], AF.Copy, [b_x2[m], b_prm["nfg"]], [b_x2[m]], scale=p_nfg[:, m:m + 1])
                    tt("pool", x2h[:, m, :], x2h[:, m, :], rstd2[:], ALU.mult, [b_x2[m], b_rstd2], [b_x2[m]])
                else:
                    stt("dve", x2h[:, m, :], x2h[:, m, :], p_nfg[:, m:m + 1], rstd2[:], ALU.mult, ALU.mult,
                        [b_x2[m], b_prm["nfg"], b_rstd2], [b_x2[m]])
                dma_sp(outT[m * 128:(m + 1) * 128, hf * 1024:(hf + 1) * 1024], x2h[:, m, :], "o%d" % (oi[0] % 4), reads=[b_x2[m]])
                oi[0] += 1
            if hf == 0:
                for kc in range(8):
                    xload(x2h[:, kc, :], b_x2[kc], 1, kc)
            side = side_runner([])

    except _Stop:
        pass

    fin = [(s, c) for s, c in sch.cnt.items() if s.startswith("o") or s.startswith("dbg")]
    sch.wait_only("sp", fin)

    sem_names = sorted(sch.cnt.keys())
    sems = {}
    import contextlib
    with contextlib.ExitStack() as es:
        for nme in sem_names:
            sems[nme] = es.enter_context(nc.semaphore("s_" + nme))
        block = es.enter_context(nc.Block())
        handles = {"pe": block.tensor, "act": block.scalar, "dve": block.vector, "pool": block.gpsimd, "sp": block.sync}
        for e in Sched.ENGS:
            ops = sch.ops[e]

            def body(h, ops=ops, e=e):
                for waits, fn, dma in ops:
                    for s_, c_ in waits:
                        h.wait_ge(sems[s_], c_)
                    if fn is None:
                        continue
                    ins = fn(h)
                    if dma is not None:
                        ins.then_inc(sems[dma], 16)
                    else:
                        ins.then_inc(sems[e], 1)
            handles[e](body)
    return nc


def _fm(v):
    v = np.asarray(v, np.float32)
    return np.ascontiguousarray(v.reshape(-1, 128).T)


def prep_inputs(b, x, c, ada_w, ada_b, norm1_g, w_in, rnn_conv_w, rnn_conv_b, rg_wa, rg_ba, rg_wx, rg_bx,
                rg_lambda, rel_bias, w_out, norm2_g, w_up, ffn_conv_w, ffn_conv_b, w_down, final_g, shared):
    m = dict(shared)
    m["xT"] = np.ascontiguousarray(np.asarray(x[b], np.float32).T)
    m["cT"] = _fm(c[b])
    return m


def prep_shared(ada_w, ada_b, norm1_g, w_in, rnn_conv_w, rnn_conv_b, rg_wa, rg_ba, rg_wx, rg_bx,
                rg_lambda, rel_bias, w_out, norm2_g, w_up, ffn_conv_w, ffn_conv_b, w_down, final_g):
    f = np.float32
    sh = {}
    sh["ada_w"] = np.ascontiguousarray(np.asarray(ada_w[0], f))
    sh["ada_bT"] = _fm(ada_b[0])
    sh["n1g"] = _fm(norm1_g[0])
    sh["n2g"] = _fm(norm2_g[0])
    sh["nfg"] = _fm(final_g)
    sh["w_in"] = np.ascontiguousarray(np.asarray(w_in[0], f))
    cwv = np.asarray(rnn_conv_w[0], f)
    sh["cw"] = np.ascontiguousarray(cwv.reshape(4, 4, 128).transpose(2, 1, 0).reshape(128, 16))
    sh["cb"] = _fm(rnn_conv_b[0])

    def bd(w):
        w = np.asarray(w, f)
        o = np.zeros((128, 4, 128), f)
        for j in range(4):
            o[0:64, j, 0:64] = w[2 * j]
            o[64:128, j, 64:128] = w[2 * j + 1]
        return np.ascontiguousarray(o.reshape(128, 512))
    sh["wa_bd"] = bd(rg_wa[0])
    sh["wx_bd"] = bd(rg_wx[0])
    sh["rba"] = _fm(rg_ba[0])
    sh["rbx"] = _fm(rg_bx[0])
    sh["rlam"] = _fm(rg_lambda[0])
    kk = np.arange(128)[:, None]
    col = np.arange(640)[None, :]
    idx = np.clip(col - kk, -128, 128) + 128
    sh["tbias"] = np.ascontiguousarray(np.asarray(rel_bias[0], f)[:, idx])
    sh["ident"] = np.eye(128, dtype=f)
    sh["w_out"] = np.ascontiguousarray(np.asarray(w_out[0], f))
    sh["w_up"] = np.ascontiguousarray(np.asarray(w_up[0], f))
    fw = np.asarray(ffn_conv_w[0], f)
    sh["fcw"] = np.ascontiguousarray(fw.reshape(3, 44, 128).transpose(2, 1, 0).reshape(128, 132))
    sh["fcb"] = _fm(ffn_conv_b[0])
    sh["w_down"] = np.ascontiguousarray(np.asarray(w_down[0], f))
    return sh


_NC_CACHE = {}


def kernel(x, c, ada_w, ada_b, norm1_g, w_in, rnn_conv_w, rnn_conv_b, rg_wa, rg_ba, rg_wx, rg_bx,
           rg_lambda, rel_bias, w_out, norm2_g, w_up, ffn_conv_w, ffn_conv_b, w_down, final_g, _debug=False, _stop=None):
    x = np.asarray(x)
    c = np.asarray(c)
    shared = prep_shared(ada_w, ada_b, norm1_g, w_in, rnn_conv_w, rnn_conv_b, rg_wa, rg_ba, rg_wx, rg_bx,
                         rg_lambda, rel_bias, w_out, norm2_g, w_up, ffn_conv_w, ffn_conv_b, w_down, final_g)
    in_maps = []
    for b in range(NCORE):
        m = dict(shared)
        m["xT"] = np.ascontiguousarray(np.asarray(x[b], np.float32).T)
        m["cT"] = _fm(c[b])
        in_maps.append(m)
    nc = build_program(debug=_debug, stop_at=_stop)
    res = run_bass_kernel_spmd(nc, in_maps, core_ids=list(range(NCORE)))
    out = np.stack([np.ascontiguousarray(res.results[b]["outT"].T) for b in range(NCORE)], axis=0).astype(np.float32)
    if _debug:
        return out, res.results
    return out
```

```python
import numpy as np
import concourse.bass as bass
import concourse.mybir as mybir
from concourse.bass_utils import run_bass_kernel_spmd

F32 = mybir.dt.float32
BF16 = mybir.dt.bfloat16
ALU = mybir.AluOpType
AF = mybir.ActivationFunctionType

S = 2048
D = 1024
NCORE = 8
DFF = 2816
NFC = DFF // 128
EPS = 1e-6
ATT_WARM = 3
NU = 5
ND = 2
B0 = 16512
SB_END = 229344


class Buf:
    __slots__ = ("name", "w", "r")

    def __init__(self, name):
        self.name = name
        self.w = None
        self.r = []


class Sched:
    ENGS = ["pe", "act", "dve", "pool", "sp"]

    def __init__(self):
        self.ops = {e: [] for e in self.ENGS}
        self.cnt = {}
        self.seen = {e: {} for e in self.ENGS}

    def _need(self, eng, waits, tok, war=False):
        if tok is None:
            return
        s, c = tok
        if s == eng and eng == "pe":
            return
        if self.seen[eng].get(s, 0) >= c:
            return
        if waits.get(s, 0) < c:
            waits[s] = c

    def op(self, eng, fn, reads=(), writes=(), dma=None):
        waits = {}
        for b in reads:
            self._need(eng, waits, b.w)
        for b in writes:
            self._need(eng, waits, b.w)
            for t in b.r:
                self._need(eng, waits, t, war=True)
        if dma is not None:
            c = self.cnt.get(dma, 0)
            if c > 0:
                self._need(eng, waits, (dma, c))
            self.cnt[dma] = c + 16
            tok = (dma, c + 16)
        else:
            self.cnt[eng] = self.cnt.get(eng, 0) + 1
            tok = (eng, self.cnt[eng])
        for s, c in waits.items():
            self.seen[eng][s] = c
        self.ops[eng].append((sorted(waits.items()), fn, dma))
        for b in reads:
            b.r.append(tok)
        for b in writes:
            b.w = tok
            b.r = []
        return tok

    def wait_only(self, eng, toks):
        waits = {}
        for t in toks:
            self._need(eng, waits, t)
        for s, c in waits.items():
            self.seen[eng][s] = c
        self.ops[eng].append((sorted(waits.items()), None, None))

    def barrier(self):
        toks = [(s, c) for s, c in self.cnt.items()]
        for e in self.ENGS:
            self.wait_only(e, toks)

    def alias(self, new_bufs, old_bufs):
        toks = []
        for b in old_bufs:
            if b.w is not None:
                toks.append(b.w)
            toks.extend(b.r)
        for nb in new_bufs:
            nb.w = None
            nb.r = list(toks)


class _Stop(Exception):
    pass


def build_program(debug=False, stop_at=None):
    nc = bass.Bass("TRN2", target_bir_lowering=False)
    sch = Sched()

    def din(name, shape):
        return nc.dram_tensor(name, list(shape), F32, kind="ExternalInput").ap()

    xT = din("xT", [D, S])
    cT = din("cT", [128, 8])
    ada_w = din("ada_w", [D, 6 * D])
    ada_bT = din("ada_bT", [128, 48])
    n1g = din("n1g", [128, 8])
    n2g = din("n2g", [128, 8])
    nfg = din("nfg", [128, 8])
    w_in = din("w_in", [D, 2560])
    cw = din("cw", [128, 16])
    cb = din("cb", [128, 4])
    wa_bd = din("wa_bd", [128, 512])
    wx_bd = din("wx_bd", [128, 512])
    rba = din("rba", [128, 4])
    rbx = din("rbx", [128, 4])
    rlam = din("rlam", [128, 4])
    tbias = din("tbias", [8, 128, 640])
    ident = din("ident", [128, 128])
    w_out = din("w_out", [D, D])
    w_up = din("w_up", [D, 2 * DFF])
    fcw = din("fcw", [128, 44 * 3])
    fcb = din("fcb", [128, 44])
    w_down = din("w_down", [DFF, D])
    outT = nc.dram_tensor("outT", [D, S], F32, kind="ExternalOutput").ap()
    dbg = {}
    if debug:
        for nm, shp in [("d_h", [D, S]), ("d_xr", [512, S]), ("d_gg", [512, S]), ("d_q", [512, S]),
                        ("d_k", [512, S]), ("d_v", [S, 512]), ("d_y", [D, S]), ("d_dl", [D, S]),
                        ("d_mod", [128, 48])]:
            dbg[nm] = nc.dram_tensor(nm, shp, F32, kind="ExternalOutput").ap()

    ada_v = ada_w.rearrange("(kc p) n -> p kc n", p=128)
    win_v = w_in.rearrange("(kc p) n -> p kc n", p=128)
    wout_v = w_out.rearrange("(kc p) n -> p kc n", p=128)
    wup_v = w_up.rearrange("(kc p) n -> p kc n", p=128)
    wdn_v = w_down.rearrange("(kc p) n -> p kc n", p=128)

    off = [B0]

    def alloc(name, shape, dt, at=None):
        nbytes = int(np.prod(shape[1:])) * (4 if dt == F32 else 2)
        if at is None:
            at = off[0]
            off[0] += (nbytes + 31) // 32 * 32
            assert off[0] <= SB_END, (name, off[0])
        return nc.alloc_sbuf_tensor_at(name, list(shape), dt, offset=at)

    prm = alloc("prm", [128, 576], F32)
    PC = {}
    pc = [0]

    def pslot(name, n):
        PC[name] = (pc[0], n)
        pc[0] += n
        assert pc[0] <= 576
        return prm[:, PC[name][0]:PC[name][0] + n]

    p_c = pslot("c", 8)
    p_adab = pslot("adab", 48)
    p_mod = pslot("mod", 48)
    p_n1g = pslot("n1g", 8)
    p_n2g = pslot("n2g", 8)
    p_nfg = pslot("nfg", 8)
    p_gm1 = pslot("gm1", 8)
    p_gm2 = pslot("gm2", 8)
    p_cw = pslot("cw", 16)
    p_cb = pslot("cb", 4)
    p_ba = pslot("ba", 4)
    p_bx = pslot("bx", 4)
    p_lam = pslot("lam", 4)
    p_cl = pslot("cl", 4)
    p_cl2 = pslot("cl2", 4)
    p_tmp = pslot("tmp", 8)
    p_fcw = pslot("fcw", 132)
    p_fcb = pslot("fcb", 44)
    p_carry = pslot("carry", 4)
    p_hz = pslot("hz", 8)
    p_hzt = pslot("hzt", 8)
    p_halo = pslot("halo", 176)
    b_prm = {k: Buf("prm_" + k) for k in PC}
    c_bf = alloc("c_bf", [128, 8], BF16)
    b_cbf = Buf("c_bf")
    ones_bf = alloc("ones_bf", [128, 128], BF16)
    b_ones = Buf("ones")
    ident_bf = alloc("ident_bf", [128, 128], BF16)
    b_ident = Buf("ident")
    wabd = alloc("wabd", [128, 512], BF16)
    wxbd = alloc("wxbd", [128, 512], BF16)
    b_wabd, b_wxbd = Buf("wabd"), Buf("wxbd")
    Etab = alloc("Etab", [128, 8, 640], BF16)
    b_E = [Buf("E%d" % h) for h in range(8)]

    STAGE0 = off[0]
    XR0 = off[0]
    xfull = alloc("xfull", [128, 8, 2048], F32)
    b_x = [Buf("x%d" % i) for i in range(8)]
    xrp = alloc("xrp", [128, 4, 2056], BF16, at=XR0)
    gg = alloc("gg", [128, 4, 2048], BF16, at=XR0 + 16448)
    qs = alloc("qs", [128, 4, 2048], BF16, at=XR0 + 16448 + 16384)
    ks = alloc("ks", [128, 4, 2048], BF16, at=XR0 + 16448 + 32768)
    off[0] = XR0 + 16448 + 49152
    DL0 = XR0 + 16448 + 16384
    delta1 = alloc("delta1", [128, 8, 2048], BF16, at=DL0)
    b_xr = [[Buf("xr%d_%d" % (i, t)) for t in range(4)] for i in range(4)]
    b_gg = [[Buf("gg%d_%d" % (i, t)) for t in range(4)] for i in range(4)]
    b_q = [[Buf("q%d_%d" % (i, t)) for t in range(4)] for i in range(4)]
    b_k = [[Buf("k%d_%d" % (i, t)) for t in range(4)] for i in range(4)]

    def flat(ll):
        return [b for row in ll for b in row]
    b_dl = [[Buf("dl%d_%d" % (i, t)) for t in range(4)] for i in range(8)]
    hy = alloc("hy", [128, 8, 2048], BF16)
    b_hh = [[Buf("h%d_%d" % (i, k)) for k in range(2)] for i in range(8)]
    b_h = [b for row in b_hh for b in row]
    b_xn = [[Buf("xn%d_%d" % (i, k)) for k in range(2)] for i in range(8)]
    b_y = [[Buf("y%d_%d" % (i, t)) for t in range(4)] for i in range(8)]
    vt = alloc("vt", [128, 16, 512], BF16)
    b_v = [Buf("v%d" % i) for i in range(16)]
    TMP0 = off[0]
    t_acc = alloc("t_acc", [128, 1024], F32)
    t_xcb = alloc("t_xcb", [128, 1024], BF16)
    t_A = alloc("t_A", [128, 1024], F32)
    t_B = [alloc("t_B%d" % i, [128, 1024], F32) for i in range(2)]
    t_C = alloc("t_C", [128, 1024], F32)
    t_D = alloc("t_D", [128, 1024], F32)
    b_acc, b_xcb, b_A, b_C, b_D = Buf("acc"), Buf("xcb"), Buf("A"), Buf("C"), Buf("D")
    b_B = [Buf("B0"), Buf("B1")]
    rstd = alloc("rstd", [128, 2048], F32, at=TMP0)
    sqb = [alloc("sqb%d" % i, [128, 2048], BF16, at=TMP0 + 8192 + 4096 * i) for i in range(2)]
    b_rstd = Buf("rstd")
    b_sq = [Buf("sq0"), Buf("sq1")]
    wring = [alloc("wring%d" % i, [128, 8, 512], BF16) for i in range(3)]
    b_wr = [Buf("wr%d" % i) for i in range(3)]
    pbuf2 = [alloc("pbuf%d" % i, [128, 2, 512], BF16) for i in range(4)]
    b_pp = [Buf("pp%d" % i) for i in range(4)]
    etmp = [nc.alloc_sbuf_tensor_at("etmp%d" % i, [128, 640], F32, offset=TMP0 + 2560 * i) for i in range(2)]
    b_et = [Buf("et0"), Buf("et1")]
    gt = [alloc("gt%d" % i, [128, 512], F32) for i in range(2)]
    b_gt = [Buf("gt0"), Buf("gt1")]
    rden = alloc("rden", [128, 512], F32)
    b_rden = Buf("rden")
    aring = [alloc("aring%d" % i, [128, 8, 512], BF16) for i in range(2)]
    b_ar = [Buf("ar0"), Buf("ar1")]
    STAGEA_END = off[0]

    off[0] = STAGE0
    h2b = [alloc("h2h%d" % i, [128, 8, 1024], BF16) for i in range(2)]
    b_h2 = [[Buf("h2_%d_%d" % (k, i)) for i in range(8)] for k in range(2)]
    b_ca = [Buf("ca%d" % i) for i in range(4)]
    b_cah = [Buf("cah%d" % i) for i in range(4)]
    b_halo = [Buf("halo0"), Buf("halo1")]
    b_hz = [[Buf("hz%d_%d" % (i, k)) for k in range(3)] for i in range(2)]
    assert off[0] <= DL0
    HY0 = DL0 + 32768
    off[0] = HY0
    sq2 = [alloc("sq2_%d" % i, [128, 1024], BF16) for i in range(2)]
    b_sq2 = [Buf("sq2_0"), Buf("sq2_1")]
    rstd2 = alloc("rstd2", [128, 1024], F32)
    b_rstd2 = Buf("rstd2")
    ntmp = [alloc("ntmp%d" % i, [128, 1024], F32) for i in range(2)]
    b_nt = [Buf("nt0"), Buf("nt1")]
    assert off[0] == HY0 + 16384, off[0]
    CACC0 = off[0]
    cacc = [alloc("cacc%d" % i, [128, 1024], F32) for i in range(4)]
    sq3 = [nc.alloc_sbuf_tensor_at("sq3_%d" % i, [128, 1024], BF16, offset=CACC0 + 4096 * i) for i in range(2)]
    assert off[0] == HY0 + 32768, off[0]
    x2h = alloc("x2h", [128, 8, 1024], F32)
    b_x2 = [Buf("x2_%d" % i) for i in range(8)]
    assert off[0] <= TMP0 + 16384 + 2048, (off[0], TMP0)
    uring = [alloc("uring%d" % i, [128, 8, 128], BF16) for i in range(NU)]
    b_ur = [Buf("ur%d" % i) for i in range(NU)]
    dring = [alloc("dring%d" % i, [128, NFC, 128], BF16) for i in range(ND)]
    b_dr = [Buf("dr%d" % i) for i in range(ND)]
    b_dr2 = [[Buf("dr%d_%d" % (i, k)) for k in range(3)] for i in range(ND)]
    gbuf = alloc("gbuf", [128, NFC, 1024], BF16)
    b_g = [Buf("g%d" % i) for i in range(NFC)]

    psum = nc.alloc_psum_tensor("psum", [128, 4096], F32)
    b_bank = [Buf("bank%d" % i) for i in range(8)]

    def bank(i, n=512, c0=0):
        return psum[:, i * 512 + c0:i * 512 + c0 + n]

    def dma_sp(out, in_, sem, reads=(), writes=()):
        return sch.op("sp", lambda h, o=out, i=in_: h.dma_start(out=o, in_=i), reads=reads, writes=writes, dma=sem)

    def dma_pool(out, in_, sem, reads=(), writes=()):
        return sch.op("pool", lambda h, o=out, i=in_: h.dma_start(out=o, in_=i), reads=reads, writes=writes, dma=sem)

    def act(out, in_, func, reads, writes, bias=None, scale=None):
        kw = {}
        if bias is not None:
            kw["bias"] = bias
        if scale is not None:
            kw["scale"] = scale
        return sch.op("act", lambda h, o=out, i=in_, f=func, kw=kw: h.activation(o, i, f, **kw), reads=reads, writes=writes)

    def tt(eng, out, a, b, op, reads, writes):
        return sch.op(eng, lambda h, o=out, a=a, b=b, op=op: h.tensor_tensor(o, a, b, op), reads=reads, writes=writes)

    def ts(eng, out, a, s1, s2, op0, op1, reads, writes):
        if op1 is None:
            return sch.op(eng, lambda h, o=out, a=a, s1=s1, op0=op0: h.tensor_scalar(o, a, s1, None, op0), reads=reads, writes=writes)
        return sch.op(eng, lambda h, o=out, a=a, s1=s1, s2=s2, op0=op0, op1=op1: h.tensor_scalar(o, a, s1, s2, op0, op1),
                      reads=reads, writes=writes)

    def stt(eng, out, a, s, b, op0, op1, reads, writes):
        return sch.op(eng, lambda h, o=out, a=a, s=s, b=b, op0=op0, op1=op1: h.scalar_tensor_tensor(o, a, s, b, op0, op1),
                      reads=reads, writes=writes)

    def mm_group(mms, reads, writes):
        def fn(h, mms=mms):
            ins = None
            for (o, l, r, st, sp_) in mms:
                ins = h.matmul(o, l, r, start=st, stop=sp_)
            return ins
        return sch.op("pe", fn, reads=reads, writes=writes)

    dd = [0]

    def dump(dst, src, rd):
        tmpf = gt[dd[0] % 2]
        bt = b_gt[dd[0] % 2]
        dd[0] += 1
        sch.op("dve", lambda h, t=tmpf, s=src: h.tensor_copy(t[:], s), reads=rd, writes=[bt])
        dma_sp(dst, tmpf[:], "dbg%d" % (dd[0] % 2), reads=[bt])

    prm_i = [0]

    def load_prm(dst, src, key):
        sem = "pr%d" % (prm_i[0] % 4)
        prm_i[0] += 1
        dma_sp(dst, src, sem, writes=[b_prm[key]])

    def _chk(name):
        if stop_at == name:
            raise _Stop()

    try:
        load_prm(p_c, cT[:, :], "c")
        load_prm(p_adab, ada_bT[:, :], "adab")
        load_prm(p_n1g, n1g[:, :], "n1g")
        load_prm(p_cw, cw[:, :], "cw")
        load_prm(p_cb, cb[:, :], "cb")
        load_prm(p_ba, rba[:, :], "ba")
        load_prm(p_bx, rbx[:, :], "bx")
        load_prm(p_lam, rlam[:, :], "lam")
        load_prm(p_n2g, n2g[:, :], "n2g")
        load_prm(p_nfg, nfg[:, :], "nfg")
        load_prm(p_fcw, fcw[:, :], "fcw")
        load_prm(p_fcb, fcb[:, :], "fcb")

        for kc in range(8):
            dma_sp(xfull[:, kc, :], xT[kc * 128:(kc + 1) * 128, :], "x%d" % (kc % 4), writes=[b_x[kc]])

        sch.op("dve", lambda h: h.memset(ones_bf[:], 1.0), writes=[b_ones])
        sch.op("dve", lambda h: h.memset(p_carry, 0.0), writes=[b_prm["carry"]])
        sch.op("dve", lambda h: h.memset(p_hz, 0.0), writes=[b_prm["hz"]])
        sch.op("dve", lambda h: h.memset(p_halo, 0.0), writes=[b_prm["halo"]] + b_halo)

        pieces = []
        for i in range(4):
            pieces.append(("ada", i, ada_v[:, :, i * 512:(i + 1) * 512]))
        for i in range(5):
            pieces.append(("win", i, win_v[:, :, i * 512:(i + 1) * 512]))
        for i in range(2):
            pieces.append(("wout", i, wout_v[:, :, i * 512:(i + 1) * 512]))
        piece_pos = {(k, i): n for n, (k, i, _) in enumerate(pieces)}
        issued = [0]

        def issue_pieces(upto):
            while issued[0] < min(upto, len(pieces)):
                n = issued[0]
                slot = n % 3
                dma_pool(wring[slot][:], pieces[n][2], "w%d" % slot, writes=[b_wr[slot]])
                issued[0] += 1

        def piece_slot(kind, idx):
            n = piece_pos[(kind, idx)]
            issue_pieces(n + 3)
            return n % 3

        issue_pieces(2)
        dma_pool(wabd[:], wa_bd[:, :], "wg", writes=[b_wabd])
        dma_pool(wxbd[:], wx_bd[:, :], "wg", writes=[b_wxbd])
        issue_pieces(3)

        act(p_tmp, p_c, AF.Tanh, [b_prm["c"]], [b_prm["tmp"]], scale=0.5)
        stt("dve", p_tmp, p_tmp, 1.0, p_c, ALU.add, ALU.mult, [b_prm["tmp"], b_prm["c"]], [b_prm["tmp"]])
        ts("dve", c_bf[:], p_tmp, 0.5, None, ALU.mult, None, [b_prm["tmp"]], [b_cbf])

        act(p_cl, p_lam, AF.Exp, [b_prm["lam"]], [b_prm["cl"]], scale=-1.0)
        act(p_cl, p_cl, AF.Ln, [b_prm["cl"]], [b_prm["cl"]], bias=1.0)
        ts("dve", p_cl2, p_cl, -8.0, None, ALU.mult, None, [b_prm["cl"]], [b_prm["cl2"]])
        ts("dve", p_cl, p_cl, -4.0, None, ALU.mult, None, [b_prm["cl"]], [b_prm["cl"]])
        ts("dve", p_ba, p_ba, 0.5, None, ALU.mult, None, [b_prm["ba"]], [b_prm["ba"]])
        ts("dve", p_bx, p_bx, 0.5, None, ALU.mult, None, [b_prm["bx"]], [b_prm["bx"]])


        MODB = 3

        def mod_piece(i, mb=MODB):
            slot = piece_slot("ada", i)
            mms = []
            for jj in range(4):
                j = i * 4 + jj
                for kc in range(8):
                    mms.append((bank(mb, 1, j), wring[slot][:, kc, jj * 128:(jj + 1) * 128], c_bf[:, kc:kc + 1], kc == 0, kc == 7))
            mm_group(mms, [b_wr[slot], b_cbf], [b_bank[mb]])
            tt("dve", p_mod[:, i * 4:(i + 1) * 4], bank(mb, 4, i * 4), p_adab[:, i * 4:(i + 1) * 4], ALU.add,
               [b_bank[mb], b_prm["adab"]], [b_prm["mod"]])

        aiss = [4]

        def issue_a(upto):
            while aiss[0] < min(upto, 12):
                i = aiss[0]
                dma_pool(aring[i % 2][:], ada_v[:, :, i * 512:(i + 1) * 512], "a%d" % (i % 2), writes=[b_ar[i % 2]])
                aiss[0] += 1

        def mod_piece2(i):
            issue_a(i + 2)
            slot = i % 2
            mms = []
            for jj in range(4):
                j = i * 4 + jj
                for kc in range(8):
                    mms.append((bank(MODB, 1, j), aring[slot][:, kc, jj * 128:(jj + 1) * 128], c_bf[:, kc:kc + 1], kc == 0, kc == 7))
            mm_group(mms, [b_ar[slot], b_cbf], [b_bank[MODB]])
            tt("dve", p_mod[:, i * 4:(i + 1) * 4], bank(MODB, 4, i * 4), p_adab[:, i * 4:(i + 1) * 4], ALU.add,
               [b_bank[MODB], b_prm["adab"]], [b_prm["mod"]])

        sh1, sc1, g1m = p_mod[:, 0:8], p_mod[:, 8:16], p_mod[:, 16:24]
        sh2, sc2, g2m = p_mod[:, 24:32], p_mod[:, 32:40], p_mod[:, 40:48]


        _chk('p0')
        sch.alias(b_sq + [b_rstd], b_et)
        for kc in range(8):
            s_ = kc % 2
            act(sqb[s_][:], xfull[:, kc, :], AF.Square, [b_x[kc]], [b_sq[s_]])
            mms = [(bank(4 + t), ones_bf[:], sqb[s_][:, t * 512:(t + 1) * 512], kc == 0, kc == 7) for t in range(4)]
            mm_group(mms, [b_sq[s_], b_ones], [b_bank[4 + t] for t in range(4)])
            if kc % 2 == 1:
                mod_piece(kc // 2)
        stt("dve", p_gm1, sc1, 1.0, p_n1g, ALU.add, ALU.mult, [b_prm["mod"], b_prm["n1g"]], [b_prm["gm1"]])
        act(rstd[:], psum[:, 2048:4096], AF.Ln, [b_bank[4 + t] for t in range(4)], [b_rstd], bias=EPS, scale=1.0 / D)
        act(rstd[:], rstd[:], AF.Exp, [b_rstd], [b_rstd], scale=-0.5)
        for hf_ in range(2):
            hs_ = slice(hf_ * 1024, (hf_ + 1) * 1024)
            for kc in range(8):
                stt("dve", xfull[:, kc, hs_], xfull[:, kc, hs_], p_gm1[:, kc:kc + 1], rstd[:, hs_], ALU.mult, ALU.mult,
                    [b_x[kc], b_prm["gm1"], b_rstd], [b_xn[kc][hf_]])
                act(hy[:, kc, hs_], xfull[:, kc, hs_], AF.Identity, [b_xn[kc][hf_], b_prm["mod"]], [b_hh[kc][hf_]], bias=sh1[:, kc:kc + 1])
        if debug:
            for kc in range(8):
                for t in range(4):
                    dump(dbg["d_h"][kc * 128:(kc + 1) * 128, t * 512:(t + 1) * 512], hy[:, kc, t * 512:(t + 1) * 512], b_hh[kc])
            dma_sp(dbg["d_mod"][:, :], p_mod, "dbg0", reads=[b_prm["mod"]])

        _chk('p1')
        sch.alias(flat(b_xr) + flat(b_gg) + flat(b_q) + flat(b_k), b_x + flat(b_xn))
        sch.alias([b_acc, b_xcb, b_A, b_C, b_D] + b_B, b_sq + [b_rstd] + b_et)
        for j in range(4):
            sch.op("dve", lambda h, j=j: h.memset(xrp[:, j, 0:8], 0.0), writes=[b_xr[j][0]])

        pb = [0]

        def next_bank():
            b = pb[0] % 3
            pb[0] += 1
            return b

        ev = [0]

        def inproj_fm(piece, mc, t):
            slot = piece_slot("win", piece)
            bk = next_bank()
            mms = [(bank(bk), wring[slot][:, kc, mc * 128:(mc + 1) * 128], hy[:, kc, t * 512:(t + 1) * 512], kc == 0, kc == 7)
                   for kc in range(8)]
            mm_group(mms, [b_wr[slot]] + [b_hh[kc][t // 2] for kc in range(8)], [b_bank[bk]])
            c0 = t * 512
            if piece == 0:
                dst = xrp[:, mc, 8 + c0:8 + c0 + 512]
                if ev[0] % 2 == 0:
                    act(dst, bank(bk), AF.Copy, [b_bank[bk]], [b_xr[mc][t]])
                else:
                    sch.op("dve", lambda h, d=dst, s=bank(bk): h.tensor_copy(d, s), reads=[b_bank[bk]], writes=[b_xr[mc][t]])
                ev[0] += 1
            elif piece == 1:
                g_ = ev[0] % 2
                ev[0] += 1
                act(gt[g_][:], bank(bk), AF.Square, [b_bank[bk]], [b_gt[g_]], scale=float(np.sqrt(0.044715)))
                stt("dve", gt[g_][:], gt[g_][:], 1.0, bank(bk), ALU.add, ALU.mult, [b_gt[g_], b_bank[bk]], [b_gt[g_]])
                act(gt[g_][:], gt[g_][:], AF.Tanh, [b_gt[g_]], [b_gt[g_]], scale=float(np.sqrt(2.0 / np.pi)))
                stt("dve", gg[:, mc, c0:c0 + 512], gt[g_][:], 1.0, bank(bk), ALU.add, ALU.mult, [b_gt[g_], b_bank[bk]], [b_gg[mc][t]])
            elif piece == 2:
                dst = qs[:, mc, c0:c0 + 512]
                if ev[0] % 4 != 3:
                    act(dst, bank(bk), AF.Copy, [b_bank[bk]], [b_q[mc][t]], scale=0.125)
                else:
                    ts("dve", dst, bank(bk), 0.125, None, ALU.mult, None, [b_bank[bk]], [b_q[mc][t]])
                ev[0] += 1
            else:
                dst = ks[:, mc, c0:c0 + 512]
                if ev[0] % 4 != 3:
                    act(dst, bank(bk), AF.Copy, [b_bank[bk]], [b_k[mc][t]])
                else:
                    sch.op("dve", lambda h, d=dst, s=bank(bk): h.tensor_copy(d, s), reads=[b_bank[bk]], writes=[b_k[mc][t]])
                ev[0] += 1

        def inproj_v(t16):
            slot = piece_slot("win", 4)
            bk = next_bank()
            mms = [(bank(bk), hy[:, kc, t16 * 128:(t16 + 1) * 128], wring[slot][:, kc, :], kc == 0, kc == 7) for kc in range(8)]
            mm_group(mms, [b_wr[slot]] + [b_hh[kc][t16 // 8] for kc in range(8)], [b_bank[bk]])
            dst = vt[:, t16, :]
            if ev[0] % 4 != 3:
                act(dst, bank(bk), AF.Copy, [b_bank[bk]], [b_v[t16]])
            else:
                sch.op("dve", lambda h, d=dst, s=bank(bk): h.tensor_copy(d, s), reads=[b_bank[bk]], writes=[b_v[t16]])
            ev[0] += 1

        for piece in (0,):
            for t in range(4):
                for mc in range(4):
                    inproj_fm(piece, mc, t)

        _chk('p2a')
        def rglru_chunk(j):
            for hf in range(2):
                c0 = hf * 1024
                act(t_acc[:], xrp[:, j, 8 + c0:8 + c0 + 1024], AF.Identity, b_xr[j] + [b_prm["cw"], b_prm["cb"]], [b_acc],
                    bias=p_cb[:, j:j + 1], scale=p_cw[:, j * 4 + 3:j * 4 + 4])
                for k in range(3):
                    src = xrp[:, j, 5 + k + c0:5 + k + c0 + 1024]
                    if k < 2:
                        stt("dve", t_acc[:], src, p_cw[:, j * 4 + k:j * 4 + k + 1], t_acc[:], ALU.mult, ALU.add,
                            b_xr[j] + [b_prm["cw"], b_acc], [b_acc])
                    else:
                        stt("dve", t_xcb[:], src, p_cw[:, j * 4 + k:j * 4 + k + 1], t_acc[:], ALU.mult, ALU.add,
                            b_xr[j] + [b_prm["cw"], b_acc], [b_xcb])
                yield
                yield
                mm_group([(bank(4 + t), wabd[:, j * 128:(j + 1) * 128], t_xcb[:, t * 512:(t + 1) * 512], True, True) for t in range(2)],
                         [b_wabd, b_xcb], [b_bank[4], b_bank[5]])
                mm_group([(bank(6 + t), wxbd[:, j * 128:(j + 1) * 128], t_xcb[:, t * 512:(t + 1) * 512], True, True) for t in range(2)],
                         [b_wxbd, b_xcb], [b_bank[6], b_bank[7]])
                act(t_A[:], psum[:, 2048:3072], AF.Tanh, [b_bank[4], b_bank[5], b_prm["ba"]], [b_A], bias=p_ba[:, j:j + 1], scale=0.5)
                act(t_C[:], psum[:, 3072:4096], AF.Tanh, [b_bank[6], b_bank[7], b_prm["bx"]], [b_C], bias=p_bx[:, j:j + 1], scale=0.5)
                act(t_B[hf][:], t_A[:], AF.Exp, [b_A, b_prm["cl2"]], [b_B[hf]], bias=p_cl2[:, j:j + 1], scale=p_cl2[:, j:j + 1])
                act(t_A[:], t_A[:], AF.Exp, [b_A, b_prm["cl"]], [b_A], bias=p_cl[:, j:j + 1], scale=p_cl[:, j:j + 1])
                yield
                stt("dve", t_C[:], t_C[:], 1.0, t_xcb[:], ALU.add, ALU.mult, [b_C, b_xcb], [b_C])
                act(t_B[hf][:], t_B[hf][:], AF.Sqrt, [b_B[hf]], [b_B[hf]], bias=1.0 / 16, scale=-1.0 / 16)
                tt("dve", t_C[:], t_C[:], t_B[hf][:], ALU.mult, [b_C, b_B[hf]], [b_C])
                sch.op("dve", lambda h, j=j: h.tensor_tensor_scan(t_D[:], t_A[:], t_C[:], p_carry[:, j:j + 1], ALU.mult, ALU.add),
                       reads=[b_A, b_C, b_prm["carry"]], writes=[b_D])
                sch.op("dve", lambda h, j=j: h.tensor_copy(p_carry[:, j:j + 1], t_D[:, 1023:1024]), reads=[b_D], writes=[b_prm["carry"]])
                yield
                tt("dve", gg[:, j, c0:c0 + 1024], gg[:, j, c0:c0 + 1024], t_D[:], ALU.mult, b_gg[j][2 * hf:2 * hf + 2] + [b_D], b_gg[j][2 * hf:2 * hf + 2])
                yield

        TR = {0: (0, 1), 1: (0, 3), 2: (0, 5), 3: (0, 7), 4: (0, 7), 5: (2, 7), 6: (4, 7), 7: (6, 7)}
        pi = [0]

        def attention():
            LAG = 3
            SBP = [0, 4]
            steps = []
            for c in range(4):
                for qt in range(4):
                    js = [4] + [j for j in (0, 1, 2, 3, 5, 6, 7) if 4 * qt - 4 + j >= 0]
                    for idx, j in enumerate(js):
                        steps.append((c, qt, idx, j, len(js)))
            npairs = len(steps)
            psv = psum.rearrange("p (b n) -> p b n", n=512)

            def geom(k):
                c, qt, idx, j, nj = steps[k]
                tlo, thi = TR[j]
                return dict(c=c, qt=qt, idx=idx, j=j, nj=nj, gkb=4 * qt - 4 + j, tlo=tlo, nco=(thi - tlo + 1) * 64,
                            q0=qt * 512 + tlo * 64, d0=(tlo - 2 * j + 8) * 64, sb=SBP[k % 2], pp=pbuf2[k % 4], bpp=b_pp[k % 4],
                            ob=(2, 3) if (c * 4 + qt) % 2 == 0 else (6, 7))

            def front(k):
                g = geom(k)
                c, nco, sb = g["c"], g["nco"], g["sb"]
                mms = []
                for e in range(2):
                    pr = slice(e * 64, (e + 1) * 64)
                    mms.append((bank(sb + e, nco), ks[pr, c, g["gkb"] * 128:(g["gkb"] + 1) * 128], qs[pr, c, g["q0"]:g["q0"] + nco], True, False))
                for e in range(2):
                    mms.append((bank(sb + e, nco), ident_bf[:], Etab[:, 2 * c + e, g["d0"]:g["d0"] + nco], False, True))
                mm_group(mms, b_k[c] + b_q[c] + [b_ident, b_E[2 * c], b_E[2 * c + 1]], [b_bank[sb], b_bank[sb + 1]])
                act(g["pp"][:, :, 0:nco], psv[:, sb:sb + 2, 0:nco], AF.Exp, [b_bank[sb], b_bank[sb + 1]], [g["bpp"]])

            def back(k):
                g = geom(k)
                c, qt, nco, tlo = g["c"], g["qt"], g["nco"], g["tlo"]
                ob, db_ = g["ob"]
                first = g["idx"] == 0
                last = g["idx"] == g["nj"] - 1
                mms = []
                for e in range(2):
                    pr = slice(e * 64, (e + 1) * 64)
                    hh = 2 * c + e
                    mms.append((psum[pr, ob * 512 + tlo * 64:ob * 512 + tlo * 64 + nco], vt[:, g["gkb"], hh * 64:(hh + 1) * 64], g["pp"][:, e, 0:nco], first, last))
                for e in range(2):
                    pr = slice(e * 64, (e + 1) * 64)
                    mms.append((psum[pr, db_ * 512 + tlo * 64:db_ * 512 + tlo * 64 + nco], ones_bf[:, 0:64], g["pp"][:, e, 0:nco], first, last))
                mm_group(mms, [b_v[g["gkb"]], g["bpp"], b_ones], [b_bank[ob], b_bank[db_]])
                if last:
                    sch.op("dve", lambda h, db_=db_: h.reciprocal(rden[:], bank(db_)), reads=[b_bank[db_]], writes=[b_rden])
                    tt("dve", hy[:, 4 + c, qt * 512:(qt + 1) * 512], bank(ob), rden[:], ALU.mult, [b_bank[ob], b_rden],
                       [b_y[4 + c][qt]] + b_hh[4 + c])

            nmod = [12]
            for k in range(npairs + LAG):
                if k < npairs:
                    front(k)
                if k >= LAG:
                    back(k - LAG)
                if k % 12 == 8 and nmod[0] < 12:
                    mod_piece(nmod[0], mb=SBP[(k + 1) % 2])
                    nmod[0] += 1
                yield
            while nmod[0] < 12:
                mod_piece(nmod[0], mb=SBP[0])
                nmod[0] += 1

        def load_bias_tables():
            dma_pool(ident_bf[:], ident[:, :], "wg", writes=[b_ident])
            dma_pool(Etab[:], tbias.rearrange("h p n -> p h n"), "wg", writes=b_E)
            for hh in range(8):
                sch.op("dve", lambda h, hh=hh: h.memset(Etab[0:64, hh, 576:640], -30000.0), writes=[b_E[hh]])
                sch.op("dve", lambda h, hh=hh: h.memset(Etab[64:128, hh, 0:64], -30000.0), writes=[b_E[hh]])

        def inproj_rest():
            cnt = [0]
            nm = [4]

            def tick():
                cnt[0] += 1
                if cnt[0] % 8 == 4 and nm[0] < 12:
                    mod_piece2(nm[0])
                    nm[0] += 1

            issue_a(6)
            for piece in (1,):
                for mc in range(4):
                    for t in range(4):
                        inproj_fm(piece, mc, t)
                        tick()
                        yield
            load_bias_tables()
            for piece in (2, 3):
                for mc in range(4):
                    for t in range(4):
                        inproj_fm(piece, mc, t)
                        tick()
                        yield
            for t16 in range(16):
                inproj_v(t16)
                tick()
                yield
            while nm[0] < 12:
                mod_piece2(nm[0])
                nm[0] += 1

        def rglru_all():
            for j in range(4):
                yield from rglru_chunk(j)

        ga, gb = rglru_all(), inproj_rest()
        NA, NB = 40, 64
        ia = ib = 0
        da = db = False
        while not (da and db):
            take_a = (not da) and (db or ia * NB <= ib * NA)
            if take_a:
                try:
                    next(ga)
                    ia += 1
                except StopIteration:
                    da = True
            else:
                try:
                    next(gb)
                    ib += 1
                except StopIteration:
                    db = True
        for _ in attention():
            pass

        _chk('p3')
        stt("dve", p_gm2, sc2, 1.0, p_n2g, ALU.add, ALU.mult, [b_prm["mod"], b_prm["n2g"]], [b_prm["gm2"]])

        if debug:
            for j in range(4):
                for t in range(4):
                    cs = slice(t * 512, (t + 1) * 512)
                    dump(dbg["d_xr"][j * 128:(j + 1) * 128, cs], xrp[:, j, 8 + t * 512:8 + (t + 1) * 512], b_xr[j])
                    dump(dbg["d_q"][j * 128:(j + 1) * 128, cs], qs[:, j, cs], b_q[j])
                    dump(dbg["d_k"][j * 128:(j + 1) * 128, cs], ks[:, j, cs], b_k[j])
            for t16 in range(16):
                dump(dbg["d_v"][t16 * 128:(t16 + 1) * 128, :], vt[:, t16, :], [b_v[t16]])
            for j in range(8):
                for t in range(4):
                    cs = slice(t * 512, (t + 1) * 512)
                    dump(dbg["d_y"][j * 128:(j + 1) * 128, cs], gg[:, j, cs] if j < 4 else hy[:, j, cs], b_gg[j] if j < 4 else b_y[j])

        _chk('p3d')
        xi = [0]

        def xload(dst, bdst, hf, kc):
            dma_sp(dst, xT[kc * 128:(kc + 1) * 128, hf * 1024:(hf + 1) * 1024], "x%d" % (xi[0] % 4), writes=[bdst])
            xi[0] += 1

        def add_delta(dst, bdst, hf, kc):
            tt("dve", dst, dst, delta1[:, kc, hf * 1024:(hf + 1) * 1024], ALU.add, [bdst] + b_dl[kc][2 * hf:2 * hf + 2], [bdst])

        def stats_sq(src, bsrc, i):
            act(sq2[i % 2][:], src, AF.Square, [bsrc], [b_sq2[i % 2]])

        def stats_mm(i, b0_, n_):
            mm_group([(bank(b0_ + t), ones_bf[:], sq2[i % 2][:, t * 512:(t + 1) * 512], i == 0, i == n_ - 1) for t in range(2)],
                     [b_sq2[i % 2], b_ones], [b_bank[b0_], b_bank[b0_ + 1]])

        def fstat_sq(m):
            act(sq3[m % 2][:], x2h[:, m, :], AF.Square, [b_x2[m]], [b_ca[m % 2], b_cah[m % 2]])

        def fstat_mm(m, b0_):
            mm_group([(bank(b0_ + t), ones_bf[:], sq3[m % 2][:, t * 512:(t + 1) * 512], m == 0, m == 7) for t in range(2)],
                     [b_ca[m % 2], b_cah[m % 2], b_ones], [b_bank[b0_], b_bank[b0_ + 1]])

        def make_rstd(b0_):
            act(rstd2[:], psum[:, b0_ * 512:b0_ * 512 + 1024], AF.Ln, [b_bank[b0_], b_bank[b0_ + 1]], [b_rstd2], bias=EPS, scale=1.0 / D)
            act(rstd2[:], rstd2[:], AF.Exp, [b_rstd2], [b_rstd2], scale=-0.5)

        sch.alias(b_x2, b_v + [b_acc, b_xcb, b_A, b_C, b_D] + b_B)
        sch.alias(b_sq2, [b for row in b_hh[0:4] for b in row])
        for kc in range(8):
            xload(x2h[:, kc, :], b_x2[kc], 0, kc)

        sch.alias(flat(b_dl), flat(b_q) + flat(b_k))
        y_all = flat(b_gg) + [b for row in b_y[4:] for b in row]

        def ysrc(kc, cs):
            return gg[:, kc, cs] if kc < 4 else hy[:, kc, cs]
        for m in range(8):
            slot = piece_slot("wout", m // 4)
            mo = (m % 4) * 128
            for t in range(4):
                bk = next_bank()
                mms = [(bank(bk), wring[slot][:, kc, mo:mo + 128], ysrc(kc, slice(t * 512, (t + 1) * 512)), kc == 0, kc == 7) for kc in range(8)]
                mm_group(mms, [b_wr[slot]] + [b_gg[kc][t] for kc in range(4)] + [b_y[kc][t] for kc in range(4, 8)], [b_bank[bk]])
                dst = delta1[:, m, t * 512:(t + 1) * 512]
                if ev[0] % 2 == 0:
                    act(dst, bank(bk), AF.Copy, [b_bank[bk], b_prm["mod"]], [b_dl[m][t]], scale=g1m[:, m:m + 1])
                else:
                    ts("dve", dst, bank(bk), g1m[:, m:m + 1], None, ALU.mult, None, [b_bank[bk], b_prm["mod"]], [b_dl[m][t]])
                ev[0] += 1
            add_delta(x2h[:, m, :], b_x2[m], 0, m)
            stats_sq(x2h[:, m, :], b_x2[m], m)
            if m >= 1:
                stats_mm(m - 1, 6, 8)
        stats_mm(7, 6, 8)
        sch.alias([b_rstd2], [b for row in b_hh[0:4] for b in row])
        make_rstd(6)
        if debug:
            for j in range(8):
                for t in range(4):
                    cs = slice(t * 512, (t + 1) * 512)
                    dump(dbg["d_dl"][j * 128:(j + 1) * 128, cs], delta1[:, j, cs], b_dl[j])

        _chk('p4')
        sch.barrier()
        uiss = [0]
        upieces = []
        for hf in range(2):
            for f in range(NFC):
                for part in range(2):
                    upieces.append(wup_v[:, :, part * DFF + f * 128:part * DFF + (f + 1) * 128])

        def issue_u(upto):
            while uiss[0] < min(upto, len(upieces)):
                n = uiss[0]
                dma_pool(uring[n % NU][:], upieces[n], "u%d" % (n % NU), writes=[b_ur[n % NU]])
                uiss[0] += 1

        diss = [0]
        dpieces = [wdn_v[:, :, m * 128:(m + 1) * 128] for hf in range(2) for m in range(8)]

        def issue_d(upto):
            while diss[0] < min(upto, len(dpieces)):
                n = diss[0]
                for hh_, (ka, kb_) in enumerate(((0, 8), (8, 16), (16, 22))):
                    dma_pool(dring[n % ND][:, ka:kb_, :], dpieces[n][:, ka:kb_, :],
                             "d%d_%d" % (n % ND, hh_), writes=[b_dr2[n % ND][hh_]])
                diss[0] += 1

        issue_u(NU)
        pend = [None]
        ub = [0]
        oi = [0]
        def side_runner(units):
            it = iter(units)

            def run(n_):
                for _ in range(n_):
                    u = next(it, None)
                    if u is None:
                        return
                    u()
            return run

        for kc in range(8):
            s_ = kc % 2
            stt("dve", ntmp[s_][:], x2h[:, kc, :], p_gm2[:, kc:kc + 1], rstd2[:], ALU.mult, ALU.mult,
                [b_x2[kc], b_prm["gm2"], b_rstd2], [b_nt[s_]])
            act(h2b[0][:, kc, :], ntmp[s_][:], AF.Identity, [b_nt[s_], b_prm["mod"]], [b_h2[0][kc]], bias=sh2[:, kc:kc + 1])
        _chk('f_norm0')
        issue_d(ND - 1)

        def n2_units():
            us_ = []
            us_.append(lambda: xload(ntmp[0][:], b_nt[0], 1, 0))
            for kc in range(8):
                def p1(kc=kc):
                    if kc + 1 < 8:
                        xload(ntmp[(kc + 1) % 2][:], b_nt[(kc + 1) % 2], 1, kc + 1)
                    add_delta(ntmp[kc % 2][:], b_nt[kc % 2], 1, kc)
                    stats_sq(ntmp[kc % 2][:], b_nt[kc % 2], kc)
                us_.append(p1)
                if kc >= 1:
                    us_.append(lambda kc=kc: stats_mm(kc - 1, 4, 8))
            us_.append(lambda: stats_mm(7, 4, 8))
            us_.append(lambda: xload(ntmp[0][:], b_nt[0], 1, 0))
            us_.append(lambda: make_rstd(4))
            for kc in range(8):
                def p2(kc=kc):
                    if kc + 1 < 8:
                        xload(ntmp[(kc + 1) % 2][:], b_nt[(kc + 1) % 2], 1, kc + 1)
                    s_ = kc % 2
                    add_delta(ntmp[s_][:], b_nt[s_], 1, kc)
                    stt("dve", ntmp[s_][:], ntmp[s_][:], p_gm2[:, kc:kc + 1], rstd2[:], ALU.mult, ALU.mult,
                        [b_nt[s_], b_prm["gm2"], b_rstd2], [b_nt[s_]])
                    act(h2b[1][:, kc, :], ntmp[s_][:], AF.Identity, [b_nt[s_], b_prm["mod"]], [b_h2[1][kc]], bias=sh2[:, kc:kc + 1])
                us_.append(p2)
            return us_

        def fin_units(hf, preload_next):
            us_ = []
            if preload_next:
                for m in range(8):
                    us_.append(lambda m=m: stats_sq(x2h[:, m, :], b_x2[m], m))
                    if m >= 1:
                        us_.append(lambda m=m: stats_mm(m - 1, 6, 8))
                us_.append(lambda: stats_mm(7, 6, 8))
                us_.append(lambda: make_rstd(6))
            else:
                us_.append(lambda: make_rstd(4))
            for m in range(8):
                def st(m=m):
                    if preload_next or m % 3 == 2:
                        act(x2h[:, m, :], x2h[:, m, :], AF.Copy, [b_x2[m], b_prm["nfg"]], [b_x2[m]], scale=p_nfg[:, m:m + 1])
                        tt("pool", x2h[:, m, :], x2h[:, m, :], rstd2[:], ALU.mult, [b_x2[m], b_rstd2], [b_x2[m]])
                    else:
                        stt("dve", x2h[:, m, :], x2h[:, m, :], p_nfg[:, m:m + 1], rstd2[:], ALU.mult, ALU.mult,
                            [b_x2[m], b_prm["nfg"], b_rstd2], [b_x2[m]])
                    dma_sp(outT[m * 128:(m + 1) * 128, hf * 1024:(hf + 1) * 1024], x2h[:, m, :], "o%d" % (oi[0] % 4), reads=[b_x2[m]])
                    oi[0] += 1
                us_.append(st)
            if preload_next:
                for kc in range(8):
                    us_.append(lambda kc=kc: xload(x2h[:, kc, :], b_x2[kc], hf + 1, kc))
                for kc in range(8):
                    us_.append(lambda kc=kc: tt("pool", x2h[:, kc, :], x2h[:, kc, :], delta1[:, kc, (hf + 1) * 1024:(hf + 2) * 1024], ALU.add,
                                                [b_x2[kc]] + b_dl[kc], [b_x2[kc]]))
            return us_

        side = side_runner([])
        for hf in range(2):
            h2h = h2b[hf]
            bh2 = b_h2[hf]
            for f in range(NFC):
                for part in range(2):
                    n = (hf * NFC + f) * 2 + part
                    issue_u(n + NU)
                    us = n % NU
                    ci = part * NFC + f
                    b0_ = (ub[0] % 3) * 2
                    ub[0] += 1
                    mms = []
                    for t in range(2):
                        for kc in range(8):
                            mms.append((bank(b0_ + t), uring[us][:, kc, :], h2h[:, kc, t * 512:(t + 1) * 512], kc == 0, kc == 7))
                    if hf == 0 and f == 0 and part == 0:
                        for kc in range(8):
                            mm_group([(bank(b0_ + t), uring[us][:, kc, :], h2h[:, kc, t * 512:(t + 1) * 512], kc == 0, kc == 7)
                                      for t in range(2)], [b_ur[us], bh2[kc]], [b_bank[b0_], b_bank[b0_ + 1]])
                    else:
                        mm_group(mms, [b_ur[us]] + bh2, [b_bank[b0_], b_bank[b0_ + 1]])
                    pu = psum[:, b0_ * 512:b0_ * 512 + 1024]
                    bks = [b_bank[b0_], b_bank[b0_ + 1]]
                    ca = cacc[part * 2 + (f % 2)]
                    bca = b_ca[part * 2 + (f % 2)]
                    w0 = p_fcw[:, ci * 3 + 0:ci * 3 + 1]
                    w1 = p_fcw[:, ci * 3 + 1:ci * 3 + 2]
                    w2 = p_fcw[:, ci * 3 + 2:ci * 3 + 3]
                    hzs = (n % 2) * 4
                    hz = p_hz[:, hzs:hzs + 4]
                    bhz = b_hz[n % 2]
                    bcah = b_cah[part * 2 + (f % 2)]
                    hrd = (1 - hf) * 88 + 2 * ci
                    hwr = hf * 88 + 2 * ci
                    sch.op("dve", lambda h, hrd=hrd, hz=hz: h.tensor_copy(hz[:, 0:2], p_halo[:, hrd:hrd + 2]),
                           reads=[b_halo[1 - hf]], writes=[bhz[0]])
                    act(hz[:, 2:4], pu[:, 0:2], AF.Copy, bks, [bhz[1]])
                    act(p_halo[:, hwr:hwr + 2], pu[:, 1022:1024], AF.Copy, bks, [b_halo[hf]])
                    act(ca[:], pu, AF.Identity, bks + [b_prm["fcw"], b_prm["fcb"]], [bca, bcah], bias=p_fcb[:, ci:ci + 1], scale=w2)
                    stt("dve", ca[:, 2:1024], pu[:, 1:1023], w1, ca[:, 2:1024], ALU.mult, ALU.add, bks + [bca, b_prm["fcw"]], [bca])
                    stt("dve", ca[:, 2:1024], pu[:, 0:1022], w0, ca[:, 2:1024], ALU.mult, ALU.add, bks + [bca, b_prm["fcw"]], [bca])
                    stt("dve", ca[:, 0:2], hz[:, 1:3], w1, ca[:, 0:2], ALU.mult, ALU.add, [bhz[0], bhz[1], bcah, b_prm["fcw"]], [bcah])
                    stt("dve", ca[:, 0:2], hz[:, 0:2], w0, ca[:, 0:2], ALU.mult, ALU.add, [bhz[0], bhz[1], bcah, b_prm["fcw"]], [bcah])
                    if pend[0] is not None:
                        pend[0]()
                    if part == 0:
                        pend[0] = (lambda ca=ca, bca=bca, bcah=bcah: act(ca[:], ca[:], AF.Silu, [bca, bcah], [bca, bcah]))
                    else:
                        pend[0] = (lambda f=f, ca=ca, bca=bca, bcah=bcah: tt("pool", gbuf[:, f, :], cacc[f % 2][:], ca[:], ALU.mult,
                                                                         [b_ca[f % 2], b_cah[f % 2], bca, bcah], [b_g[f]]))
                side(2)
            if pend[0] is not None:
                pend[0]()
                pend[0] = None
            side(1000)
            _chk('f_up%d' % hf)
            side = side_runner(n2_units() if hf == 0 else [])
            fsb = 2 if hf == 0 else 4
            for m in range(8):
                n = hf * 8 + m
                issue_d(n + ND)
                ds_ = n % ND
                if hf == 1:
                    add_delta(x2h[:, m, :], b_x2[m], 1, m)
                for t in range(2):
                    bk = 6 + t
                    mms = [(bank(bk), dring[ds_][:, kc, :], gbuf[:, kc, t * 512:(t + 1) * 512], kc == 0, kc == NFC - 1) for kc in range(NFC)]
                    if m == 0 and t == 0:
                        mm_group(mms[:16], b_dr2[ds_] + b_g[:16], [b_bank[bk]])
                        mm_group(mms[16:20], b_dr2[ds_] + b_g[16:20], [b_bank[bk]])
                        mm_group(mms[20:], b_dr2[ds_] + b_g[20:], [b_bank[bk]])
                    else:
                        mm_group(mms, b_dr2[ds_] + b_g, [b_bank[bk]])
                    stt("dve", x2h[:, m, t * 512:(t + 1) * 512], bank(bk), g2m[:, m:m + 1], x2h[:, m, t * 512:(t + 1) * 512],
                        ALU.mult, ALU.add, [b_bank[bk], b_prm["mod"], b_x2[m]], [b_x2[m]])
                    side(2)
                fstat_sq(m)
                if m >= 1:
                    fstat_mm(m - 1, fsb)
            fstat_mm(7, fsb)
            side(1000)
            _chk('f_down%d' % hf)
            make_rstd(fsb)
            for m in range(8):
                if hf == 1 and m % 3 == 2:
                    act(x2h[:, m, :], x2h[:, m, :], AF.Copy, [b_x2[m], b_prm["nfg"]], [b_x2[m]], scale=p_nfg[:, m:m + 1])
                    tt("pool", x2h[:, m, :], x2h[:, m, :], rstd2[:], ALU.mult, [b_x2[m], b_rstd2], [b_x2[m]])
                else:
                    stt("dve", x2h[:, m, :], x2h[:, m, :], p_nfg[:, m:m + 1], rstd2[:], ALU.mult, ALU.mult,
                        [b_x2[m], b_prm["nfg"], b_rstd2], [b_x2[m]])
                dma_sp(outT[m * 128:(m + 1) * 128, hf * 1024:(hf + 1) * 1024], x2h[:, m, :], "o%d" % (oi[0] % 4), reads=[b_x2[m]])
                oi[0] += 1
            if hf == 0:
                for kc in range(8):
                    xload(x2h[:, kc, :], b_x2[kc], 1, kc)
            side = side_runner([])

    except _Stop:
        pass

    fin = [(s, c) for s, c in sch.cnt.items() if s.startswith("o") or s.startswith("dbg")]
    sch.wait_only("sp", fin)

    sem_names = sorted(sch.cnt.keys())
    sems = {}
    import contextlib
    with contextlib.ExitStack() as es:
        for nme in sem_names:
            sems[nme] = es.enter_context(nc.semaphore("s_" + nme))
        block = es.enter_context(nc.Block())
        handles = {"pe": block.tensor, "act": block.scalar, "dve": block.vector, "pool": block.gpsimd, "sp": block.sync}
        for e in Sched.ENGS:
            ops = sch.ops[e]

            def body(h, ops=ops, e=e):
                for waits, fn, dma in ops:
                    for s_, c_ in waits:
                        h.wait_ge(sems[s_], c_)
                    if fn is None:
                        continue
                    ins = fn(h)
                    if dma is not None:
                        ins.then_inc(sems[dma], 16)
                    else:
                        ins.then_inc(sems[e], 1)
            handles[e](body)
    return nc


def _fm(v):
    v = np.asarray(v, np.float32)
    return np.ascontiguousarray(v.reshape(-1, 128).T)


def prep_inputs(b, x, c, ada_w, ada_b, norm1_g, w_in, rnn_conv_w, rnn_conv_b, rg_wa, rg_ba, rg_wx, rg_bx,
                rg_lambda, rel_bias, w_out, norm2_g, w_up, ffn_conv_w, ffn_conv_b, w_down, final_g, shared):
    m = dict(shared)
    m["xT"] = np.ascontiguousarray(np.asarray(x[b], np.float32).T)
    m["cT"] = _fm(c[b])
    return m


def prep_shared(ada_w, ada_b, norm1_g, w_in, rnn_conv_w, rnn_conv_b, rg_wa, rg_ba, rg_wx, rg_bx,
                rg_lambda, rel_bias, w_out, norm2_g, w_up, ffn_conv_w, ffn_conv_b, w_down, final_g):
    f = np.float32
    sh = {}
    sh["ada_w"] = np.ascontiguousarray(np.asarray(ada_w[0], f))
    sh["ada_bT"] = _fm(ada_b[0])
    sh["n1g"] = _fm(norm1_g[0])
    sh["n2g"] = _fm(norm2_g[0])
    sh["nfg"] = _fm(final_g)
    sh["w_in"] = np.ascontiguousarray(np.asarray(w_in[0], f))
    cwv = np.asarray(rnn_conv_w[0], f)
    sh["cw"] = np.ascontiguousarray(cwv.reshape(4, 4, 128).transpose(2, 1, 0).reshape(128, 16))
    sh["cb"] = _fm(rnn_conv_b[0])

    def bd(w):
        w = np.asarray(w, f)
        o = np.zeros((128, 4, 128), f)
        for j in range(4):
            o[0:64, j, 0:64] = w[2 * j]
            o[64:128, j, 64:128] = w[2 * j + 1]
        return np.ascontiguousarray(o.reshape(128, 512))
    sh["wa_bd"] = bd(rg_wa[0])
    sh["wx_bd"] = bd(rg_wx[0])
    sh["rba"] = _fm(rg_ba[0])
    sh["rbx"] = _fm(rg_bx[0])
    sh["rlam"] = _fm(rg_lambda[0])
    kk = np.arange(128)[:, None]
    col = np.arange(640)[None, :]
    idx = np.clip(col - kk, -128, 128) + 128
    sh["tbias"] = np.ascontiguousarray(np.asarray(rel_bias[0], f)[:, idx])
    sh["ident"] = np.eye(128, dtype=f)
    sh["w_out"] = np.ascontiguousarray(np.asarray(w_out[0], f))
    sh["w_up"] = np.ascontiguousarray(np.asarray(w_up[0], f))
    fw = np.asarray(ffn_conv_w[0], f)
    sh["fcw"] = np.ascontiguousarray(fw.reshape(3, 44, 128).transpose(2, 1, 0).reshape(128, 132))
    sh["fcb"] = _fm(ffn_conv_b[0])
    sh["w_down"] = np.ascontiguousarray(np.asarray(w_down[0], f))
    return sh


_NC_CACHE = {}


def kernel(x, c, ada_w, ada_b, norm1_g, w_in, rnn_conv_w, rnn_conv_b, rg_wa, rg_ba, rg_wx, rg_bx,
           rg_lambda, rel_bias, w_out, norm2_g, w_up, ffn_conv_w, ffn_conv_b, w_down, final_g, _debug=False, _stop=None):
    x = np.asarray(x)
    c = np.asarray(c)
    shared = prep_shared(ada_w, ada_b, norm1_g, w_in, rnn_conv_w, rnn_conv_b, rg_wa, rg_ba, rg_wx, rg_bx,
                         rg_lambda, rel_bias, w_out, norm2_g, w_up, ffn_conv_w, ffn_conv_b, w_down, final_g)
    in_maps = []
    for b in range(NCORE):
        m = dict(shared)
        m["xT"] = np.ascontiguousarray(np.asarray(x[b], np.float32).T)
        m["cT"] = _fm(c[b])
        in_maps.append(m)
    nc = build_program(debug=_debug, stop_at=_stop)
    res = run_bass_kernel_spmd(nc, in_maps, core_ids=list(range(NCORE)))
    out = np.stack([np.ascontiguousarray(res.results[b]["outT"].T) for b in range(NCORE)], axis=0).astype(np.float32)
    if _debug:
        return out, res.results
    return out
```
